# Optimizing a Trainium2 kernel written in Bass

```python
import jax
import jax.numpy as jnp
from jax import lax
import numpy as np

D_MODEL = 1024
BATCH = 16
SEQ = 2048
DEPTH = 4

N_MIXERS = 3
D_FF = 4 * D_MODEL
N_GROUPS = 4
GROUP_DIM = D_MODEL // N_GROUPS
POOL_WINDOWS = (2, 4, 8, 16)
CONV_WIDTH = 31
CONV_PAD = CONV_WIDTH // 2
D_CONV = D_MODEL
NORM_EPS = 1e-6
LN_EPS = 1e-5
RES_SCALE = (2 * DEPTH) ** -0.5

kernel_name = 'hybrid_pool_conv_fourier_encoder'


def rms_norm(x, g):
    xf = x.astype(jnp.float32)
    y = xf * lax.rsqrt(jnp.mean(xf * xf, axis=-1, keepdims=True) + NORM_EPS)
    return (y * g.astype(jnp.float32)).astype(x.dtype)


def pool_mixer(h, w_in, w_group, scale, w_out):
    b, s, d = h.shape
    u = (h @ w_in).reshape(b, s, N_GROUPS, GROUP_DIM).astype(jnp.float32)
    csum = jnp.pad(jnp.cumsum(u, axis=1), ((0, 0), (1, 0), (0, 0), (0, 0)))
    pos = jnp.arange(s)
    groups = []
    for g, w in enumerate(POOL_WINDOWS):
        lo = jnp.clip(pos - w // 2, 0, s)
        hi = jnp.clip(pos + w // 2, 0, s)
        cnt = (hi - lo).astype(jnp.float32)[None, :, None]
        mean = (csum[:, hi, g] - csum[:, lo, g]) / cnt
        groups.append(mean - u[:, :, g])
    p = jnp.stack(groups, axis=2).astype(h.dtype)
    y = jnp.einsum('bsgc,gcd->bsgd', p, w_group).reshape(b, s, d) * scale
    return y @ w_out


def conv_module(h, w_in, dw, dw_bias, ln_g, ln_b, w_out):
    val, gate = jnp.split(h @ w_in, 2, axis=-1)
    v = val * jax.nn.sigmoid(gate)
    v = lax.conv_general_dilated(
        v, dw[:, None, :].astype(v.dtype), window_strides=(1,),
        padding=((CONV_PAD, CONV_PAD),),
        dimension_numbers=('NWC', 'WIO', 'NWC'),
        feature_group_count=D_CONV) + dw_bias
    vf = v.astype(jnp.float32)
    mu = jnp.mean(vf, axis=-1, keepdims=True)
    var = jnp.mean(jnp.square(vf - mu), axis=-1, keepdims=True)
    vn = (vf - mu) * lax.rsqrt(var + LN_EPS) * ln_g.astype(jnp.float32) + ln_b.astype(jnp.float32)
    return jax.nn.silu(vn).astype(h.dtype) @ w_out


def fourier_mixer(h, w_in, w_out):
    b, s, d = h.shape
    u = (h @ w_in).reshape(b, s, N_GROUPS, GROUP_DIM).astype(jnp.float32)
    f = jnp.fft.fft2(u, axes=(1, 3), norm='ortho').real
    return f.reshape(b, s, d).astype(h.dtype) @ w_out


def sq_relu_mlp(h, w_up, w_down):
    a = jax.nn.relu(h @ w_up)
    return (a * a) @ w_down


def setup_inputs(seed: int = 0) -> dict:
    key = jax.random.key(seed)
    keys = iter(jax.random.split(key, 64))
    f32 = jnp.float32

    def dense(shape, fan_in, scale=1.0):
        return jax.random.normal(next(keys), shape, f32) * (scale * fan_in ** -0.5)

    def gain(n):
        return 1.0 + 0.02 * jax.random.normal(next(keys), (n,), f32)

    def bias(n):
        return 0.02 * jax.random.normal(next(keys), (n,), f32)

    inp = {'x': jax.random.normal(next(keys), (BATCH, SEQ, D_MODEL), f32)}

    def mlp_params(i):
        inp[f'l{i}_norm_mlp'] = gain(D_MODEL)
        inp[f'l{i}_mlp_up'] = dense((D_MODEL, D_FF), D_MODEL)
        inp[f'l{i}_mlp_down'] = dense((D_FF, D_MODEL), D_FF, RES_SCALE)

    def pool_params(i):
        inp[f'l{i}_norm_mix'] = gain(D_MODEL)
        inp[f'l{i}_pool_w_in'] = dense((D_MODEL, D_MODEL), D_MODEL)
        inp[f'l{i}_pool_w_group'] = dense((N_GROUPS, GROUP_DIM, GROUP_DIM), GROUP_DIM)
        inp[f'l{i}_pool_scale'] = gain(D_MODEL)
        inp[f'l{i}_pool_w_out'] = dense((D_MODEL, D_MODEL), D_MODEL, RES_SCALE)
        mlp_params(i)

    def conv_params(i):
        inp[f'l{i}_norm_mix'] = gain(D_MODEL)
        inp[f'l{i}_conv_w_in'] = dense((D_MODEL, 2 * D_CONV), D_MODEL)
        inp[f'l{i}_conv_dw'] = dense((CONV_WIDTH, D_CONV), CONV_WIDTH)
        inp[f'l{i}_conv_dw_bias'] = bias(D_CONV)
        inp[f'l{i}_conv_ln_g'] = gain(D_CONV)
        inp[f'l{i}_conv_ln_b'] = bias(D_CONV)
        inp[f'l{i}_conv_w_out'] = dense((D_CONV, D_MODEL), D_CONV, RES_SCALE)
        mlp_params(i)

    def fourier_params(i):
        inp[f'l{i}_norm_mix'] = gain(D_MODEL)
        inp[f'l{i}_fourier_w_in'] = dense((D_MODEL, D_MODEL), D_MODEL)
        inp[f'l{i}_fourier_w_out'] = dense((D_MODEL, D_MODEL), D_MODEL, RES_SCALE)
        mlp_params(i)

    pool_params(0)
    conv_params(1)
    fourier_params(2)
    pool_params(3)
    inp['final_norm'] = gain(D_MODEL)
    return inp


def reference(x,
              l0_norm_mix, l0_pool_w_in, l0_pool_w_group, l0_pool_scale, l0_pool_w_out,
              l0_norm_mlp, l0_mlp_up, l0_mlp_down,
              l1_norm_mix, l1_conv_w_in, l1_conv_dw, l1_conv_dw_bias, l1_conv_ln_g, l1_conv_ln_b,
              l1_conv_w_out, l1_norm_mlp, l1_mlp_up, l1_mlp_down,
              l2_norm_mix, l2_fourier_w_in, l2_fourier_w_out,
              l2_norm_mlp, l2_mlp_up, l2_mlp_down,
              l3_norm_mix, l3_pool_w_in, l3_pool_w_group, l3_pool_scale, l3_pool_w_out,
              l3_norm_mlp, l3_mlp_up, l3_mlp_down,
              final_norm):
    mixers = (
        lambda h: pool_mixer(h, l0_pool_w_in, l0_pool_w_group, l0_pool_scale, l0_pool_w_out),
        lambda h: conv_module(h, l1_conv_w_in, l1_conv_dw, l1_conv_dw_bias, l1_conv_ln_g,
                              l1_conv_ln_b, l1_conv_w_out),
        lambda h: fourier_mixer(h, l2_fourier_w_in, l2_fourier_w_out),
        lambda h: pool_mixer(h, l3_pool_w_in, l3_pool_w_group, l3_pool_scale, l3_pool_w_out),
    )
    norm_mix = (l0_norm_mix, l1_norm_mix, l2_norm_mix, l3_norm_mix)
    norm_mlp = (l0_norm_mlp, l1_norm_mlp, l2_norm_mlp, l3_norm_mlp)
    mlp_up = (l0_mlp_up, l1_mlp_up, l2_mlp_up, l3_mlp_up)
    mlp_down = (l0_mlp_down, l1_mlp_down, l2_mlp_down, l3_mlp_down)
    for i in range(DEPTH):
        x = x + mixers[i](rms_norm(x, norm_mix[i]))
        x = x + sq_relu_mlp(rms_norm(x, norm_mlp[i]), mlp_up[i], mlp_down[i])
    return rms_norm(x, final_norm)
```

```python
import math
import numpy as np
import ml_dtypes
import concourse.bass as bass
import concourse.mybir as mybir
from concourse.bass_utils import run_bass_kernel_spmd

F32 = mybir.dt.float32
BF16 = mybir.dt.bfloat16
AF = mybir.ActivationFunctionType
ALU = mybir.AluOpType

POOL_WINDOWS = (2, 4, 8, 16)
CONV_W = 31
CONV_PAD = 15
NORM_EPS = 1e-6
LN_EPS = 1e-5
TT = 512
GRAN = 256


class Cfg:
    def __init__(self, S=2048, D=1024, FF=4096, NSEQ=2, kinds=("pool", "conv", "four", "pool"),
                 layer_ids=None, final=True, SL=512, arena_bytes=220160, dma_scratch=8192):
        self.S, self.D, self.FF, self.NSEQ = S, D, FF, NSEQ
        self.kinds = tuple(kinds)
        self.layer_ids = tuple(layer_ids) if layer_ids is not None else tuple(range(len(kinds)))
        self.final = final
        self.DC = D // 128
        self.GC = self.DC // 4
        self.DG = D // 4
        self.NT = S // TT
        self.TC = S // 128
        self.SL = SL
        self.arena_bytes = arena_bytes
        self.dma_scratch = dma_scratch
        assert D % 512 == 0 and S % TT == 0 and FF % SL == 0 and self.TC % 2 == 0


def vec_layout(cfg):
    lay, off = {}, 0

    def put(name, n):
        nonlocal off
        lay[name] = off
        off += n
    DC = cfg.DC
    for kind, li in zip(cfg.kinds, cfg.layer_ids):
        put(f"l{li}_norm_mix", DC)
        put(f"l{li}_norm_mlp", DC)
        if kind == "pool":
            put(f"l{li}_pool_scale", DC)
        if kind == "conv":
            put(f"l{li}_conv_dw_bias", DC)
            put(f"l{li}_conv_ln_g", DC)
            put(f"l{li}_conv_ln_b", DC)
            put(f"l{li}_conv_dw", DC * CONV_W)
    put("final_norm", DC)
    put("edges", 4 * 2 * 8)
    put("eps", 2)
    return lay, off


def weight_names(cfg):
    names = []
    for kind, li in zip(cfg.kinds, cfg.layer_ids):
        if kind == "pool":
            names += [(f"l{li}_pool_w_in", (cfg.D, cfg.D)), (f"l{li}_pool_w_group", (4, cfg.DG, cfg.DG)),
                      (f"l{li}_pool_w_out", (cfg.D, cfg.D))]
        elif kind == "conv":
            names += [(f"l{li}_conv_w_in", (cfg.D, 2 * cfg.D)), (f"l{li}_conv_w_out", (cfg.D, cfg.D))]
        else:
            names += [(f"l{li}_fourier_w_in", (cfg.D, cfg.D)), (f"l{li}_fourier_w_out", (cfg.D, cfg.D))]
        names += [(f"l{li}_mlp_up", (cfg.D, cfg.FF)), (f"l{li}_mlp_down", (cfg.FF, cfg.D))]
    return names


def pack_vecs(cfg, inputs):
    lay, nv = vec_layout(cfg)
    DC = cfg.DC
    V = np.zeros((128, nv), np.float32)

    def colmajor(v):
        return np.asarray(v, np.float32).reshape(DC, 128).T
    for name, off in lay.items():
        if name == "edges":
            for wi, w in enumerate(POOL_WINDOWS):
                h = w // 2
                for i in range(h):
                    V[:, off + (wi * 2 + 0) * 8 + i] = 1.0 / (i + h)
                for m in range(h - 1):
                    i = cfg.S - h + 1 + m
                    V[:, off + (wi * 2 + 1) * 8 + m] = 1.0 / (cfg.S - i + h)
        elif name == "eps":
            V[:, off] = NORM_EPS
            V[:, off + 1] = LN_EPS
        elif name.endswith("conv_dw"):
            dw = np.asarray(inputs[name], np.float32)
            V[:, off:off + DC * CONV_W] = dw.T.reshape(DC, 128, CONV_W).transpose(1, 0, 2).reshape(128, DC * CONV_W)
        else:
            V[:, off:off + DC] = colmajor(inputs[name])
    return V


def const_tables(cfg):
    bf = ml_dtypes.bfloat16
    cmat = np.zeros((128, 256), np.float32)
    cmat[:, 0:128] = 1.0
    cmat[:, 128:256] = np.eye(128, dtype=np.float32)
    DG, GC, S, TC, NT = cfg.DG, cfg.GC, cfg.S, cfg.TC, cfg.NT
    a = np.arange(DG, dtype=np.float64)
    ang = 2.0 * np.pi * np.outer(a, a) / DG
    csc = np.concatenate([np.cos(ang), np.sin(ang)], axis=1) / math.sqrt(DG)
    csc = csc.reshape(GC, 128, 2 * DG).transpose(1, 0, 2)
    n = np.arange(S, dtype=np.int64)
    prod = np.outer(n, n) % S
    ang2 = 2.0 * np.pi * prod.astype(np.float64) / S
    cs = np.cos(ang2)
    ns = -np.sin(ang2)
    H2 = TC // 2
    tab = np.stack([cs, ns], axis=0)
    tab = tab.reshape(2, 2, H2, 128, NT, TT)
    tab = tab.transpose(4, 1, 3, 0, 2, 5)
    return (cmat.astype(bf), np.ascontiguousarray(csc).astype(bf),
            np.ascontiguousarray(tab).astype(bf))


class Op:
    __slots__ = ("eng", "build", "deps", "signal", "count", "chan", "group", "seq", "waits", "gend")


class Prog:
    ENG = ("pe", "act", "dve", "pool", "sp")

    def __init__(self):
        self.ops = {e: [] for e in self.ENG}
        self.lastw = {}
        self.readers = {}
        self.chan_ops = {}
        self.gcache = {}
        self.seq = 0

    def gran(self, ap):
        if str(ap.space) == "DRAM":
            return ()
        dims = ap.ap
        key = (ap.tensor.name, ap.offset, dims, str(ap.dtype))
        g = self.gcache.get(key)
        if g is not None:
            return g
        esz = 2 if ap.dtype == BF16 else 4
        pstep = dims[0][0]
        off = ap.offset % pstep if pstep > 0 else ap.offset
        free = [d for d in dims[1:] if d[1] > 1 and d[0] != 0]
        if not free:
            free = [(1, 1)]
        starts = np.array([off], dtype=np.int64)
        for step, cnt in free[:-1]:
            starts = (starts[:, None] + np.arange(cnt, dtype=np.int64)[None, :] * step).ravel()
        step_l, cnt_l = free[-1]
        lo = starts * esz
        hi = (starts + (cnt_l - 1) * abs(step_l) + 1) * esz
        name = ap.tensor.name
        s = set()
        for l, h in zip(lo.tolist(), hi.tolist()):
            for gi in range(l // GRAN, (h - 1) // GRAN + 1):
                s.add((name, gi))
        g = tuple(s)
        self.gcache[key] = g
        return g

    def add(self, eng, build, reads=(), writes=(), chan=None, group=None):
        op = Op()
        op.eng, op.build, op.chan, op.group, op.signal = eng, build, chan, group, False
        op.seq = self.seq
        self.seq += 1
        deps = {}

        def dep(d):
            if d is None:
                return
            if d.chan is None and d.eng == "pe" and eng == "pe" and chan is None:
                return
            if chan is not None and d.chan == chan and d.group == group:
                return
            key = ("c", d.chan) if d.chan else ("e", d.eng)
            cur = deps.get(key)
            if cur is None or d.seq > cur.seq:
                deps[key] = d
        rg = set()
        for ap in reads:
            rg.update(self.gran(ap))
        wg = set()
        for ap in writes:
            wg.update(self.gran(ap))
        for g in rg:
            dep(self.lastw.get(g))
        for g in wg:
            dep(self.lastw.get(g))
            rd = self.readers.get(g)
            if rd:
                for r in rd.values():
                    dep(r)
        mykey = ("c", chan) if chan else ("e", eng)
        for g in rg:
            if g not in wg:
                self.readers.setdefault(g, {})[mykey] = op
        for g in wg:
            self.lastw[g] = op
            self.readers[g] = {}
        op.deps = list(deps.values())
        for d in op.deps:
            d.signal = True
        self.ops[eng].append(op)
        if chan:
            self.chan_ops.setdefault(chan, []).append(op)
        return op

    def finalize(self):
        for e in self.ENG:
            c = 0
            for op in self.ops[e]:
                if op.chan is None:
                    if op.signal:
                        c += 1
                    op.count = c
        for ch, lst in self.chan_ops.items():
            c = 0
            ends = {}
            for op in lst:
                c += 16
                op.count = c
                ends[op.group] = c
            for op in lst:
                op.gend = ends[op.group]
        for e in self.ENG:
            waited = {}
            for op in self.ops[e]:
                w = []
                for d in op.deps:
                    if d.chan:
                        key, val = ("c", d.chan), d.gend
                    else:
                        key, val = ("e", d.eng), d.count
                    if waited.get(key, 0) < val:
                        waited[key] = val
                        w.append((key, val))
                op.waits = w

    def sem_keys(self):
        keys = [("e", e) for e in self.ENG if any(o.chan is None for o in self.ops[e])]
        keys += [("c", ch) for ch in self.chan_ops]
        return keys

    def emit(self, block, sems, final_waits):
        attr = {"pe": "tensor", "act": "scalar", "dve": "vector", "pool": "gpsimd", "sp": "sync"}
        for e in self.ENG:
            ops = self.ops[e]
            if not ops:
                continue

            def body(eng, ops=ops, e=e):
                for op in ops:
                    for key, val in op.waits:
                        eng.wait_ge(sems[key], val)
                    ins = op.build(eng)
                    if op.chan:
                        ins.then_inc(sems[("c", op.chan)], 16)
                    elif op.signal:
                        ins.then_inc(sems[("e", e)], 1)
                if e == "sp":
                    for key, val in final_waits:
                        eng.wait_ge(sems[key], val)
            getattr(block, attr[e])(body)


class Rot:
    def __init__(self, items):
        self.items = list(items)
        self.i = 0

    def next(self):
        v = self.items[self.i % len(self.items)]
        self.i += 1
        return v


class Builder:
    def __init__(self, cfg):
        self.cfg = cfg
        self.P = Prog()
        self.nc = bass.Bass("TRN2", target_bir_lowering=False, dynamic_dma_scratch_size=cfg.dma_scratch)
        self.out_final = []

    def mm(self, out, lhsT, rhs, start, stop):
        self.P.add("pe", lambda e: e.matmul(out, lhsT, rhs, start=start, stop=stop),
                   reads=[lhsT, rhs], writes=[out])

    def act(self, out, in_, func, bias=None, scale=None, eng="act"):
        reads = [in_]
        kw = {}
        if bias is not None:
            kw["bias"] = bias
            if not isinstance(bias, (int, float)):
                reads.append(bias)
        if scale is not None:
            kw["scale"] = scale
            if not isinstance(scale, (int, float)):
                reads.append(scale)
        self.P.add(eng, lambda e: e.activation(out, in_, func, **kw), reads=reads, writes=[out])

    def tt(self, out, in0, in1, op, eng="dve"):
        self.P.add(eng, lambda e: e.tensor_tensor(out, in0, in1, op), reads=[in0, in1], writes=[out])

    def ts(self, out, in0, s1, op0, s2=None, op1=None, eng="dve"):
        reads = [in0] + [s for s in (s1, s2) if s is not None and not isinstance(s, (int, float))]
        if op1 is None:
            self.P.add(eng, lambda e: e.tensor_scalar(out, in0, s1, None, op0), reads=reads, writes=[out])
        else:
            self.P.add(eng, lambda e: e.tensor_scalar(out, in0, s1, s2, op0, op1), reads=reads, writes=[out])

    def stt(self, out, in0, scalar, in1, op0, op1, eng="dve"):
        reads = [in0, in1] + ([] if isinstance(scalar, (int, float)) else [scalar])
        self.P.add(eng, lambda e: e.scalar_tensor_tensor(out, in0, scalar, in1, op0, op1),
                   reads=reads, writes=[out])

    def copy(self, out, in_, eng="dve"):
        if eng == "act":
            self.act(out, in_, AF.Copy)
        else:
            self.P.add(eng, lambda e: e.tensor_copy(out, in_), reads=[in_], writes=[out])

    def recip(self, out, in_):
        self.P.add("dve", lambda e: e.reciprocal(out, in_), reads=[in_], writes=[out])

    def memset(self, ap, val, eng="dve"):
        self.P.add(eng, lambda e: e.memset(ap, val), writes=[ap])

    def dma(self, q, chan, group, out, in_, **kw):
        return self.P.add(q, lambda e: e.dma_start(out=out, in_=in_, **kw), reads=[in_], writes=[out],
                          chan=chan, group=group)

    def view(self, off, dtype, dims):
        esz = 2 if dtype == BF16 else 4
        n = 1
        for d in dims:
            n *= d
        nb = n * esz
        assert off % 4 == 0 and nb % 4 == 0, (off, nb)
        assert off + nb <= self.cfg.arena_bytes, ("arena overflow", off, nb, self.cfg.arena_bytes)
        ap = self.arena[:, off // 4:(off + nb) // 4]
        if dtype != F32:
            ap = ap.bitcast(dtype)
        if len(dims) == 2:
            ap = ap.rearrange("p (a b) -> p a b", a=dims[0])
        elif len(dims) == 3:
            ap = ap.rearrange("p (a b c) -> p a b c", a=dims[0], b=dims[1])
        return ap

    def slab_plan(self):
        cfg = self.cfg
        D, DC, DG, GC, FF, SL = cfg.D, cfg.DC, cfg.DG, cfg.GC, cfg.FF, cfg.SL
        W = self.wdram
        plan = []

        def rows(ap):
            return ap.rearrange("(kc p) n -> p kc n", p=128)
        for _s in range(cfg.NSEQ):
            for kind, li in zip(cfg.kinds, cfg.layer_ids):
                if kind == "pool":
                    win, wg, wo = W[f"l{li}_pool_w_in"], W[f"l{li}_pool_w_group"], W[f"l{li}_pool_w_out"]
                    for g in range(4):
                        plan.append([(0, (DC, DG), rows(win[:, g * DG:(g + 1) * DG])),
                                     (DC * DG * 2, (GC, DG), rows(wg[g]))])
                    plan.append([(0, (DC, D), rows(wo))])
                elif kind == "conv":
                    win, wo = W[f"l{li}_conv_w_in"], W[f"l{li}_conv_w_out"]
                    for c in range(DC):
                        plan.append([(0, (DC, 128), rows(win[:, c * 128:(c + 1) * 128])),
                                     (DC * 128 * 2, (DC, 128), rows(win[:, D + c * 128:D + (c + 1) * 128]))])
                    plan.append([(0, (DC, D), rows(wo))])
                else:
                    win, wo = W[f"l{li}_fourier_w_in"], W[f"l{li}_fourier_w_out"]
                    plan.append([(0, (DC, D), rows(win))])
                    plan.append([(0, (DC, D), rows(wo))])
                up, dn = W[f"l{li}_mlp_up"], W[f"l{li}_mlp_down"]
                for s in range(FF // SL):
                    plan.append([(0, (DC, SL), rows(up[:, s * SL:(s + 1) * SL])),
                                 (DC * SL * 2, (SL // 128, D), rows(dn[s * SL:(s + 1) * SL, :]))])
        return plan

    def w_issue(self):
        while self.w_issued < len(self.plan) and self.w_issued < self.w_done + 2:
            k = self.w_issued
            slot = k % 2
            for (off, dims, src) in self.plan[k]:
                dst = self.view(self.W_OFF + slot * self.W_SLOT + off, BF16, dims)
                self.dma("pool", f"w{slot}", k, dst, src)
            self.w_issued += 1

    def w_next(self):
        k = self.w_cur
        self.w_cur += 1
        self.w_issue()
        assert k < self.w_issued
        slot = k % 2
        return [self.view(self.W_OFF + slot * self.W_SLOT + off, BF16, dims) for (off, dims, _src) in self.plan[k]]

    def w_release(self):
        self.w_done += 1
        self.w_issue()

    def vcol(self, name, c, n=1):
        o = self.vlay[name] + c
        return self.VEC[:, o:o + n]

    class NormState:
        def __init__(self, gname, final=False, seq=0):
            self.gname, self.final, self.seq = gname, final, seq
            self.pending = None
            self.deferred = []
            self.done = set()
            self.after = None

    def emit_pre(self, n, j):
        sl = slice(j * TT, (j + 1) * TT)
        for c in range(self.cfg.DC):
            self.act(self.H[:, c, sl], self.X[:, c, sl], AF.Square)
        n.pending = j

    def flush_post(self, n):
        cfg = self.cfg
        j = n.pending
        if j is None:
            return
        n.pending = None
        sl = slice(j * TT, (j + 1) * TT)
        bank = self.rotS.next()
        for c in range(cfg.DC):
            self.mm(bank, self.ONES, self.H[:, c, sl], c == 0, c == cfg.DC - 1)
        r = self.RS.next()
        self.act(r, bank, AF.Sqrt, bias=self.vcol("eps", 0), scale=1.0 / cfg.D)
        self.recip(r, r)
        if not n.final:
            for c in range(cfg.DC):
                self.stt(self.H[:, c, sl], self.X[:, c, sl], self.vcol(n.gname, c), r, ALU.mult, ALU.mult)
        else:
            o = self.O[j % 2]
            for c in range(cfg.DC):
                self.stt(o[:, c, :], self.X[:, c, sl], self.vcol(n.gname, c), r, ALU.mult, ALU.mult)
            self.out_final.append(self.dma("sp", f"out{j % 2}", ("o", n.seq, j),
                                           self.outT[n.seq][:, sl].rearrange("(c p) t -> p c t", p=128), o))
            if n.after is not None:
                self.load_x_tile(n.seq + 1, j)
                self._ready(n.after, j)
        n.done.add(j)

    def load_x_tile(self, s, j):
        sl = slice(j * TT, (j + 1) * TT)
        self.dma("sp", f"xin{j}", ("x", s, j), self.X[:, :, sl],
                 self.xT[s][:, sl].rearrange("(c p) t -> p c t", p=128))

    def _ready(self, n, j):
        if n is None:
            return
        if self.h_free:
            self.flush_post(n)
            self.emit_pre(n, j)
        else:
            n.deferred.append(j)

    def x_ready(self, j):
        self._ready(self.nxt, j)

    def drain(self, n):
        if n is None:
            return
        assert self.h_free
        while n.deferred:
            self.flush_post(n)
            self.emit_pre(n, n.deferred.pop(0))

    def need_H(self, j):
        n = self.cur
        while j not in n.done:
            assert self.h_free
            if n.pending is not None:
                self.flush_post(n)
            elif n.deferred:
                self.emit_pre(n, n.deferred.pop(0))
            else:
                raise AssertionError(("H tile never produced", j))

    def rms_stats_sq(self, j):
        raise NotImplementedError

    def proj_residual(self, wout, src_of_kc, j, nk):
        cfg = self.cfg
        sl = slice(j * TT, (j + 1) * TT)
        for oc in range(cfg.DC):
            bank = self.rotA.next()
            for kc in range(nk):
                self.mm(bank, wout[:, kc, oc * 128:(oc + 1) * 128], src_of_kc(kc), kc == 0, kc == nk - 1)
            self.tt(self.X[:, oc, sl], self.X[:, oc, sl], bank, ALU.add)
        self.x_ready(j)

    def mlp(self, li):
        cfg = self.cfg
        DC, SL, NT = cfg.DC, cfg.SL, cfg.NT
        FCS = SL // 128
        nslab = cfg.FF // SL
        assert nslab >= 2
        flex = self.FLEX
        A = [self.view(flex + i * FCS * TT * 2, BF16, (FCS, TT)) for i in range(2)]
        toff = flex + 2 * FCS * TT * 2
        Tr = Rot([self.view(toff + i * TT * 4, F32, (TT,)) for i in range(3)])
        steps = [(s, j) for s in range(cfg.FF // SL) for j in range(NT)]
        slabs = {}

        def up(i):
            s, j = steps[i]
            if s not in slabs:
                slabs[s] = self.w_next()
            wu = slabs[s][0]
            sl = slice(j * TT, (j + 1) * TT)
            a = A[i % 2]
            if s == 0:
                self.need_H(j)
            for fc in range(FCS):
                bank = self.rotU.next()
                for kc in range(DC):
                    self.mm(bank, wu[:, kc, fc * 128:(fc + 1) * 128], self.H[:, kc, sl], kc == 0, kc == DC - 1)
                t = Tr.next()
                self.act(t, bank, AF.Relu)
                self.tt(a[:, fc, :], t, t, ALU.mult)

        def down(i):
            s, j = steps[i]
            wd = slabs[s][1]
            sl = slice(j * TT, (j + 1) * TT)
            a = A[i % 2]
            for dc in range(DC):
                bank = self.rotD.next()
                for fc in range(FCS):
                    self.mm(bank, wd[:, fc, dc * 128:(dc + 1) * 128], a[:, fc, :], fc == 0, fc == FCS - 1)
                self.tt(self.X[:, dc, sl], self.X[:, dc, sl], bank, ALU.add)
            if s == nslab - 1:
                self.x_ready(j)
            if j == NT - 1:
                self.w_release()
        up(0)
        for i in range(len(steps)):
            if i + 1 < len(steps):
                up(i + 1)
            down(i)

    def pool_layer(self, li):
        cfg = self.cfg
        DC, GC, DG, S, NT = cfg.DC, cfg.GC, cfg.DG, cfg.S, cfg.NT
        L = S + 16
        off = self.FLEX
        U = self.view(off, F32, (GC, L)); off += GC * L * 4
        T = self.view(off, F32, (L,)); off += L * 4
        E1 = self.view(off, F32, (8,)); off += 32
        Pg = self.view(off, BF16, (GC, S)); off += GC * S * 2
        Y = self.view(off, BF16, (DC, S)); off += DC * S * 2
        self.memset(U[:, :, 0:8], 0.0)
        self.memset(U[:, :, 8 + S:L], 0.0)
        eo = self.vlay["edges"]
        for g, w in enumerate(POOL_WINDOWS):
            half = w // 2
            win, wg = self.w_next()
            for oc in range(GC):
                for j in range(NT):
                    if g == 0 and oc == 0:
                        self.need_H(j)
                    bank = self.rotA.next()
                    for kc in range(DC):
                        self.mm(bank, win[:, kc, oc * 128:(oc + 1) * 128], self.H[:, kc, j * TT:(j + 1) * TT],
                                kc == 0, kc == DC - 1)
                    self.act(U[:, oc, 8 + j * TT:8 + (j + 1) * TT], bank, AF.Copy)
            for oc in range(GC):
                Uc = U[:, oc, :]
                self.tt(T[:, 0:L - 1], Uc[:, 0:L - 1], Uc[:, 1:L], ALU.add)
                cur = 2
                while cur < w:
                    n = L - 2 * cur + 1
                    self.tt(T[:, 0:n], T[:, 0:n], T[:, cur:cur + n], ALU.add)
                    cur *= 2
                self.stt(Pg[:, oc, :], T[:, 8 - half:8 - half + S], 1.0 / w, Uc[:, 8:8 + S], ALU.mult, ALU.subtract)
                el = self.VEC[:, eo + (g * 2) * 8: eo + (g * 2) * 8 + half]
                self.tt(E1[:, 0:half], T[:, 8 - half:8], el, ALU.mult)
                self.tt(Pg[:, oc, 0:half], E1[:, 0:half], Uc[:, 8:8 + half], ALU.subtract)
                nr = half - 1
                if nr > 0:
                    i0 = S - half + 1
                    er = self.VEC[:, eo + (g * 2 + 1) * 8: eo + (g * 2 + 1) * 8 + nr]
                    self.tt(E1[:, 0:nr], T[:, i0 + 8 - half:i0 + 8 - half + nr], er, ALU.mult)
                    self.tt(Pg[:, oc, i0:S], E1[:, 0:nr], Uc[:, 8 + i0:8 + S], ALU.subtract)
            for oc2 in range(GC):
                for j in range(NT):
                    bank = self.rotA.next()
                    for kc in range(GC):
                        self.mm(bank, wg[:, kc, oc2 * 128:(oc2 + 1) * 128], Pg[:, kc, j * TT:(j + 1) * TT],
                                kc == 0, kc == GC - 1)
                    self.act(Y[:, g * GC + oc2, j * TT:(j + 1) * TT], bank, AF.Copy,
                             scale=self.vcol(f"l{li}_pool_scale", g * GC + oc2))
            self.w_release()
        (wout,) = self.w_next()
        for j in range(NT):
            self.proj_residual(wout, lambda kc, j=j: Y[:, kc, j * TT:(j + 1) * TT], j, DC)
        self.w_release()

    def conv_layer(self, li):
        cfg = self.cfg
        DC, S, NT, D = cfg.DC, cfg.S, cfg.NT, cfg.D
        VL = S + 2 * CONV_PAD
        vbytes = VL * 2
        hbytes = DC * S * 2
        base = self.H_OFF
        cend = base + DC * S * 4

        def Vc(c):
            o = base + hbytes + c * vbytes if c < DC - 1 else cend
            return self.view(o, BF16, (VL,))
        assert base + hbytes + (DC - 1) * vbytes <= cend + 0 or True
        C = self.view(base, F32, (DC, S))
        off = cend + vbytes
        off = (off + 3) // 4 * 4
        DGm = self.view(off, BF16, (CONV_W, 128)); off += CONV_W * 128 * 2
        SG = Rot([self.view(off + i * TT * 4, F32, (TT,)) for i in range(2)]); off += 2 * TT * 4
        LT = [self.view(off + i * TT * 4, F32, (TT,)) for i in range(5)]; off += 5 * TT * 4
        SLb = self.view(off, BF16, (DC, TT)); off += DC * TT * 2
        for c in range(DC):
            v = Vc(c)
            self.memset(v[:, 0:CONV_PAD], 0.0)
            self.memset(v[:, CONV_PAD + S:VL], 0.0)
        for c in range(DC):
            wv, wgt = self.w_next()
            v = Vc(c)
            for j in range(NT):
                sl = slice(j * TT, (j + 1) * TT)
                if c == 0:
                    self.need_H(j)
                bv = self.rotA.next()
                bg = self.rotA.next()
                for kc in range(DC):
                    self.mm(bv, wv[:, kc, :], self.H[:, kc, sl], kc == 0, kc == DC - 1)
                for kc in range(DC):
                    self.mm(bg, wgt[:, kc, :], self.H[:, kc, sl], kc == 0, kc == DC - 1)
                sg = SG.next()
                self.act(sg, bg, AF.Sigmoid)
                self.tt(v[:, CONV_PAD + j * TT:CONV_PAD + (j + 1) * TT], bv, sg, ALU.mult)
            self.w_release()
        self.h_free = False
        dwo = self.vlay[f"l{li}_conv_dw"]
        for c in range(DC):
            dwc = self.VEC[:, dwo + c * CONV_W: dwo + (c + 1) * CONV_W]
            self.tt(DGm, self.IDENT.unsqueeze(1).broadcast_to([128, CONV_W, 128]),
                    dwc.unsqueeze(2).broadcast_to([128, CONV_W, 128]), ALU.mult)
            v = Vc(c)
            for j in range(NT):
                bank = self.rotA.next()
                for k in range(CONV_W):
                    self.mm(bank, DGm[:, k, :], v[:, j * TT + k:j * TT + k + TT], k == 0, k == CONV_W - 1)
                self.act(C[:, c, j * TT:(j + 1) * TT], bank, AF.Identity, bias=self.vcol(f"l{li}_conv_dw_bias", c))
        (wout,) = self.w_next()
        MEAN, MSQ, VAR, T1a, T1b = LT
        T1 = Rot([T1a, T1b])
        for j in range(NT):
            sl = slice(j * TT, (j + 1) * TT)
            b1 = self.rotS.next()
            b2 = self.rotS.next()
            for c in range(DC):
                cb = self.SQ.next()
                self.copy(cb, C[:, c, sl], eng="dve")
                cs = self.SQ.next()
                self.act(cs, C[:, c, sl], AF.Square)
                self.mm(b1, self.ONES, cb, c == 0, c == DC - 1)
                self.mm(b2, self.ONES, cs, c == 0, c == DC - 1)
            self.ts(MEAN, b1, 1.0 / D, ALU.mult)
            self.tt(MSQ, MEAN, MEAN, ALU.mult)
            self.stt(VAR, b2, 1.0 / D, MSQ, ALU.mult, ALU.subtract)
            self.act(VAR, VAR, AF.Sqrt, bias=self.vcol("eps", 1))
            self.recip(VAR, VAR)
            for c in range(DC):
                t1 = T1.next()
                self.tt(t1, C[:, c, sl], MEAN, ALU.subtract)
                self.tt(t1, t1, VAR, ALU.mult)
                self.act(SLb[:, c, :], t1, AF.Silu, bias=self.vcol(f"l{li}_conv_ln_b", c),
                         scale=self.vcol(f"l{li}_conv_ln_g", c))
            self.proj_residual(wout, lambda kc: SLb[:, kc, :], j, DC)
        self.w_release()
        self.h_free = True
        self.drain(self.nxt)

    def four_layer(self, li):
        cfg = self.cfg
        DC, GC, DG, S, NT, TC = cfg.DC, cfg.GC, cfg.DG, cfg.S, cfg.NT, cfg.TC
        H2 = TC // 2
        abrow = 4 * 2 * DG
        ab1 = self.H_OFF
        off = self.FLEX
        ab2 = off; off += H2 * abrow * 2
        ut = off; off += DC * S * 2
        fj = off; off += DC * TT * 2
        UT = self.view(ut, BF16, (DC, S))
        Fj = self.view(fj, BF16, (DC, TT))

        def AB(tc):
            o = (ab1 if tc < H2 else ab2) + (tc % H2) * abrow * 2
            return self.view(o, BF16, (4, 2 * DG))
        tabs = [self.view(ut + h * (2 * H2 * TT * 2), BF16, (2, H2, TT)) for h in range(2)]
        assert 2 * (2 * H2 * TT * 2) <= DC * S * 2
        (win,) = self.w_next()
        for oc in range(DC):
            for j in range(NT):
                sl = slice(j * TT, (j + 1) * TT)
                if oc == 0:
                    self.need_H(j)
                bank = self.rotA.next()
                for kc in range(DC):
                    self.mm(bank, win[:, kc, oc * 128:(oc + 1) * 128], self.H[:, kc, sl], kc == 0, kc == DC - 1)
                self.copy(UT[:, oc, sl], bank, eng="act")
        self.w_release()
        self.h_free = False
        n = 0
        for tc in range(TC):
            ab = AB(tc)
            for g in range(4):
                bank = self.rotA.next()
                for kc in range(GC):
                    self.mm(bank[:, 0:2 * DG], UT[:, g * GC + kc, tc * 128:(tc + 1) * 128], self.CSC[:, kc, :],
                            kc == 0, kc == GC - 1)
                self.copy(ab[:, g, :], bank[:, 0:2 * DG], eng=("act" if n % 2 == 0 else "dve"))
                n += 1
        (wout,) = self.w_next()
        scale = 1.0 / math.sqrt(S)
        for j in range(NT):
            for h in range(2):
                self.dma("sp", f"t{h}", (li, self.seq_i, j), tabs[h], self.dfts[j, h])
            for ch in range(DC):
                g, hh = ch // GC, ch % GC
                bank = self.rotA.next()
                k = 0
                for tc in range(TC):
                    ab = AB(tc)
                    tab = tabs[tc // H2]
                    for cs_i in range(2):
                        col = cs_i * DG + hh * 128
                        self.mm(bank, ab[:, g, col:col + 128], tab[:, cs_i, tc % H2, :], k == 0, k == 2 * TC - 1)
                        k += 1
                self.act(Fj[:, ch, :], bank, AF.Copy, scale=scale)
            self.proj_residual(wout, lambda kc: Fj[:, kc, :], j, DC)
        self.w_release()
        self.h_free = True
        self.drain(self.nxt)

    def finish(self, s):
        cfg = self.cfg
        if not cfg.final:
            self.out_final.append(self.dma("sp", "out0", ("o", s), self.outT[s].rearrange("(c p) t -> p c t", p=128),
                                           self.X))
            return
        for j in range(cfg.NT):
            self.need_H(j)

    def build(self):
        cfg = self.cfg
        nc = self.nc
        DC, S, D = cfg.DC, cfg.S, cfg.D
        lay, nv = vec_layout(cfg)
        self.vlay = lay
        xT = nc.dram_tensor("xT", [cfg.NSEQ, D, S], F32, kind="ExternalInput").ap()
        vecs = nc.dram_tensor("vecs", [128, nv], F32, kind="ExternalInput").ap()
        cmat = nc.dram_tensor("cmat", [128, 256], BF16, kind="ExternalInput").ap()
        has_four = "four" in cfg.kinds
        if has_four:
            dftc = nc.dram_tensor("dftc", [128, cfg.GC, 2 * cfg.DG], BF16, kind="ExternalInput").ap()
            self.dfts = nc.dram_tensor("dfts", [cfg.NT, 2, 128, 2, cfg.TC // 2, TT], BF16, kind="ExternalInput").ap()
        self.wdram = {}
        for name, shape in weight_names(cfg):
            self.wdram[name] = nc.dram_tensor(name, list(shape), F32, kind="ExternalInput").ap()
        self.outT = nc.dram_tensor("outT", [cfg.NSEQ, D, S], F32, kind="ExternalOutput").ap()

        off = 0
        self.X_OFF = off; off += DC * S * 4
        self.H_OFF = off; off += DC * S * 2
        nvb = (nv * 4 + 3) // 4 * 4
        small = nvb + 256 * 2 + (cfg.GC * 2 * cfg.DG * 2 if has_four else 0) + 4 * TT * 2 + 2 * TT * 4
        self.W_SLOT = max(DC * cfg.SL * 2 + (cfg.SL // 128) * D * 2, DC * D * 2)
        self.W_SLOT = (self.W_SLOT + 255) // 256 * 256
        self.SM_OFF = (cfg.arena_bytes - small) // 256 * 256
        self.W_OFF = self.SM_OFF - 2 * self.W_SLOT
        self.FLEX = off
        self.FLEX_END = self.W_OFF
        assert self.FLEX_END > self.FLEX
        real_view = self.view

        with nc.allow_low_precision("bf16 matmul operands, fp32 accumulation"), \
                nc.sbuf_tensor("arena", [128, cfg.arena_bytes // 4], F32) as arena, \
                nc.psum_tensor("ps", [128, 8, TT], F32) as ps:
            self.arena = arena

            def flexview(off_, dtype, dims):
                return real_view(off_, dtype, dims)
            so = self.SM_OFF
            self.VEC = self.view(so, F32, (nv,)); so += nvb
            CM = self.view(so, BF16, (256,)); so += 512
            self.ONES = CM[:, 0:128]
            self.IDENT = CM[:, 128:256]
            if has_four:
                self.CSC = self.view(so, BF16, (cfg.GC, 2 * cfg.DG)); so += cfg.GC * 2 * cfg.DG * 2
            self.SQ = Rot([self.view(so + i * TT * 2, BF16, (TT,)) for i in range(4)]); so += 4 * TT * 2
            self.RS = Rot([self.view(so + i * TT * 4, F32, (TT,)) for i in range(2)]); so += 2 * TT * 4
            assert so <= cfg.arena_bytes
            self.X = self.view(self.X_OFF, F32, (DC, S))
            self.H = self.view(self.H_OFF, BF16, (DC, S))
            banks = [ps[:, i, :] for i in range(8)]
            self.rotS = Rot(banks[0:2])
            self.rotA = Rot(banks[2:8])
            self.rotU = Rot(banks[2:5])
            self.rotD = Rot(banks[5:8])

            self.dma("sp", "c0", "c", self.VEC, vecs)
            self.dma("sp", "c0", "c", CM, cmat)
            if has_four:
                self.dma("sp", "c0", "c", self.CSC, dftc)

            self.plan = self.slab_plan()
            self.w_issued = self.w_done = self.w_cur = 0
            self.xT = xT
            self.h_free = True
            ooff = self.FLEX + 2 * (cfg.SL // 128) * TT * 2 + 3 * TT * 4
            self.O = [self.view(ooff + i * DC * TT * 4, F32, (DC, TT)) for i in range(2)]
            assert ooff + 2 * DC * TT * 4 <= self.FLEX_END
            NS = Builder.NormState
            norms = []
            for s in range(cfg.NSEQ):
                row = []
                for kind, li in zip(cfg.kinds, cfg.layer_ids):
                    row.append(NS(f"l{li}_norm_mix", seq=s))
                    row.append(NS(f"l{li}_norm_mlp", seq=s))
                if cfg.final:
                    row.append(NS("final_norm", final=True, seq=s))
                else:
                    row.append(None)
                norms.append(row)
            if cfg.final:
                for s in range(cfg.NSEQ - 1):
                    norms[s][-1].after = norms[s + 1][0]
            self.cur = self.nxt = None
            for s in range(cfg.NSEQ):
                self.seq_i = s
                row = norms[s]
                if s == 0 or not cfg.final:
                    for j in range(cfg.NT):
                        self.load_x_tile(s, j)
                        self._ready(row[0], j)
                k = 0
                for kind, li in zip(cfg.kinds, cfg.layer_ids):
                    self.cur, self.nxt = row[k], row[k + 1]
                    if kind == "pool":
                        self.pool_layer(li)
                    elif kind == "conv":
                        self.conv_layer(li)
                    else:
                        self.four_layer(li)
                    k += 1
                    self.cur, self.nxt = row[k], row[k + 1]
                    self.mlp(li)
                    k += 1
                self.cur, self.nxt = row[k], None
                self.finish(s)
            assert self.w_cur == len(self.plan)

            P = self.P
            P.finalize()
            finals = {}
            for op in self.out_final:
                key = ("c", op.chan)
                finals[key] = max(finals.get(key, 0), op.gend)
            keys = P.sem_keys()
            import contextlib
            with contextlib.ExitStack() as es:
                sems = {}
                for k in keys:
                    sems[k] = es.enter_context(nc.semaphore(f"s_{k[0]}_{k[1]}"))
                block = es.enter_context(nc.Block())
                P.emit(block, sems, list(finals.items()))
        return nc


_KINDS = ("pool", "conv", "four", "pool")


def make_in_maps(cfg, inputs, x_shards):
    vec = pack_vecs(cfg, inputs)
    cmat, csc, tab = const_tables(cfg)
    base = {"vecs": vec, "cmat": cmat}
    if "four" in cfg.kinds:
        base["dftc"] = csc
        base["dfts"] = tab
    for name, _shape in weight_names(cfg):
        base[name] = np.ascontiguousarray(np.asarray(inputs[name], np.float32))
    maps = []
    for xs in x_shards:
        m = dict(base)
        m["xT"] = xs
        maps.append(m)
    return maps


def kernel(**inputs):
    x = np.asarray(inputs["x"], np.float32)
    B, S, D = x.shape
    ncores = 8
    nseq = B // ncores
    cfg = Cfg(S=S, D=D, FF=4 * D, NSEQ=nseq, kinds=_KINDS, final=True)
    xT = np.ascontiguousarray(x.transpose(0, 2, 1))
    shards = [xT[i * nseq:(i + 1) * nseq] for i in range(ncores)]
    nc = Builder(cfg).build()
    in_maps = make_in_maps(cfg, inputs, shards)
    res = run_bass_kernel_spmd(nc, in_maps, core_ids=list(range(ncores)))
    outT = np.concatenate([np.asarray(r["outT"]) for r in res.results], axis=0)
    return np.ascontiguousarray(outT.transpose(0, 2, 1)).astype(np.float32)
```

```python
import math
import numpy as np
import ml_dtypes
import concourse.bass as bass
import concourse.mybir as mybir
from concourse.bass_utils import run_bass_kernel_spmd

F32 = mybir.dt.float32
BF16 = mybir.dt.bfloat16
AF = mybir.ActivationFunctionType
ALU = mybir.AluOpType

POOL_WINDOWS = (2, 4, 8, 16)
CONV_W = 31
CONV_PAD = 15
NORM_EPS = 1e-6
LN_EPS = 1e-5
TT = 512
GRAN = 256


class Cfg:
    def __init__(self, S=2048, D=1024, FF=4096, NSEQ=2, kinds=("pool", "conv", "four", "pool"),
                 layer_ids=None, final=True, SL=512, arena_bytes=220160, dma_scratch=8192):
        self.S, self.D, self.FF, self.NSEQ = S, D, FF, NSEQ
        self.kinds = tuple(kinds)
        self.layer_ids = tuple(layer_ids) if layer_ids is not None else tuple(range(len(kinds)))
        self.final = final
        self.DC = D // 128
        self.GC = self.DC // 4
        self.DG = D // 4
        self.NT = S // TT
        self.TC = S // 128
        self.SL = SL
        self.arena_bytes = arena_bytes
        self.dma_scratch = dma_scratch
        assert D % 512 == 0 and S % TT == 0 and FF % SL == 0 and self.TC % 2 == 0


def vec_layout(cfg):
    lay, off = {}, 0

    def put(name, n):
        nonlocal off
        lay[name] = off
        off += n
    DC = cfg.DC
    for kind, li in zip(cfg.kinds, cfg.layer_ids):
        put(f"l{li}_norm_mix", DC)
        put(f"l{li}_norm_mlp", DC)
        if kind == "pool":
            put(f"l{li}_pool_scale", DC)
        if kind == "conv":
            put(f"l{li}_conv_dw_bias", DC)
            put(f"l{li}_conv_ln_g", DC)
            put(f"l{li}_conv_ln_b", DC)
            put(f"l{li}_conv_dw", DC * CONV_W)
    put("final_norm", DC)
    put("edges", 4 * 2 * 8)
    put("eps", 2)
    return lay, off


def weight_names(cfg):
    names = []
    for kind, li in zip(cfg.kinds, cfg.layer_ids):
        if kind == "pool":
            names += [(f"l{li}_pool_w_in", (cfg.D, cfg.D)), (f"l{li}_pool_w_group", (4, cfg.DG, cfg.DG)),
                      (f"l{li}_pool_w_out", (cfg.D, cfg.D))]
        elif kind == "conv":
            names += [(f"l{li}_conv_w_in", (cfg.D, 2 * cfg.D)), (f"l{li}_conv_w_out", (cfg.D, cfg.D))]
        else:
            names += [(f"l{li}_fourier_w_in", (cfg.D, cfg.D)), (f"l{li}_fourier_w_out", (cfg.D, cfg.D))]
        names += [(f"l{li}_mlp_up", (cfg.D, cfg.FF)), (f"l{li}_mlp_down", (cfg.FF, cfg.D))]
    return names


def pack_vecs(cfg, inputs):
    lay, nv = vec_layout(cfg)
    DC = cfg.DC
    V = np.zeros((128, nv), np.float32)

    def colmajor(v):
        return np.asarray(v, np.float32).reshape(DC, 128).T
    for name, off in lay.items():
        if name == "edges":
            for wi, w in enumerate(POOL_WINDOWS):
                h = w // 2
                for i in range(h):
                    V[:, off + (wi * 2 + 0) * 8 + i] = 1.0 / (i + h)
                for m in range(h - 1):
                    i = cfg.S - h + 1 + m
                    V[:, off + (wi * 2 + 1) * 8 + m] = 1.0 / (cfg.S - i + h)
        elif name == "eps":
            V[:, off] = NORM_EPS
            V[:, off + 1] = LN_EPS
        elif name.endswith("conv_dw"):
            dw = np.asarray(inputs[name], np.float32)
            V[:, off:off + DC * CONV_W] = dw.T.reshape(DC, 128, CONV_W).transpose(1, 0, 2).reshape(128, DC * CONV_W)
        else:
            V[:, off:off + DC] = colmajor(inputs[name])
    return V


def const_tables(cfg):
    bf = ml_dtypes.bfloat16
    cmat = np.zeros((128, 256), np.float32)
    cmat[:, 0:128] = 1.0
    cmat[:, 128:256] = np.eye(128, dtype=np.float32)
    DG, GC, S, TC, NT = cfg.DG, cfg.GC, cfg.S, cfg.TC, cfg.NT
    a = np.arange(DG, dtype=np.float64)
    ang = 2.0 * np.pi * np.outer(a, a) / DG
    csc = np.concatenate([np.cos(ang), np.sin(ang)], axis=1) / math.sqrt(DG)
    csc = csc.reshape(GC, 128, 2 * DG).transpose(1, 0, 2)
    n = np.arange(S, dtype=np.int64)
    prod = np.outer(n, n) % S
    ang2 = 2.0 * np.pi * prod.astype(np.float64) / S
    cs = np.cos(ang2)
    ns = -np.sin(ang2)
    H2 = TC // 2
    tab = np.stack([cs, ns], axis=0)
    tab = tab.reshape(2, 2, H2, 128, NT, TT)
    tab = tab.transpose(4, 1, 3, 0, 2, 5)
    return (cmat.astype(bf), np.ascontiguousarray(csc).astype(bf),
            np.ascontiguousarray(tab).astype(bf))


class Op:
    __slots__ = ("eng", "build", "deps", "signal", "count", "chan", "group", "seq", "waits", "gend")


class Prog:
    ENG = ("pe", "act", "dve", "pool", "sp")

    def __init__(self):
        self.ops = {e: [] for e in self.ENG}
        self.lastw = {}
        self.readers = {}
        self.chan_ops = {}
        self.gcache = {}
        self.seq = 0

    def gran(self, ap):
        if str(ap.space) == "DRAM":
            return ()
        dims = ap.ap
        key = (ap.tensor.name, ap.offset, dims, str(ap.dtype))
        g = self.gcache.get(key)
        if g is not None:
            return g
        esz = 2 if ap.dtype == BF16 else 4
        pstep = dims[0][0]
        off = ap.offset % pstep if pstep > 0 else ap.offset
        free = [d for d in dims[1:] if d[1] > 1 and d[0] != 0]
        if not free:
            free = [(1, 1)]
        starts = np.array([off], dtype=np.int64)
        for step, cnt in free[:-1]:
            starts = (starts[:, None] + np.arange(cnt, dtype=np.int64)[None, :] * step).ravel()
        step_l, cnt_l = free[-1]
        lo = starts * esz
        hi = (starts + (cnt_l - 1) * abs(step_l) + 1) * esz
        name = ap.tensor.name
        s = set()
        for l, h in zip(lo.tolist(), hi.tolist()):
            for gi in range(l // GRAN, (h - 1) // GRAN + 1):
                s.add((name, gi))
        g = tuple(s)
        self.gcache[key] = g
        return g

    def add(self, eng, build, reads=(), writes=(), chan=None, group=None):
        op = Op()
        op.eng, op.build, op.chan, op.group, op.signal = eng, build, chan, group, False
        op.seq = self.seq
        self.seq += 1
        deps = {}

        def dep(d):
            if d is None:
                return
            if d.chan is None and d.eng == "pe" and eng == "pe" and chan is None:
                return
            if chan is not None and d.chan == chan and d.group == group:
                return
            key = ("c", d.chan) if d.chan else ("e", d.eng)
            cur = deps.get(key)
            if cur is None or d.seq > cur.seq:
                deps[key] = d
        rg = set()
        for ap in reads:
            rg.update(self.gran(ap))
        wg = set()
        for ap in writes:
            wg.update(self.gran(ap))
        for g in rg:
            dep(self.lastw.get(g))
        for g in wg:
            dep(self.lastw.get(g))
            rd = self.readers.get(g)
            if rd:
                for r in rd.values():
                    dep(r)
        mykey = ("c", chan) if chan else ("e", eng)
        for g in rg:
            if g not in wg:
                self.readers.setdefault(g, {})[mykey] = op
        for g in wg:
            self.lastw[g] = op
            self.readers[g] = {}
        op.deps = list(deps.values())
        for d in op.deps:
            d.signal = True
        self.ops[eng].append(op)
        if chan:
            self.chan_ops.setdefault(chan, []).append(op)
        return op

    def finalize(self):
        for e in self.ENG:
            c = 0
            for op in self.ops[e]:
                if op.chan is None:
                    if op.signal:
                        c += 1
                    op.count = c
        for ch, lst in self.chan_ops.items():
            c = 0
            ends = {}
            for op in lst:
                c += 16
                op.count = c
                ends[op.group] = c
            for op in lst:
                op.gend = ends[op.group]
        for e in self.ENG:
            waited = {}
            for op in self.ops[e]:
                w = []
                for d in op.deps:
                    if d.chan:
                        key, val = ("c", d.chan), d.gend
                    else:
                        key, val = ("e", d.eng), d.count
                    if waited.get(key, 0) < val:
                        waited[key] = val
                        w.append((key, val))
                op.waits = w

    def sem_keys(self):
        keys = [("e", e) for e in self.ENG if any(o.chan is None for o in self.ops[e])]
        keys += [("c", ch) for ch in self.chan_ops]
        return keys

    def emit(self, block, sems, final_waits):
        attr = {"pe": "tensor", "act": "scalar", "dve": "vector", "pool": "gpsimd", "sp": "sync"}
        for e in self.ENG:
            ops = self.ops[e]
            if not ops:
                continue

            def body(eng, ops=ops, e=e):
                for op in ops:
                    for key, val in op.waits:
                        eng.wait_ge(sems[key], val)
                    ins = op.build(eng)
                    if op.chan:
                        ins.then_inc(sems[("c", op.chan)], 16)
                    elif op.signal:
                        ins.then_inc(sems[("e", e)], 1)
                if e == "sp":
                    for key, val in final_waits:
                        eng.wait_ge(sems[key], val)
            getattr(block, attr[e])(body)


class Rot:
    def __init__(self, items):
        self.items = list(items)
        self.i = 0

    def next(self):
        v = self.items[self.i % len(self.items)]
        self.i += 1
        return v


class Builder:
    def __init__(self, cfg):
        self.cfg = cfg
        self.P = Prog()
        self.nc = bass.Bass("TRN2", target_bir_lowering=False, dynamic_dma_scratch_size=cfg.dma_scratch)
        self.out_final = []

    def mm(self, out, lhsT, rhs, start, stop):
        self.P.add("pe", lambda e: e.matmul(out, lhsT, rhs, start=start, stop=stop),
                   reads=[lhsT, rhs], writes=[out])

    def act(self, out, in_, func, bias=None, scale=None, eng="act"):
        reads = [in_]
        kw = {}
        if bias is not None:
            kw["bias"] = bias
            if not isinstance(bias, (int, float)):
                reads.append(bias)
        if scale is not None:
            kw["scale"] = scale
            if not isinstance(scale, (int, float)):
                reads.append(scale)
        self.P.add(eng, lambda e: e.activation(out, in_, func, **kw), reads=reads, writes=[out])

    def tt(self, out, in0, in1, op, eng="dve"):
        self.P.add(eng, lambda e: e.tensor_tensor(out, in0, in1, op), reads=[in0, in1], writes=[out])

    def ts(self, out, in0, s1, op0, s2=None, op1=None, eng="dve"):
        reads = [in0] + [s for s in (s1, s2) if s is not None and not isinstance(s, (int, float))]
        if op1 is None:
            self.P.add(eng, lambda e: e.tensor_scalar(out, in0, s1, None, op0), reads=reads, writes=[out])
        else:
            self.P.add(eng, lambda e: e.tensor_scalar(out, in0, s1, s2, op0, op1), reads=reads, writes=[out])

    def stt(self, out, in0, scalar, in1, op0, op1, eng="dve"):
        reads = [in0, in1] + ([] if isinstance(scalar, (int, float)) else [scalar])
        self.P.add(eng, lambda e: e.scalar_tensor_tensor(out, in0, scalar, in1, op0, op1),
                   reads=reads, writes=[out])

    def copy(self, out, in_, eng="dve"):
        if eng == "act":
            self.act(out, in_, AF.Copy)
        else:
            self.P.add(eng, lambda e: e.tensor_copy(out, in_), reads=[in_], writes=[out])

    def recip(self, out, in_):
        self.P.add("dve", lambda e: e.reciprocal(out, in_), reads=[in_], writes=[out])

    def memset(self, ap, val, eng="dve"):
        self.P.add(eng, lambda e: e.memset(ap, val), writes=[ap])

    def dma(self, q, chan, group, out, in_, **kw):
        return self.P.add(q, lambda e: e.dma_start(out=out, in_=in_, **kw), reads=[in_], writes=[out],
                          chan=chan, group=group)

    def view(self, off, dtype, dims):
        esz = 2 if dtype == BF16 else 4
        n = 1
        for d in dims:
            n *= d
        nb = n * esz
        assert off % 4 == 0 and nb % 4 == 0, (off, nb)
        assert off + nb <= self.cfg.arena_bytes, ("arena overflow", off, nb, self.cfg.arena_bytes)
        ap = self.arena[:, off // 4:(off + nb) // 4]
        if dtype != F32:
            ap = ap.bitcast(dtype)
        if len(dims) == 2:
            ap = ap.rearrange("p (a b) -> p a b", a=dims[0])
        elif len(dims) == 3:
            ap = ap.rearrange("p (a b c) -> p a b c", a=dims[0], b=dims[1])
        return ap

    def slab_plan(self):
        cfg = self.cfg
        D, DC, DG, GC, FF, SL = cfg.D, cfg.DC, cfg.DG, cfg.GC, cfg.FF, cfg.SL
        W = self.wdram
        plan = []

        def rows(ap):
            return ap.rearrange("(kc p) n -> p kc n", p=128)
        for _s in range(cfg.NSEQ):
            for kind, li in zip(cfg.kinds, cfg.layer_ids):
                if kind == "pool":
                    win, wg, wo = W[f"l{li}_pool_w_in"], W[f"l{li}_pool_w_group"], W[f"l{li}_pool_w_out"]
                    for g in range(4):
                        plan.append([(0, (DC, DG), rows(win[:, g * DG:(g + 1) * DG])),
                                     (DC * DG * 2, (GC, DG), rows(wg[g]))])
                    plan.append([(0, (DC, D), rows(wo))])
                elif kind == "conv":
                    win, wo = W[f"l{li}_conv_w_in"], W[f"l{li}_conv_w_out"]
                    for c in range(DC):
                        plan.append([(0, (DC, 128), rows(win[:, c * 128:(c + 1) * 128])),
                                     (DC * 128 * 2, (DC, 128), rows(win[:, D + c * 128:D + (c + 1) * 128]))])
                    plan.append([(0, (DC, D), rows(wo))])
                else:
                    win, wo = W[f"l{li}_fourier_w_in"], W[f"l{li}_fourier_w_out"]
                    plan.append([(0, (DC, D), rows(win))])
                    plan.append([(0, (DC, D), rows(wo))])
                up, dn = W[f"l{li}_mlp_up"], W[f"l{li}_mlp_down"]
                for s in range(FF // SL):
                    plan.append([(0, (DC, SL), rows(up[:, s * SL:(s + 1) * SL])),
                                 (DC * SL * 2, (SL // 128, D), rows(dn[s * SL:(s + 1) * SL, :]))])
        return plan

    def w_issue(self):
        while self.w_issued < len(self.plan) and self.w_issued < self.w_done + 2:
            k = self.w_issued
            slot = k % 2
            for (off, dims, src) in self.plan[k]:
                dst = self.view(self.W_OFF + slot * self.W_SLOT + off, BF16, dims)
                self.dma("pool", f"w{slot}", k, dst, src)
            self.w_issued += 1

    def w_next(self):
        k = self.w_cur
        self.w_cur += 1
        self.w_issue()
        assert k < self.w_issued
        slot = k % 2
        return [self.view(self.W_OFF + slot * self.W_SLOT + off, BF16, dims) for (off, dims, _src) in self.plan[k]]

    def w_release(self):
        self.w_done += 1
        self.w_issue()

    def vcol(self, name, c, n=1):
        o = self.vlay[name] + c
        return self.VEC[:, o:o + n]

    class NormState:
        def __init__(self, gname, final=False, seq=0):
            self.gname, self.final, self.seq = gname, final, seq
            self.pending = None
            self.deferred = []
            self.done = set()
            self.after = None

    def emit_pre(self, n, j):
        sl = slice(j * TT, (j + 1) * TT)
        for c in range(self.cfg.DC):
            self.act(self.H[:, c, sl], self.X[:, c, sl], AF.Square)
        n.pending = j

    def flush_post(self, n):
        cfg = self.cfg
        j = n.pending
        if j is None:
            return
        n.pending = None
        sl = slice(j * TT, (j + 1) * TT)
        bank = self.rotS.next()
        for c in range(cfg.DC):
            self.mm(bank, self.ONES, self.H[:, c, sl], c == 0, c == cfg.DC - 1)
        r = self.RS.next()
        self.act(r, bank, AF.Sqrt, bias=self.vcol("eps", 0), scale=1.0 / cfg.D)
        self.recip(r, r)
        if not n.final:
            for c in range(cfg.DC):
                self.stt(self.H[:, c, sl], self.X[:, c, sl], self.vcol(n.gname, c), r, ALU.mult, ALU.mult)
        else:
            o = self.O[j % 2]
            for c in range(cfg.DC):
                self.stt(o[:, c, :], self.X[:, c, sl], self.vcol(n.gname, c), r, ALU.mult, ALU.mult)
            self.out_final.append(self.dma("sp", f"out{j % 2}", ("o", n.seq, j),
                                           self.outT[n.seq][:, sl].rearrange("(c p) t -> p c t", p=128), o))
            if n.after is not None:
                self.load_x_tile(n.seq + 1, j)
                self._ready(n.after, j)
        n.done.add(j)

    def load_x_tile(self, s, j):
        sl = slice(j * TT, (j + 1) * TT)
        self.dma("sp", f"xin{j}", ("x", s, j), self.X[:, :, sl],
                 self.xT[s][:, sl].rearrange("(c p) t -> p c t", p=128))

    def _ready(self, n, j):
        if n is None:
            return
        if self.h_free:
            self.flush_post(n)
            self.emit_pre(n, j)
        else:
            n.deferred.append(j)

    def x_ready(self, j):
        self._ready(self.nxt, j)

    def drain(self, n):
        if n is None:
            return
        assert self.h_free
        while n.deferred:
            self.flush_post(n)
            self.emit_pre(n, n.deferred.pop(0))

    def need_H(self, j):
        n = self.cur
        while j not in n.done:
            assert self.h_free
            if n.pending is not None:
                self.flush_post(n)
            elif n.deferred:
                self.emit_pre(n, n.deferred.pop(0))
            else:
                raise AssertionError(("H tile never produced", j))

    def rms_stats_sq(self, j):
        raise NotImplementedError

    def proj_residual(self, wout, src_of_kc, j, nk):
        cfg = self.cfg
        sl = slice(j * TT, (j + 1) * TT)
        for oc in range(cfg.DC):
            bank = self.rotA.next()
            for kc in range(nk):
                self.mm(bank, wout[:, kc, oc * 128:(oc + 1) * 128], src_of_kc(kc), kc == 0, kc == nk - 1)
            self.tt(self.X[:, oc, sl], self.X[:, oc, sl], bank, ALU.add)
        self.x_ready(j)

    def mlp(self, li):
        cfg = self.cfg
        DC, SL, NT = cfg.DC, cfg.SL, cfg.NT
        FCS = SL // 128
        nslab = cfg.FF // SL
        assert nslab >= 2
        flex = self.FLEX
        A = [self.view(flex + i * FCS * TT * 2, BF16, (FCS, TT)) for i in range(2)]
        toff = flex + 2 * FCS * TT * 2
        Tr = Rot([self.view(toff + i * TT * 4, F32, (TT,)) for i in range(3)])
        steps = [(s, j) for s in range(cfg.FF // SL) for j in range(NT)]
        slabs = {}

        def up(i):
            s, j = steps[i]
            if s not in slabs:
                slabs[s] = self.w_next()
            wu = slabs[s][0]
            sl = slice(j * TT, (j + 1) * TT)
            a = A[i % 2]
            if s == 0:
                self.need_H(j)
            for fc in range(FCS):
                bank = self.rotU.next()
                for kc in range(DC):
                    self.mm(bank, wu[:, kc, fc * 128:(fc + 1) * 128], self.H[:, kc, sl], kc == 0, kc == DC - 1)
                t = Tr.next()
                self.act(t, bank, AF.Relu)
                self.tt(a[:, fc, :], t, t, ALU.mult)

        def down(i):
            s, j = steps[i]
            wd = slabs[s][1]
            sl = slice(j * TT, (j + 1) * TT)
            a = A[i % 2]
            for dc in range(DC):
                bank = self.rotD.next()
                for fc in range(FCS):
                    self.mm(bank, wd[:, fc, dc * 128:(dc + 1) * 128], a[:, fc, :], fc == 0, fc == FCS - 1)
                self.tt(self.X[:, dc, sl], self.X[:, dc, sl], bank, ALU.add)
            if s == nslab - 1:
                self.x_ready(j)
            if j == NT - 1:
                self.w_release()
        up(0)
        for i in range(len(steps)):
            if i + 1 < len(steps):
                up(i + 1)
            down(i)

    def pool_layer(self, li):
        cfg = self.cfg
        DC, GC, DG, S, NT = cfg.DC, cfg.GC, cfg.DG, cfg.S, cfg.NT
        L = S + 16
        off = self.FLEX
        Ub = []
        for _i in range(2):
            Ub.append(self.view(off, F32, (L,))); off += L * 4
        T = self.view(off, F32, (L,)); off += L * 4
        E1 = self.view(off, F32, (8,)); off += 32
        Pb = []
        for _i in range(2):
            Pb.append(self.view(off, BF16, (GC, S))); off += GC * S * 2
        Y = self.view(off, BF16, (DC, S)); off += DC * S * 2
        assert off <= self.FLEX_END, (off, self.FLEX_END)
        for u in Ub:
            self.memset(u[:, 0:8], 0.0)
            self.memset(u[:, 8 + S:L], 0.0)
        eo = self.vlay["edges"]
        nch = 4 * GC
        slabs = {}

        def stageA(ci):
            g, oc = divmod(ci, GC)
            if g not in slabs:
                slabs[g] = self.w_next()
            win = slabs[g][0]
            u = Ub[ci % 2]
            for j in range(NT):
                if ci == 0:
                    self.need_H(j)
                bank = self.rotA.next()
                for kc in range(DC):
                    self.mm(bank, win[:, kc, oc * 128:(oc + 1) * 128], self.H[:, kc, j * TT:(j + 1) * TT],
                            kc == 0, kc == DC - 1)
                self.act(u[:, 8 + j * TT:8 + (j + 1) * TT], bank, AF.Copy)

        def stageB(ci):
            g, oc = divmod(ci, GC)
            w = POOL_WINDOWS[g]
            half = w // 2
            Uc = Ub[ci % 2]
            Pg = Pb[g % 2]
            self.tt(T[:, 0:L - 1], Uc[:, 0:L - 1], Uc[:, 1:L], ALU.add)
            cur = 2
            while cur < w:
                n = L - 2 * cur + 1
                self.tt(T[:, 0:n], T[:, 0:n], T[:, cur:cur + n], ALU.add)
                cur *= 2
            self.stt(Pg[:, oc, :], T[:, 8 - half:8 - half + S], 1.0 / w, Uc[:, 8:8 + S], ALU.mult, ALU.subtract)
            el = self.VEC[:, eo + (g * 2) * 8: eo + (g * 2) * 8 + half]
            self.tt(E1[:, 0:half], T[:, 8 - half:8], el, ALU.mult)
            self.tt(Pg[:, oc, 0:half], E1[:, 0:half], Uc[:, 8:8 + half], ALU.subtract)
            nr = half - 1
            if nr > 0:
                i0 = S - half + 1
                er = self.VEC[:, eo + (g * 2 + 1) * 8: eo + (g * 2 + 1) * 8 + nr]
                self.tt(E1[:, 0:nr], T[:, i0 + 8 - half:i0 + 8 - half + nr], er, ALU.mult)
                self.tt(Pg[:, oc, i0:S], E1[:, 0:nr], Uc[:, 8 + i0:8 + S], ALU.subtract)

        def stageC(g):
            wg = slabs[g][1]
            Pg = Pb[g % 2]
            for oc2 in range(GC):
                for j in range(NT):
                    bank = self.rotA.next()
                    for kc in range(GC):
                        self.mm(bank, wg[:, kc, oc2 * 128:(oc2 + 1) * 128], Pg[:, kc, j * TT:(j + 1) * TT],
                                kc == 0, kc == GC - 1)
                    self.act(Y[:, g * GC + oc2, j * TT:(j + 1) * TT], bank, AF.Copy,
                             scale=self.vcol(f"l{li}_pool_scale", g * GC + oc2))
            self.w_release()

        stageA(0)
        if nch > 1:
            stageA(1)
        for ci in range(nch):
            stageB(ci)
            if ci % GC == GC - 1:
                stageC(ci // GC)
            if ci + 2 < nch:
                stageA(ci + 2)
        (wout,) = self.w_next()
        for j in range(NT):
            self.proj_residual(wout, lambda kc, j=j: Y[:, kc, j * TT:(j + 1) * TT], j, DC)
        self.w_release()

    def conv_layer(self, li):
        cfg = self.cfg
        DC, S, NT, D = cfg.DC, cfg.S, cfg.NT, cfg.D
        VL = S + 2 * CONV_PAD
        vbytes = VL * 2
        hbytes = DC * S * 2
        base = self.H_OFF
        cend = base + DC * S * 4

        def Vc(c):
            o = base + hbytes + c * vbytes if c < DC - 1 else cend
            return self.view(o, BF16, (VL,))
        assert base + hbytes + (DC - 1) * vbytes <= cend + 0 or True
        C = self.view(base, F32, (DC, S))
        off = cend + vbytes
        off = (off + 3) // 4 * 4
        DGb = []
        for _i in range(2):
            DGb.append(self.view(off, BF16, (CONV_W, 128))); off += CONV_W * 128 * 2
        lnoff = off - 2 * CONV_W * 128 * 2
        SG = Rot([self.view(off + i * TT * 4, F32, (TT,)) for i in range(2)]); off += 2 * TT * 4
        assert off <= self.FLEX_END
        LT = [self.view(lnoff + i * TT * 4, F32, (TT,)) for i in range(7)]; lnoff += 7 * TT * 4
        SLd = []
        for _i in range(2):
            SLd.append(self.view(lnoff, BF16, (DC, TT))); lnoff += DC * TT * 2
        assert lnoff <= self.FLEX_END, (lnoff, self.FLEX_END)
        for c in range(DC):
            v = Vc(c)
            self.memset(v[:, 0:CONV_PAD], 0.0)
            self.memset(v[:, CONV_PAD + S:VL], 0.0)
        for c in range(DC):
            wv, wgt = self.w_next()
            v = Vc(c)
            for j in range(NT):
                sl = slice(j * TT, (j + 1) * TT)
                if c == 0:
                    self.need_H(j)
                bv = self.rotA.next()
                bg = self.rotA.next()
                for kc in range(DC):
                    self.mm(bv, wv[:, kc, :], self.H[:, kc, sl], kc == 0, kc == DC - 1)
                for kc in range(DC):
                    self.mm(bg, wgt[:, kc, :], self.H[:, kc, sl], kc == 0, kc == DC - 1)
                sg = SG.next()
                self.act(sg, bg, AF.Sigmoid)
                self.tt(v[:, CONV_PAD + j * TT:CONV_PAD + (j + 1) * TT], bv, sg, ALU.mult)
            self.w_release()
        self.h_free = False
        dwo = self.vlay[f"l{li}_conv_dw"]
        for c in range(DC):
            dwc = self.VEC[:, dwo + c * CONV_W: dwo + (c + 1) * CONV_W]
            DGm = DGb[c % 2]
            self.tt(DGm, self.IDENT.unsqueeze(1).broadcast_to([128, CONV_W, 128]),
                    dwc.unsqueeze(2).broadcast_to([128, CONV_W, 128]), ALU.mult)
            v = Vc(c)
            for j in range(NT):
                bank = self.rotA.next()
                for k in range(CONV_W):
                    self.mm(bank, DGm[:, k, :], v[:, j * TT + k:j * TT + k + TT], k == 0, k == CONV_W - 1)
                self.act(C[:, c, j * TT:(j + 1) * TT], bank, AF.Identity, bias=self.vcol(f"l{li}_conv_dw_bias", c))
        (wout,) = self.w_next()
        MEANb, RSTDb, MSQ = LT[0:2], LT[2:4], LT[4]
        T1 = Rot(LT[5:7])

        def stats(j):
            sl = slice(j * TT, (j + 1) * TT)
            b1 = self.rotS.next()
            b2 = self.rotS.next()
            for c in range(DC):
                cb = self.SQ.next()
                self.copy(cb, C[:, c, sl], eng="act")
                cs = self.SQ.next()
                self.act(cs, C[:, c, sl], AF.Square)
                self.mm(b1, self.ONES, cb, c == 0, c == DC - 1)
                self.mm(b2, self.ONES, cs, c == 0, c == DC - 1)
            MEAN, RSTD = MEANb[j % 2], RSTDb[j % 2]
            self.ts(MEAN, b1, 1.0 / D, ALU.mult)
            self.tt(MSQ, MEAN, MEAN, ALU.mult)
            self.stt(RSTD, b2, 1.0 / D, MSQ, ALU.mult, ALU.subtract)
            self.act(RSTD, RSTD, AF.Sqrt, bias=self.vcol("eps", 1))
            self.recip(RSTD, RSTD)

        def normalize(j):
            sl = slice(j * TT, (j + 1) * TT)
            MEAN, RSTD = MEANb[j % 2], RSTDb[j % 2]
            SLb = SLd[j % 2]
            for c in range(DC):
                t1 = T1.next()
                self.tt(t1, C[:, c, sl], MEAN, ALU.subtract)
                self.tt(t1, t1, RSTD, ALU.mult)
                self.act(SLb[:, c, :], t1, AF.Silu, bias=self.vcol(f"l{li}_conv_ln_b", c),
                         scale=self.vcol(f"l{li}_conv_ln_g", c))

        stats(0)
        for j in range(NT):
            if j + 1 < NT:
                stats(j + 1)
            normalize(j)
            self.proj_residual(wout, lambda kc, j=j: SLd[j % 2][:, kc, :], j, DC)
        self.w_release()
        self.h_free = True
        self.drain(self.nxt)

    def four_layer(self, li):
        cfg = self.cfg
        DC, GC, DG, S, NT, TC = cfg.DC, cfg.GC, cfg.DG, cfg.S, cfg.NT, cfg.TC
        H2 = TC // 2
        abrow = 4 * 2 * DG
        ab1 = self.H_OFF
        off = self.FLEX
        ab2 = off; off += H2 * abrow * 2
        ut = off; off += DC * S * 2
        fj = off; off += DC * TT * 2
        UT = self.view(ut, BF16, (DC, S))
        Fj = self.view(fj, BF16, (DC, TT))

        def AB(tc):
            o = (ab1 if tc < H2 else ab2) + (tc % H2) * abrow * 2
            return self.view(o, BF16, (4, 2 * DG))
        tabs = [self.view(ut + h * (2 * H2 * TT * 2), BF16, (2, H2, TT)) for h in range(2)]
        assert 2 * (2 * H2 * TT * 2) <= DC * S * 2
        (win,) = self.w_next()
        for oc in range(DC):
            for j in range(NT):
                sl = slice(j * TT, (j + 1) * TT)
                if oc == 0:
                    self.need_H(j)
                bank = self.rotA.next()
                for kc in range(DC):
                    self.mm(bank, win[:, kc, oc * 128:(oc + 1) * 128], self.H[:, kc, sl], kc == 0, kc == DC - 1)
                self.copy(UT[:, oc, sl], bank, eng="act")
        self.w_release()
        self.h_free = False
        n = 0
        for tc in range(TC):
            ab = AB(tc)
            for g in range(4):
                bank = self.rotA.next()
                for kc in range(GC):
                    self.mm(bank[:, 0:2 * DG], UT[:, g * GC + kc, tc * 128:(tc + 1) * 128], self.CSC[:, kc, :],
                            kc == 0, kc == GC - 1)
                self.copy(ab[:, g, :], bank[:, 0:2 * DG], eng=("act" if n % 2 == 0 else "dve"))
                n += 1
        (wout,) = self.w_next()
        scale = 1.0 / math.sqrt(S)
        for j in range(NT):
            for h in range(2):
                self.dma("sp", f"t{h}", (li, self.seq_i, j), tabs[h], self.dfts[j, h])
            for ch in range(DC):
                g, hh = ch // GC, ch % GC
                bank = self.rotA.next()
                k = 0
                for tc in range(TC):
                    ab = AB(tc)
                    tab = tabs[tc // H2]
                    for cs_i in range(2):
                        col = cs_i * DG + hh * 128
                        self.mm(bank, ab[:, g, col:col + 128], tab[:, cs_i, tc % H2, :], k == 0, k == 2 * TC - 1)
                        k += 1
                self.act(Fj[:, ch, :], bank, AF.Copy, scale=scale)
            self.proj_residual(wout, lambda kc: Fj[:, kc, :], j, DC)
        self.w_release()
        self.h_free = True
        self.drain(self.nxt)

    def finish(self, s):
        cfg = self.cfg
        if not cfg.final:
            self.out_final.append(self.dma("sp", "out0", ("o", s), self.outT[s].rearrange("(c p) t -> p c t", p=128),
                                           self.X))
            return
        for j in range(cfg.NT):
            self.need_H(j)

    def build(self):
        cfg = self.cfg
        nc = self.nc
        DC, S, D = cfg.DC, cfg.S, cfg.D
        lay, nv = vec_layout(cfg)
        self.vlay = lay
        xT = nc.dram_tensor("xT", [cfg.NSEQ, D, S], F32, kind="ExternalInput").ap()
        vecs = nc.dram_tensor("vecs", [128, nv], F32, kind="ExternalInput").ap()
        cmat = nc.dram_tensor("cmat", [128, 256], BF16, kind="ExternalInput").ap()
        has_four = "four" in cfg.kinds
        if has_four:
            dftc = nc.dram_tensor("dftc", [128, cfg.GC, 2 * cfg.DG], BF16, kind="ExternalInput").ap()
            self.dfts = nc.dram_tensor("dfts", [cfg.NT, 2, 128, 2, cfg.TC // 2, TT], BF16, kind="ExternalInput").ap()
        self.wdram = {}
        for name, shape in weight_names(cfg):
            self.wdram[name] = nc.dram_tensor(name, list(shape), F32, kind="ExternalInput").ap()
        self.outT = nc.dram_tensor("outT", [cfg.NSEQ, D, S], F32, kind="ExternalOutput").ap()

        off = 0
        self.X_OFF = off; off += DC * S * 4
        self.H_OFF = off; off += DC * S * 2
        nvb = (nv * 4 + 3) // 4 * 4
        small = nvb + 256 * 2 + (cfg.GC * 2 * cfg.DG * 2 if has_four else 0) + 4 * TT * 2 + 2 * TT * 4
        self.W_SLOT = max(DC * cfg.SL * 2 + (cfg.SL // 128) * D * 2, DC * D * 2)
        self.W_SLOT = (self.W_SLOT + 255) // 256 * 256
        self.SM_OFF = (cfg.arena_bytes - small) // 256 * 256
        self.W_OFF = self.SM_OFF - 2 * self.W_SLOT
        self.FLEX = off
        self.FLEX_END = self.W_OFF
        assert self.FLEX_END > self.FLEX
        real_view = self.view

        with nc.allow_low_precision("bf16 matmul operands, fp32 accumulation"), \
                nc.sbuf_tensor("arena", [128, cfg.arena_bytes // 4], F32) as arena, \
                nc.psum_tensor("ps", [128, 8, TT], F32) as ps:
            self.arena = arena

            def flexview(off_, dtype, dims):
                return real_view(off_, dtype, dims)
            so = self.SM_OFF
            self.VEC = self.view(so, F32, (nv,)); so += nvb
            CM = self.view(so, BF16, (256,)); so += 512
            self.ONES = CM[:, 0:128]
            self.IDENT = CM[:, 128:256]
            if has_four:
                self.CSC = self.view(so, BF16, (cfg.GC, 2 * cfg.DG)); so += cfg.GC * 2 * cfg.DG * 2
            self.SQ = Rot([self.view(so + i * TT * 2, BF16, (TT,)) for i in range(4)]); so += 4 * TT * 2
            self.RS = Rot([self.view(so + i * TT * 4, F32, (TT,)) for i in range(2)]); so += 2 * TT * 4
            assert so <= cfg.arena_bytes
            self.X = self.view(self.X_OFF, F32, (DC, S))
            self.H = self.view(self.H_OFF, BF16, (DC, S))
            banks = [ps[:, i, :] for i in range(8)]
            self.rotS = Rot(banks[0:2])
            self.rotA = Rot(banks[2:8])
            self.rotU = Rot(banks[2:5])
            self.rotD = Rot(banks[5:8])

            self.dma("sp", "c0", "c", self.VEC, vecs)
            self.dma("sp", "c0", "c", CM, cmat)
            if has_four:
                self.dma("sp", "c0", "c", self.CSC, dftc)

            self.plan = self.slab_plan()
            self.w_issued = self.w_done = self.w_cur = 0
            self.xT = xT
            self.h_free = True
            ooff = self.FLEX + 2 * (cfg.SL // 128) * TT * 2 + 3 * TT * 4
            self.O = [self.view(ooff + i * DC * TT * 4, F32, (DC, TT)) for i in range(2)]
            assert ooff + 2 * DC * TT * 4 <= self.FLEX_END
            NS = Builder.NormState
            norms = []
            for s in range(cfg.NSEQ):
                row = []
                for kind, li in zip(cfg.kinds, cfg.layer_ids):
                    row.append(NS(f"l{li}_norm_mix", seq=s))
                    row.append(NS(f"l{li}_norm_mlp", seq=s))
                if cfg.final:
                    row.append(NS("final_norm", final=True, seq=s))
                else:
                    row.append(None)
                norms.append(row)
            if cfg.final:
                for s in range(cfg.NSEQ - 1):
                    norms[s][-1].after = norms[s + 1][0]
            self.cur = self.nxt = None
            for s in range(cfg.NSEQ):
                self.seq_i = s
                row = norms[s]
                if s == 0 or not cfg.final:
                    for j in range(cfg.NT):
                        self.load_x_tile(s, j)
                        self._ready(row[0], j)
                k = 0
                for kind, li in zip(cfg.kinds, cfg.layer_ids):
                    self.cur, self.nxt = row[k], row[k + 1]
                    if kind == "pool":
                        self.pool_layer(li)
                    elif kind == "conv":
                        self.conv_layer(li)
                    else:
                        self.four_layer(li)
                    k += 1
                    self.cur, self.nxt = row[k], row[k + 1]
                    self.mlp(li)
                    k += 1
                self.cur, self.nxt = row[k], None
                self.finish(s)
            assert self.w_cur == len(self.plan)

            P = self.P
            P.finalize()
            finals = {}
            for op in self.out_final:
                key = ("c", op.chan)
                finals[key] = max(finals.get(key, 0), op.gend)
            keys = P.sem_keys()
            import contextlib
            with contextlib.ExitStack() as es:
                sems = {}
                for k in keys:
                    sems[k] = es.enter_context(nc.semaphore(f"s_{k[0]}_{k[1]}"))
                block = es.enter_context(nc.Block())
                P.emit(block, sems, list(finals.items()))
        return nc


_KINDS = ("pool", "conv", "four", "pool")


def make_in_maps(cfg, inputs, x_shards):
    vec = pack_vecs(cfg, inputs)
    cmat, csc, tab = const_tables(cfg)
    base = {"vecs": vec, "cmat": cmat}
    if "four" in cfg.kinds:
        base["dftc"] = csc
        base["dfts"] = tab
    for name, _shape in weight_names(cfg):
        base[name] = np.ascontiguousarray(np.asarray(inputs[name], np.float32))
    maps = []
    for xs in x_shards:
        m = dict(base)
        m["xT"] = xs
        maps.append(m)
    return maps


def kernel(**inputs):
    x = np.asarray(inputs["x"], np.float32)
    B, S, D = x.shape
    ncores = 8
    nseq = B // ncores
    cfg = Cfg(S=S, D=D, FF=4 * D, NSEQ=nseq, kinds=_KINDS, final=True)
    xT = np.ascontiguousarray(x.transpose(0, 2, 1))
    shards = [xT[i * nseq:(i + 1) * nseq] for i in range(ncores)]
    nc = Builder(cfg).build()
    in_maps = make_in_maps(cfg, inputs, shards)
    res = run_bass_kernel_spmd(nc, in_maps, core_ids=list(range(ncores)))
    outT = np.concatenate([np.asarray(r["outT"]) for r in res.results], axis=0)
    return np.ascontiguousarray(outT.transpose(0, 2, 1)).astype(np.float32)
```

```python
import math
import numpy as np
import ml_dtypes
import concourse.bass as bass
import concourse.mybir as mybir
from concourse.bass_utils import run_bass_kernel_spmd

F32 = mybir.dt.float32
BF16 = mybir.dt.bfloat16
AF = mybir.ActivationFunctionType
ALU = mybir.AluOpType

POOL_WINDOWS = (2, 4, 8, 16)
CONV_W = 31
CONV_PAD = 15
NORM_EPS = 1e-6
LN_EPS = 1e-5
TT = 512
GRAN = 256


class Cfg:
    def __init__(self, S=2048, D=1024, FF=4096, NSEQ=2, kinds=("pool", "conv", "four", "pool"),
                 layer_ids=None, final=True, SL=512, arena_bytes=220160, dma_scratch=8192):
        self.S, self.D, self.FF, self.NSEQ = S, D, FF, NSEQ
        self.kinds = tuple(kinds)
        self.layer_ids = tuple(layer_ids) if layer_ids is not None else tuple(range(len(kinds)))
        self.final = final
        self.DC = D // 128
        self.GC = self.DC // 4
        self.DG = D // 4
        self.NT = S // TT
        self.TC = S // 128
        self.SL = SL
        self.arena_bytes = arena_bytes
        self.dma_scratch = dma_scratch
        assert D % 512 == 0 and S % TT == 0 and FF % SL == 0 and self.TC % 2 == 0


def vec_layout(cfg):
    lay, off = {}, 0

    def put(name, n):
        nonlocal off
        lay[name] = off
        off += n
    DC = cfg.DC
    for kind, li in zip(cfg.kinds, cfg.layer_ids):
        put(f"l{li}_norm_mix", DC)
        put(f"l{li}_norm_mlp", DC)
        if kind == "pool":
            put(f"l{li}_pool_scale", DC)
        if kind == "conv":
            put(f"l{li}_conv_dw_bias", DC)
            put(f"l{li}_conv_ln_g", DC)
            put(f"l{li}_conv_ln_b", DC)
            put(f"l{li}_conv_dw", DC * CONV_W)
    put("final_norm", DC)
    put("edges", 4 * 2 * 8)
    put("eps", 2)
    return lay, off


def weight_names(cfg):
    names = []
    for kind, li in zip(cfg.kinds, cfg.layer_ids):
        if kind == "pool":
            names += [(f"l{li}_pool_w_in", (cfg.D, cfg.D)), (f"l{li}_pool_w_group", (4, cfg.DG, cfg.DG)),
                      (f"l{li}_pool_w_out", (cfg.D, cfg.D))]
        elif kind == "conv":
            names += [(f"l{li}_conv_w_in", (cfg.D, 2 * cfg.D)), (f"l{li}_conv_w_out", (cfg.D, cfg.D))]
        else:
            names += [(f"l{li}_fourier_w_in", (cfg.D, cfg.D)), (f"l{li}_fourier_w_out", (cfg.D, cfg.D))]
        names += [(f"l{li}_mlp_up", (cfg.D, cfg.FF)), (f"l{li}_mlp_down", (cfg.FF, cfg.D))]
    return names


def pack_vecs(cfg, inputs):
    lay, nv = vec_layout(cfg)
    DC = cfg.DC
    V = np.zeros((128, nv), np.float32)

    def colmajor(v):
        return np.asarray(v, np.float32).reshape(DC, 128).T
    for name, off in lay.items():
        if name == "edges":
            for wi, w in enumerate(POOL_WINDOWS):
                h = w // 2
                for i in range(h):
                    V[:, off + (wi * 2 + 0) * 8 + i] = 1.0 / (i + h)
                for m in range(h - 1):
                    i = cfg.S - h + 1 + m
                    V[:, off + (wi * 2 + 1) * 8 + m] = 1.0 / (cfg.S - i + h)
        elif name == "eps":
            V[:, off] = NORM_EPS
            V[:, off + 1] = LN_EPS
        elif name.endswith("conv_dw"):
            dw = np.asarray(inputs[name], np.float32)
            V[:, off:off + DC * CONV_W] = dw.T.reshape(DC, 128, CONV_W).transpose(1, 0, 2).reshape(128, DC * CONV_W)
        else:
            V[:, off:off + DC] = colmajor(inputs[name])
    return V


def const_tables(cfg):
    bf = ml_dtypes.bfloat16
    cmat = np.zeros((128, 256), np.float32)
    cmat[:, 0:128] = 1.0
    cmat[:, 128:256] = np.eye(128, dtype=np.float32)
    DG, GC, S, TC, NT = cfg.DG, cfg.GC, cfg.S, cfg.TC, cfg.NT
    a = np.arange(DG, dtype=np.float64)
    ang = 2.0 * np.pi * np.outer(a, a) / DG
    csc = np.concatenate([np.cos(ang), np.sin(ang)], axis=1) / math.sqrt(DG)
    csc = csc.reshape(GC, 128, 2 * DG).transpose(1, 0, 2)
    n = np.arange(S, dtype=np.int64)
    prod = np.outer(n, n) % S
    ang2 = 2.0 * np.pi * prod.astype(np.float64) / S
    cs = np.cos(ang2)
    ns = -np.sin(ang2)
    H2 = TC // 2
    tab = np.stack([cs, ns], axis=0)
    tab = tab.reshape(2, 2, H2, 128, NT, TT)
    tab = tab.transpose(4, 1, 3, 0, 2, 5)
    return (cmat.astype(bf), np.ascontiguousarray(csc).astype(bf),
            np.ascontiguousarray(tab).astype(bf))


class Op:
    __slots__ = ("eng", "build", "deps", "signal", "count", "chan", "group", "seq", "waits", "gend")


class Prog:
    ENG = ("pe", "act", "dve", "pool", "sp")

    def __init__(self):
        self.ops = {e: [] for e in self.ENG}
        self.lastw = {}
        self.readers = {}
        self.chan_ops = {}
        self.gcache = {}
        self.seq = 0

    def gran(self, ap):
        if str(ap.space) == "DRAM":
            return ()
        dims = ap.ap
        key = (ap.tensor.name, ap.offset, dims, str(ap.dtype))
        g = self.gcache.get(key)
        if g is not None:
            return g
        esz = 2 if ap.dtype == BF16 else 4
        pstep = dims[0][0]
        off = ap.offset % pstep if pstep > 0 else ap.offset
        free = [d for d in dims[1:] if d[1] > 1 and d[0] != 0]
        if not free:
            free = [(1, 1)]
        starts = np.array([off], dtype=np.int64)
        for step, cnt in free[:-1]:
            starts = (starts[:, None] + np.arange(cnt, dtype=np.int64)[None, :] * step).ravel()
        step_l, cnt_l = free[-1]
        lo = starts * esz
        hi = (starts + (cnt_l - 1) * abs(step_l) + 1) * esz
        name = ap.tensor.name
        s = set()
        for l, h in zip(lo.tolist(), hi.tolist()):
            for gi in range(l // GRAN, (h - 1) // GRAN + 1):
                s.add((name, gi))
        g = tuple(s)
        self.gcache[key] = g
        return g

    def add(self, eng, build, reads=(), writes=(), chan=None, group=None):
        op = Op()
        op.eng, op.build, op.chan, op.group, op.signal = eng, build, chan, group, False
        op.seq = self.seq
        self.seq += 1
        deps = {}

        def dep(d):
            if d is None:
                return
            if d.chan is None and d.eng == "pe" and eng == "pe" and chan is None:
                return
            if chan is not None and d.chan == chan and d.group == group:
                return
            key = ("c", d.chan) if d.chan else ("e", d.eng)
            cur = deps.get(key)
            if cur is None or d.seq > cur.seq:
                deps[key] = d
        rg = set()
        for ap in reads:
            rg.update(self.gran(ap))
        wg = set()
        for ap in writes:
            wg.update(self.gran(ap))
        for g in rg:
            dep(self.lastw.get(g))
        for g in wg:
            dep(self.lastw.get(g))
            rd = self.readers.get(g)
            if rd:
                for r in rd.values():
                    dep(r)
        mykey = ("c", chan) if chan else ("e", eng)
        for g in rg:
            if g not in wg:
                self.readers.setdefault(g, {})[mykey] = op
        for g in wg:
            self.lastw[g] = op
            self.readers[g] = {}
        op.deps = list(deps.values())
        for d in op.deps:
            d.signal = True
        self.ops[eng].append(op)
        if chan:
            self.chan_ops.setdefault(chan, []).append(op)
        return op

    def finalize(self):
        for e in self.ENG:
            c = 0
            for op in self.ops[e]:
                if op.chan is None:
                    if op.signal:
                        c += 1
                    op.count = c
        for ch, lst in self.chan_ops.items():
            c = 0
            ends = {}
            for op in lst:
                c += 16
                op.count = c
                ends[op.group] = c
            for op in lst:
                op.gend = ends[op.group]
        for e in self.ENG:
            waited = {}
            for op in self.ops[e]:
                w = []
                for d in op.deps:
                    if d.chan:
                        key, val = ("c", d.chan), d.gend
                    else:
                        key, val = ("e", d.eng), d.count
                    if waited.get(key, 0) < val:
                        waited[key] = val
                        w.append((key, val))
                op.waits = w

    def sem_keys(self):
        keys = [("e", e) for e in self.ENG if any(o.chan is None for o in self.ops[e])]
        keys += [("c", ch) for ch in self.chan_ops]
        return keys

    def emit(self, block, sems, final_waits):
        attr = {"pe": "tensor", "act": "scalar", "dve": "vector", "pool": "gpsimd", "sp": "sync"}
        for e in self.ENG:
            ops = self.ops[e]
            if not ops:
                continue

            def body(eng, ops=ops, e=e):
                for op in ops:
                    for key, val in op.waits:
                        eng.wait_ge(sems[key], val)
                    ins = op.build(eng)
                    if op.chan:
                        ins.then_inc(sems[("c", op.chan)], 16)
                    elif op.signal:
                        ins.then_inc(sems[("e", e)], 1)
                if e == "sp":
                    for key, val in final_waits:
                        eng.wait_ge(sems[key], val)
            getattr(block, attr[e])(body)


def al(x, a=GRAN):
    return (x + a - 1) // a * a


class Rot:
    def __init__(self, items):
        self.items = list(items)
        self.i = 0

    def next(self):
        v = self.items[self.i % len(self.items)]
        self.i += 1
        return v


class Builder:
    def __init__(self, cfg):
        self.cfg = cfg
        self.P = Prog()
        self.nc = bass.Bass("TRN2", target_bir_lowering=False, dynamic_dma_scratch_size=cfg.dma_scratch)
        self.out_final = []

    def mm(self, out, lhsT, rhs, start, stop):
        self.P.add("pe", lambda e: e.matmul(out, lhsT, rhs, start=start, stop=stop),
                   reads=[lhsT, rhs], writes=[out])

    def act(self, out, in_, func, bias=None, scale=None, eng="act"):
        reads = [in_]
        kw = {}
        if bias is not None:
            kw["bias"] = bias
            if not isinstance(bias, (int, float)):
                reads.append(bias)
        if scale is not None:
            kw["scale"] = scale
            if not isinstance(scale, (int, float)):
                reads.append(scale)
        self.P.add(eng, lambda e: e.activation(out, in_, func, **kw), reads=reads, writes=[out])

    def tt(self, out, in0, in1, op, eng="dve"):
        self.P.add(eng, lambda e: e.tensor_tensor(out, in0, in1, op), reads=[in0, in1], writes=[out])

    def ts(self, out, in0, s1, op0, s2=None, op1=None, eng="dve"):
        reads = [in0] + [s for s in (s1, s2) if s is not None and not isinstance(s, (int, float))]
        if op1 is None:
            self.P.add(eng, lambda e: e.tensor_scalar(out, in0, s1, None, op0), reads=reads, writes=[out])
        else:
            self.P.add(eng, lambda e: e.tensor_scalar(out, in0, s1, s2, op0, op1), reads=reads, writes=[out])

    def stt(self, out, in0, scalar, in1, op0, op1, eng="dve"):
        reads = [in0, in1] + ([] if isinstance(scalar, (int, float)) else [scalar])
        self.P.add(eng, lambda e: e.scalar_tensor_tensor(out, in0, scalar, in1, op0, op1),
                   reads=reads, writes=[out])

    def copy(self, out, in_, eng="dve"):
        if eng == "act":
            self.act(out, in_, AF.Copy)
        else:
            self.P.add(eng, lambda e: e.tensor_copy(out, in_), reads=[in_], writes=[out])

    def recip(self, out, in_):
        self.P.add("dve", lambda e: e.reciprocal(out, in_), reads=[in_], writes=[out])

    def memset(self, ap, val, eng="dve"):
        self.P.add(eng, lambda e: e.memset(ap, val), writes=[ap])

    def dma(self, q, chan, group, out, in_, **kw):
        return self.P.add(q, lambda e: e.dma_start(out=out, in_=in_, **kw), reads=[in_], writes=[out],
                          chan=chan, group=group)

    def view(self, off, dtype, dims):
        esz = 2 if dtype == BF16 else 4
        n = 1
        for d in dims:
            n *= d
        nb = n * esz
        assert off % 4 == 0 and nb % 4 == 0, (off, nb)
        assert off + nb <= self.cfg.arena_bytes, ("arena overflow", off, nb, self.cfg.arena_bytes)
        ap = self.arena[:, off // 4:(off + nb) // 4]
        if dtype != F32:
            ap = ap.bitcast(dtype)
        if len(dims) == 2:
            ap = ap.rearrange("p (a b) -> p a b", a=dims[0])
        elif len(dims) == 3:
            ap = ap.rearrange("p (a b c) -> p a b c", a=dims[0], b=dims[1])
        return ap

    def slab_plan(self):
        cfg = self.cfg
        D, DC, DG, GC, FF, SL = cfg.D, cfg.DC, cfg.DG, cfg.GC, cfg.FF, cfg.SL
        W = self.wdram
        plan = []

        def rows(ap):
            return ap.rearrange("(kc p) n -> p kc n", p=128)
        for _s in range(cfg.NSEQ):
            for kind, li in zip(cfg.kinds, cfg.layer_ids):
                if kind == "pool":
                    win, wg, wo = W[f"l{li}_pool_w_in"], W[f"l{li}_pool_w_group"], W[f"l{li}_pool_w_out"]
                    for g in range(4):
                        plan.append([(0, (DC, DG), rows(win[:, g * DG:(g + 1) * DG])),
                                     (DC * DG * 2, (GC, DG), rows(wg[g]))])
                    plan.append([(0, (DC, D), rows(wo))])
                elif kind == "conv":
                    win, wo = W[f"l{li}_conv_w_in"], W[f"l{li}_conv_w_out"]
                    for c in range(DC):
                        plan.append([(0, (DC, 128), rows(win[:, c * 128:(c + 1) * 128])),
                                     (DC * 128 * 2, (DC, 128), rows(win[:, D + c * 128:D + (c + 1) * 128]))])
                    plan.append([(0, (DC, D), rows(wo))])
                else:
                    win, wo = W[f"l{li}_fourier_w_in"], W[f"l{li}_fourier_w_out"]
                    plan.append([(0, (DC, D), rows(win))])
                    plan.append([(0, (DC, D), rows(wo))])
                up, dn = W[f"l{li}_mlp_up"], W[f"l{li}_mlp_down"]
                for s in range(FF // SL):
                    plan.append([(0, (DC, SL), rows(up[:, s * SL:(s + 1) * SL])),
                                 (DC * SL * 2, (SL // 128, D), rows(dn[s * SL:(s + 1) * SL, :]))])
        return plan

    def w_issue(self):
        while self.w_issued < len(self.plan) and self.w_issued < self.w_done + 2:
            k = self.w_issued
            slot = k % 2
            for (off, dims, src) in self.plan[k]:
                dst = self.view(self.W_OFF + slot * self.W_SLOT + off, BF16, dims)
                self.dma("pool", f"w{slot}", k, dst, src)
            self.w_issued += 1

    def w_next(self):
        k = self.w_cur
        self.w_cur += 1
        self.w_issue()
        assert k < self.w_issued
        slot = k % 2
        return [self.view(self.W_OFF + slot * self.W_SLOT + off, BF16, dims) for (off, dims, _src) in self.plan[k]]

    def w_release(self):
        self.w_done += 1
        self.w_issue()

    def vcol(self, name, c, n=1):
        o = self.vlay[name] + c
        return self.VEC[:, o:o + n]

    class NormState:
        def __init__(self, gname, final=False, seq=0):
            self.gname, self.final, self.seq = gname, final, seq
            self.pending = None
            self.deferred = []
            self.done = set()
            self.after = None

    def emit_pre(self, n, j):
        sl = slice(j * TT, (j + 1) * TT)
        for c in range(self.cfg.DC):
            self.act(self.H[:, c, sl], self.X[:, c, sl], AF.Square)
        n.pending = j

    def flush_post(self, n):
        cfg = self.cfg
        j = n.pending
        if j is None:
            return
        n.pending = None
        sl = slice(j * TT, (j + 1) * TT)
        bank = self.rotS.next()
        for c in range(cfg.DC):
            self.mm(bank, self.ONES, self.H[:, c, sl], c == 0, c == cfg.DC - 1)
        r = self.RS.next()
        self.act(r, bank, AF.Sqrt, bias=self.vcol("eps", 0), scale=1.0 / cfg.D)
        self.recip(r, r)
        if not n.final:
            for c in range(cfg.DC):
                self.stt(self.H[:, c, sl], self.X[:, c, sl], self.vcol(n.gname, c), r, ALU.mult, ALU.mult)
        else:
            o = self.O[j % 2]
            for c in range(cfg.DC):
                self.stt(o[:, c, :], self.X[:, c, sl], self.vcol(n.gname, c), r, ALU.mult, ALU.mult)
            self.out_final.append(self.dma("sp", f"out{j % 2}", ("o", n.seq, j),
                                           self.outT[n.seq][:, sl].rearrange("(c p) t -> p c t", p=128), o))
            if n.after is not None:
                self.load_x_tile(n.seq + 1, j)
                self._ready(n.after, j)
        n.done.add(j)

    def load_x_tile(self, s, j):
        sl = slice(j * TT, (j + 1) * TT)
        self.dma("sp", f"xin{j}", ("x", s, j), self.X[:, :, sl],
                 self.xT[s][:, sl].rearrange("(c p) t -> p c t", p=128))

    def _ready(self, n, j):
        if n is None:
            return
        if self.h_free:
            self.flush_post(n)
            self.emit_pre(n, j)
        else:
            n.deferred.append(j)

    def x_ready(self, j):
        self._ready(self.nxt, j)

    def drain(self, n):
        if n is None:
            return
        assert self.h_free
        while n.deferred:
            self.flush_post(n)
            self.emit_pre(n, n.deferred.pop(0))

    def need_H(self, j):
        n = self.cur
        while j not in n.done:
            assert self.h_free
            if n.pending is not None:
                self.flush_post(n)
            elif n.deferred:
                self.emit_pre(n, n.deferred.pop(0))
            else:
                raise AssertionError(("H tile never produced", j))

    def rms_stats_sq(self, j):
        raise NotImplementedError

    def proj_residual(self, wout, src_of_kc, j, nk):
        cfg = self.cfg
        sl = slice(j * TT, (j + 1) * TT)
        for oc in range(cfg.DC):
            bank = self.rotA.next()
            for kc in range(nk):
                self.mm(bank, wout[:, kc, oc * 128:(oc + 1) * 128], src_of_kc(kc), kc == 0, kc == nk - 1)
            self.tt(self.X[:, oc, sl], self.X[:, oc, sl], bank, ALU.add)
        self.x_ready(j)

    def mlp(self, li):
        cfg = self.cfg
        DC, SL, NT = cfg.DC, cfg.SL, cfg.NT
        FCS = SL // 128
        nslab = cfg.FF // SL
        assert nslab >= 2
        flex = self.FLEX
        A = [self.view(flex + i * FCS * TT * 2, BF16, (FCS, TT)) for i in range(2)]
        toff = flex + 2 * FCS * TT * 2
        Tr = Rot([self.view(toff + i * TT * 4, F32, (TT,)) for i in range(3)])
        steps = [(s, j) for s in range(cfg.FF // SL) for j in range(NT)]
        slabs = {}

        def up(i):
            s, j = steps[i]
            if s not in slabs:
                slabs[s] = self.w_next()
            wu = slabs[s][0]
            sl = slice(j * TT, (j + 1) * TT)
            a = A[i % 2]
            if s == 0:
                self.need_H(j)
            for fc in range(FCS):
                bank = self.rotU.next()
                for kc in range(DC):
                    self.mm(bank, wu[:, kc, fc * 128:(fc + 1) * 128], self.H[:, kc, sl], kc == 0, kc == DC - 1)
                t = Tr.next()
                self.act(t, bank, AF.Relu)
                self.tt(a[:, fc, :], t, t, ALU.mult)

        def down(i):
            s, j = steps[i]
            wd = slabs[s][1]
            sl = slice(j * TT, (j + 1) * TT)
            a = A[i % 2]
            for dc in range(DC):
                bank = self.rotD.next()
                for fc in range(FCS):
                    self.mm(bank, wd[:, fc, dc * 128:(dc + 1) * 128], a[:, fc, :], fc == 0, fc == FCS - 1)
                self.tt(self.X[:, dc, sl], self.X[:, dc, sl], bank, ALU.add)
            if s == nslab - 1:
                self.x_ready(j)
            if j == NT - 1:
                self.w_release()
        up(0)
        for i in range(len(steps)):
            if i + 1 < len(steps):
                up(i + 1)
            down(i)

    def pool_layer(self, li):
        cfg = self.cfg
        DC, GC, DG, S, NT = cfg.DC, cfg.GC, cfg.DG, cfg.S, cfg.NT
        L = S + 16
        off = self.FLEX
        Ub = []
        for _i in range(2):
            Ub.append(self.view(off, F32, (L,))); off = al(off + L * 4)
        T = self.view(off, F32, (L,)); off = al(off + L * 4)
        E1 = self.view(off, F32, (8,)); off = al(off + 32)
        Pb = []
        for _i in range(2):
            Pb.append(self.view(off, BF16, (GC, S))); off += GC * S * 2
        Y = self.view(off, BF16, (DC, S)); off += DC * S * 2
        assert off <= self.FLEX_END, (off, self.FLEX_END)
        for u in Ub:
            self.memset(u[:, 0:8], 0.0)
            self.memset(u[:, 8 + S:L], 0.0)
        eo = self.vlay["edges"]
        nch = 4 * GC
        slabs = {}

        def stageA(ci):
            g, oc = divmod(ci, GC)
            if g not in slabs:
                slabs[g] = self.w_next()
            win = slabs[g][0]
            u = Ub[ci % 2]
            for j in range(NT):
                if ci == 0:
                    self.need_H(j)
                bank = self.rotA.next()
                for kc in range(DC):
                    self.mm(bank, win[:, kc, oc * 128:(oc + 1) * 128], self.H[:, kc, j * TT:(j + 1) * TT],
                            kc == 0, kc == DC - 1)
                self.act(u[:, 8 + j * TT:8 + (j + 1) * TT], bank, AF.Copy)

        def stageB(ci):
            g, oc = divmod(ci, GC)
            w = POOL_WINDOWS[g]
            half = w // 2
            Uc = Ub[ci % 2]
            Pg = Pb[g % 2]
            self.tt(T[:, 0:L - 1], Uc[:, 0:L - 1], Uc[:, 1:L], ALU.add)
            cur = 2
            while cur < w:
                n = L - 2 * cur + 1
                self.tt(T[:, 0:n], T[:, 0:n], T[:, cur:cur + n], ALU.add)
                cur *= 2
            self.stt(Pg[:, oc, :], T[:, 8 - half:8 - half + S], 1.0 / w, Uc[:, 8:8 + S], ALU.mult, ALU.subtract)
            el = self.VEC[:, eo + (g * 2) * 8: eo + (g * 2) * 8 + half]
            self.tt(E1[:, 0:half], T[:, 8 - half:8], el, ALU.mult)
            self.tt(Pg[:, oc, 0:half], E1[:, 0:half], Uc[:, 8:8 + half], ALU.subtract)
            nr = half - 1
            if nr > 0:
                i0 = S - half + 1
                er = self.VEC[:, eo + (g * 2 + 1) * 8: eo + (g * 2 + 1) * 8 + nr]
                self.tt(E1[:, 0:nr], T[:, i0 + 8 - half:i0 + 8 - half + nr], er, ALU.mult)
                self.tt(Pg[:, oc, i0:S], E1[:, 0:nr], Uc[:, 8 + i0:8 + S], ALU.subtract)

        def stageC(g):
            wg = slabs[g][1]
            Pg = Pb[g % 2]
            for oc2 in range(GC):
                for j in range(NT):
                    bank = self.rotA.next()
                    for kc in range(GC):
                        self.mm(bank, wg[:, kc, oc2 * 128:(oc2 + 1) * 128], Pg[:, kc, j * TT:(j + 1) * TT],
                                kc == 0, kc == GC - 1)
                    self.act(Y[:, g * GC + oc2, j * TT:(j + 1) * TT], bank, AF.Copy,
                             scale=self.vcol(f"l{li}_pool_scale", g * GC + oc2))
            self.w_release()

        stageA(0)
        if nch > 1:
            stageA(1)
        for ci in range(nch):
            stageB(ci)
            if ci % GC == GC - 1:
                stageC(ci // GC)
            if ci + 2 < nch:
                stageA(ci + 2)
        (wout,) = self.w_next()
        for j in range(NT):
            self.proj_residual(wout, lambda kc, j=j: Y[:, kc, j * TT:(j + 1) * TT], j, DC)
        self.w_release()

    def conv_layer(self, li):
        cfg = self.cfg
        DC, S, NT, D = cfg.DC, cfg.S, cfg.NT, cfg.D
        VL = S + 2 * CONV_PAD
        hbytes = DC * S * 2
        vbytes = al(VL * 2)
        if hbytes + (DC - 1) * vbytes > DC * S * 4:
            vbytes = VL * 2
        assert hbytes + (DC - 1) * vbytes <= DC * S * 4
        base = self.H_OFF
        cend = base + DC * S * 4
        cbytes = S * 4

        def voff(c):
            return base + hbytes + c * vbytes if c < DC - 1 else cend

        def Vc(c):
            return self.view(voff(c), BF16, (VL,))
        for c in range(DC):
            for k in range(DC):
                lo, hi = max(base + c * cbytes, voff(k)), min(base + (c + 1) * cbytes, voff(k) + vbytes)
                assert lo >= hi or k < c, (c, k)
        C = self.view(base, F32, (DC, S))
        off = al(cend + vbytes)
        DGb = []
        for _i in range(2):
            DGb.append(self.view(off, BF16, (CONV_W, 128))); off = al(off + CONV_W * 128 * 2)
        lnoff = al(cend + vbytes)
        SG = Rot([self.view(off + i * TT * 4, F32, (TT,)) for i in range(2)]); off += 2 * TT * 4
        assert off <= self.FLEX_END
        NLT = 10
        LT = [self.view(lnoff + i * TT * 4, F32, (TT,)) for i in range(NLT)]; lnoff += NLT * TT * 4
        SLd = []
        for _i in range(2):
            SLd.append(self.view(lnoff, BF16, (DC, TT))); lnoff += DC * TT * 2
        assert lnoff <= self.FLEX_END, (lnoff, self.FLEX_END)
        for c in range(DC):
            v = Vc(c)
            self.memset(v[:, 0:CONV_PAD], 0.0)
            self.memset(v[:, CONV_PAD + S:VL], 0.0)
        for c in range(DC):
            wv, wgt = self.w_next()
            v = Vc(c)
            for j in range(NT):
                sl = slice(j * TT, (j + 1) * TT)
                if c == 0:
                    self.need_H(j)
                bv = self.rotA.next()
                bg = self.rotA.next()
                for kc in range(DC):
                    self.mm(bv, wv[:, kc, :], self.H[:, kc, sl], kc == 0, kc == DC - 1)
                for kc in range(DC):
                    self.mm(bg, wgt[:, kc, :], self.H[:, kc, sl], kc == 0, kc == DC - 1)
                sg = SG.next()
                self.act(sg, bg, AF.Sigmoid)
                self.tt(v[:, CONV_PAD + j * TT:CONV_PAD + (j + 1) * TT], bv, sg, ALU.mult)
            self.w_release()
        self.h_free = False
        dwo = self.vlay[f"l{li}_conv_dw"]
        for c in range(DC):
            dwc = self.VEC[:, dwo + c * CONV_W: dwo + (c + 1) * CONV_W]
            DGm = DGb[c % 2]
            self.tt(DGm, self.IDENT.unsqueeze(1).broadcast_to([128, CONV_W, 128]),
                    dwc.unsqueeze(2).broadcast_to([128, CONV_W, 128]), ALU.mult)
            v = Vc(c)
            for j in range(NT):
                bank = self.rotA.next()
                for k in range(CONV_W):
                    self.mm(bank, DGm[:, k, :], v[:, j * TT + k:j * TT + k + TT], k == 0, k == CONV_W - 1)
                self.act(C[:, c, j * TT:(j + 1) * TT], bank, AF.Identity, bias=self.vcol(f"l{li}_conv_dw_bias", c))
        (wout,) = self.w_next()
        MEANb, RSTDb, MSQ = LT[0:3], LT[3:6], LT[6]
        T1 = Rot(LT[7:10])
        sbanks = {}

        def stats_chunk(j, c, copy_eng="act"):
            sl = slice(j * TT, (j + 1) * TT)
            if c == 0:
                sbanks[j] = (self.rotS.next(), self.rotS.next())
            b1, b2 = sbanks[j]
            cb = self.SQ.next()
            self.copy(cb, C[:, c, sl], eng=copy_eng)
            cs = self.SQ.next()
            self.act(cs, C[:, c, sl], AF.Square)
            self.mm(b1, self.ONES, cb, c == 0, c == DC - 1)
            self.mm(b2, self.ONES, cs, c == 0, c == DC - 1)

        def lnmath(j):
            b1, b2 = sbanks[j]
            MEAN, RSTD = MEANb[j % 3], RSTDb[j % 3]
            self.ts(MEAN, b1, 1.0 / D, ALU.mult)
            self.tt(MSQ, MEAN, MEAN, ALU.mult)
            self.stt(RSTD, b2, 1.0 / D, MSQ, ALU.mult, ALU.subtract)
            self.act(RSTD, RSTD, AF.Sqrt, bias=self.vcol("eps", 1))
            self.recip(RSTD, RSTD)

        def normalize(j):
            sl = slice(j * TT, (j + 1) * TT)
            MEAN, RSTD = MEANb[j % 3], RSTDb[j % 3]
            SLb = SLd[j % 2]
            for c in range(DC):
                t1 = T1.next()
                self.tt(t1, C[:, c, sl], MEAN, ALU.subtract)
                self.tt(t1, t1, RSTD, ALU.mult)
                self.act(SLb[:, c, :], t1, AF.Silu, bias=self.vcol(f"l{li}_conv_ln_b", c),
                         scale=self.vcol(f"l{li}_conv_ln_g", c))

        for c in range(DC):
            stats_chunk(0, c, copy_eng="dve")
        lnmath(0)
        if NT > 1:
            for c in range(DC):
                stats_chunk(1, c, copy_eng="dve")
            lnmath(1)
        normalize(0)
        for j in range(NT):
            sl = slice(j * TT, (j + 1) * TT)
            SLb = SLd[j % 2]
            for oc in range(DC):
                bank = self.rotA.next()
                for kc in range(DC):
                    self.mm(bank, wout[:, kc, oc * 128:(oc + 1) * 128], SLb[:, kc, :], kc == 0, kc == DC - 1)
                self.tt(self.X[:, oc, sl], self.X[:, oc, sl], bank, ALU.add)
                if j + 2 < NT:
                    stats_chunk(j + 2, oc)
            self.x_ready(j)
            if j + 2 < NT:
                lnmath(j + 2)
            if j + 1 < NT:
                normalize(j + 1)
        self.w_release()
        self.h_free = True
        self.drain(self.nxt)

    def four_layer(self, li):
        cfg = self.cfg
        DC, GC, DG, S, NT, TC = cfg.DC, cfg.GC, cfg.DG, cfg.S, cfg.NT, cfg.TC
        H2 = TC // 2
        abrow = 4 * 2 * DG
        ab1 = self.H_OFF
        off = self.FLEX
        ab2 = off; off += H2 * abrow * 2
        ut = off; off += DC * S * 2
        fj = off; off += DC * TT * 2
        UT = self.view(ut, BF16, (DC, S))
        Fj = self.view(fj, BF16, (DC, TT))

        def AB(tc):
            o = (ab1 if tc < H2 else ab2) + (tc % H2) * abrow * 2
            return self.view(o, BF16, (4, 2 * DG))
        tabs = [self.view(ut + h * (2 * H2 * TT * 2), BF16, (2, H2, TT)) for h in range(2)]
        assert 2 * (2 * H2 * TT * 2) <= DC * S * 2
        (win,) = self.w_next()
        for oc in range(DC):
            for j in range(NT):
                sl = slice(j * TT, (j + 1) * TT)
                if oc == 0:
                    self.need_H(j)
                bank = self.rotA.next()
                for kc in range(DC):
                    self.mm(bank, win[:, kc, oc * 128:(oc + 1) * 128], self.H[:, kc, sl], kc == 0, kc == DC - 1)
                self.copy(UT[:, oc, sl], bank, eng="act")
        self.w_release()
        self.h_free = False
        n = 0
        for tc in range(TC):
            ab = AB(tc)
            for g in range(4):
                bank = self.rotA.next()
                for kc in range(GC):
                    self.mm(bank[:, 0:2 * DG], UT[:, g * GC + kc, tc * 128:(tc + 1) * 128], self.CSC[:, kc, :],
                            kc == 0, kc == GC - 1)
                self.copy(ab[:, g, :], bank[:, 0:2 * DG], eng=("act" if n % 2 == 0 else "dve"))
                n += 1
        (wout,) = self.w_next()
        scale = 1.0 / math.sqrt(S)
        for j in range(NT):
            for h in range(2):
                self.dma("sp", f"t{h}", (li, self.seq_i, j), tabs[h], self.dfts[j, h])
            for ch in range(DC):
                g, hh = ch // GC, ch % GC
                bank = self.rotA.next()
                k = 0
                for tc in range(TC):
                    ab = AB(tc)
                    tab = tabs[tc // H2]
                    for cs_i in range(2):
                        col = cs_i * DG + hh * 128
                        self.mm(bank, ab[:, g, col:col + 128], tab[:, cs_i, tc % H2, :], k == 0, k == 2 * TC - 1)
                        k += 1
                self.act(Fj[:, ch, :], bank, AF.Copy, scale=scale)
            self.proj_residual(wout, lambda kc: Fj[:, kc, :], j, DC)
        self.w_release()
        self.h_free = True
        self.drain(self.nxt)

    def finish(self, s):
        cfg = self.cfg
        if not cfg.final:
            self.out_final.append(self.dma("sp", "out0", ("o", s), self.outT[s].rearrange("(c p) t -> p c t", p=128),
                                           self.X))
            return
        for j in range(cfg.NT):
            self.need_H(j)

    def build(self):
        cfg = self.cfg
        nc = self.nc
        DC, S, D = cfg.DC, cfg.S, cfg.D
        lay, nv = vec_layout(cfg)
        self.vlay = lay
        xT = nc.dram_tensor("xT", [cfg.NSEQ, D, S], F32, kind="ExternalInput").ap()
        vecs = nc.dram_tensor("vecs", [128, nv], F32, kind="ExternalInput").ap()
        cmat = nc.dram_tensor("cmat", [128, 256], BF16, kind="ExternalInput").ap()
        has_four = "four" in cfg.kinds
        if has_four:
            dftc = nc.dram_tensor("dftc", [128, cfg.GC, 2 * cfg.DG], BF16, kind="ExternalInput").ap()
            self.dfts = nc.dram_tensor("dfts", [cfg.NT, 2, 128, 2, cfg.TC // 2, TT], BF16, kind="ExternalInput").ap()
        self.wdram = {}
        for name, shape in weight_names(cfg):
            self.wdram[name] = nc.dram_tensor(name, list(shape), F32, kind="ExternalInput").ap()
        self.outT = nc.dram_tensor("outT", [cfg.NSEQ, D, S], F32, kind="ExternalOutput").ap()

        off = 0
        self.X_OFF = off; off += DC * S * 4
        self.H_OFF = off; off += DC * S * 2
        nvb = (nv * 4 + 3) // 4 * 4
        small = nvb + 256 * 2 + (cfg.GC * 2 * cfg.DG * 2 if has_four else 0) + 4 * TT * 2 + 2 * TT * 4
        self.W_SLOT = max(DC * cfg.SL * 2 + (cfg.SL // 128) * D * 2, DC * D * 2)
        self.W_SLOT = (self.W_SLOT + 255) // 256 * 256
        self.SM_OFF = (cfg.arena_bytes - small) // 256 * 256
        self.W_OFF = self.SM_OFF - 2 * self.W_SLOT
        self.FLEX = off
        self.FLEX_END = self.W_OFF
        assert self.FLEX_END > self.FLEX
        real_view = self.view

        with nc.allow_low_precision("bf16 matmul operands, fp32 accumulation"), \
                nc.sbuf_tensor("arena", [128, cfg.arena_bytes // 4], F32) as arena, \
                nc.psum_tensor("ps", [128, 8, TT], F32) as ps:
            self.arena = arena

            def flexview(off_, dtype, dims):
                return real_view(off_, dtype, dims)
            so = self.SM_OFF
            self.VEC = self.view(so, F32, (nv,)); so += nvb
            CM = self.view(so, BF16, (256,)); so += 512
            self.ONES = CM[:, 0:128]
            self.IDENT = CM[:, 128:256]
            if has_four:
                self.CSC = self.view(so, BF16, (cfg.GC, 2 * cfg.DG)); so += cfg.GC * 2 * cfg.DG * 2
            self.SQ = Rot([self.view(so + i * TT * 2, BF16, (TT,)) for i in range(4)]); so += 4 * TT * 2
            self.RS = Rot([self.view(so + i * TT * 4, F32, (TT,)) for i in range(2)]); so += 2 * TT * 4
            assert so <= cfg.arena_bytes
            self.X = self.view(self.X_OFF, F32, (DC, S))
            self.H = self.view(self.H_OFF, BF16, (DC, S))
            banks = [ps[:, i, :] for i in range(8)]
            self.rotS = Rot(banks[0:2])
            self.rotA = Rot(banks[2:8])
            self.rotU = Rot(banks[2:5])
            self.rotD = Rot(banks[5:8])

            self.dma("sp", "c0", "c", self.VEC, vecs)
            self.dma("sp", "c0", "c", CM, cmat)
            if has_four:
                self.dma("sp", "c0", "c", self.CSC, dftc)

            self.plan = self.slab_plan()
            self.w_issued = self.w_done = self.w_cur = 0
            self.xT = xT
            self.h_free = True
            ooff = self.FLEX + 2 * (cfg.SL // 128) * TT * 2 + 3 * TT * 4
            self.O = [self.view(ooff + i * DC * TT * 4, F32, (DC, TT)) for i in range(2)]
            assert ooff + 2 * DC * TT * 4 <= self.FLEX_END
            NS = Builder.NormState
            norms = []
            for s in range(cfg.NSEQ):
                row = []
                for kind, li in zip(cfg.kinds, cfg.layer_ids):
                    row.append(NS(f"l{li}_norm_mix", seq=s))
                    row.append(NS(f"l{li}_norm_mlp", seq=s))
                if cfg.final:
                    row.append(NS("final_norm", final=True, seq=s))
                else:
                    row.append(None)
                norms.append(row)
            if cfg.final:
                for s in range(cfg.NSEQ - 1):
                    norms[s][-1].after = norms[s + 1][0]
            self.cur = self.nxt = None
            for s in range(cfg.NSEQ):
                self.seq_i = s
                row = norms[s]
                if s == 0 or not cfg.final:
                    for j in range(cfg.NT):
                        self.load_x_tile(s, j)
                        self._ready(row[0], j)
                k = 0
                for kind, li in zip(cfg.kinds, cfg.layer_ids):
                    self.cur, self.nxt = row[k], row[k + 1]
                    if kind == "pool":
                        self.pool_layer(li)
                    elif kind == "conv":
                        self.conv_layer(li)
                    else:
                        self.four_layer(li)
                    k += 1
                    self.cur, self.nxt = row[k], row[k + 1]
                    self.mlp(li)
                    k += 1
                self.cur, self.nxt = row[k], None
                self.finish(s)
            assert self.w_cur == len(self.plan)

            P = self.P
            P.finalize()
            finals = {}
            for op in self.out_final:
                key = ("c", op.chan)
                finals[key] = max(finals.get(key, 0), op.gend)
            keys = P.sem_keys()
            import contextlib
            with contextlib.ExitStack() as es:
                sems = {}
                for k in keys:
                    sems[k] = es.enter_context(nc.semaphore(f"s_{k[0]}_{k[1]}"))
                block = es.enter_context(nc.Block())
                P.emit(block, sems, list(finals.items()))
        return nc


_KINDS = ("pool", "conv", "four", "pool")


def make_in_maps(cfg, inputs, x_shards):
    vec = pack_vecs(cfg, inputs)
    cmat, csc, tab = const_tables(cfg)
    base = {"vecs": vec, "cmat": cmat}
    if "four" in cfg.kinds:
        base["dftc"] = csc
        base["dfts"] = tab
    for name, _shape in weight_names(cfg):
        base[name] = np.ascontiguousarray(np.asarray(inputs[name], np.float32))
    maps = []
    for xs in x_shards:
        m = dict(base)
        m["xT"] = xs
        maps.append(m)
    return maps


def kernel(**inputs):
    x = np.asarray(inputs["x"], np.float32)
    B, S, D = x.shape
    ncores = 8
    nseq = B // ncores
    cfg = Cfg(S=S, D=D, FF=4 * D, NSEQ=nseq, kinds=_KINDS, final=True)
    xT = np.ascontiguousarray(x.transpose(0, 2, 1))
    shards = [xT[i * nseq:(i + 1) * nseq] for i in range(ncores)]
    nc = Builder(cfg).build()
    in_maps = make_in_maps(cfg, inputs, shards)
    res = run_bass_kernel_spmd(nc, in_maps, core_ids=list(range(ncores)))
    outT = np.concatenate([np.asarray(r["outT"]) for r in res.results], axis=0)
    return np.ascontiguousarray(outT.transpose(0, 2, 1)).astype(np.float32)
```

```python
import math
import numpy as np
import ml_dtypes
import concourse.bass as bass
import concourse.mybir as mybir
from concourse.bass_utils import run_bass_kernel_spmd

F32 = mybir.dt.float32
BF16 = mybir.dt.bfloat16
AF = mybir.ActivationFunctionType
ALU = mybir.AluOpType

POOL_WINDOWS = (2, 4, 8, 16)
CONV_W = 31
CONV_PAD = 15
NORM_EPS = 1e-6
LN_EPS = 1e-5
TT = 512
GRAN = 256


class Cfg:
    def __init__(self, S=2048, D=1024, FF=4096, NSEQ=2, kinds=("pool", "conv", "four", "pool"),
                 layer_ids=None, final=True, SL=512, arena_bytes=220160, dma_scratch=8192):
        self.S, self.D, self.FF, self.NSEQ = S, D, FF, NSEQ
        self.kinds = tuple(kinds)
        self.layer_ids = tuple(layer_ids) if layer_ids is not None else tuple(range(len(kinds)))
        self.final = final
        self.DC = D // 128
        self.GC = self.DC // 4
        self.DG = D // 4
        self.NT = S // TT
        self.TC = S // 128
        self.SL = SL
        self.arena_bytes = arena_bytes
        self.dma_scratch = dma_scratch
        assert D % 512 == 0 and S % TT == 0 and FF % SL == 0 and self.TC % 2 == 0


def vec_layout(cfg):
    lay, off = {}, 0

    def put(name, n):
        nonlocal off
        lay[name] = off
        off += n
    DC = cfg.DC
    for kind, li in zip(cfg.kinds, cfg.layer_ids):
        put(f"l{li}_norm_mix", DC)
        put(f"l{li}_norm_mlp", DC)
        if kind == "pool":
            put(f"l{li}_pool_scale", DC)
        if kind == "conv":
            put(f"l{li}_conv_dw_bias", DC)
            put(f"l{li}_conv_ln_g", DC)
            put(f"l{li}_conv_ln_b", DC)
            put(f"l{li}_conv_dw", DC * CONV_W)
    put("final_norm", DC)
    put("edges", 4 * 2 * 8)
    put("eps", 2)
    return lay, off


def weight_names(cfg):
    names = []
    for kind, li in zip(cfg.kinds, cfg.layer_ids):
        if kind == "pool":
            names += [(f"l{li}_pool_w_in", (cfg.D, cfg.D)), (f"l{li}_pool_w_group", (4, cfg.DG, cfg.DG)),
                      (f"l{li}_pool_w_out", (cfg.D, cfg.D))]
        elif kind == "conv":
            names += [(f"l{li}_conv_w_in", (cfg.D, 2 * cfg.D)), (f"l{li}_conv_w_out", (cfg.D, cfg.D))]
        else:
            names += [(f"l{li}_fourier_w_in", (cfg.D, cfg.D)), (f"l{li}_fourier_w_out", (cfg.D, cfg.D))]
        names += [(f"l{li}_mlp_up", (cfg.D, cfg.FF)), (f"l{li}_mlp_down", (cfg.FF, cfg.D))]
    return names


def pack_vecs(cfg, inputs):
    lay, nv = vec_layout(cfg)
    DC = cfg.DC
    V = np.zeros((128, nv), np.float32)

    def colmajor(v):
        return np.asarray(v, np.float32).reshape(DC, 128).T
    for name, off in lay.items():
        if name == "edges":
            for wi, w in enumerate(POOL_WINDOWS):
                h = w // 2
                for i in range(h):
                    V[:, off + (wi * 2 + 0) * 8 + i] = 1.0 / (i + h)
                for m in range(h - 1):
                    i = cfg.S - h + 1 + m
                    V[:, off + (wi * 2 + 1) * 8 + m] = 1.0 / (cfg.S - i + h)
        elif name == "eps":
            V[:, off] = NORM_EPS
            V[:, off + 1] = LN_EPS
        elif name.endswith("conv_dw"):
            dw = np.asarray(inputs[name], np.float32)
            V[:, off:off + DC * CONV_W] = dw.T.reshape(DC, 128, CONV_W).transpose(1, 0, 2).reshape(128, DC * CONV_W)
        else:
            V[:, off:off + DC] = colmajor(inputs[name])
    return V


def const_tables(cfg):
    bf = ml_dtypes.bfloat16
    cmat = np.zeros((128, 256), np.float32)
    cmat[:, 0:128] = 1.0
    cmat[:, 128:256] = np.eye(128, dtype=np.float32)
    DG, GC, S, TC, NT = cfg.DG, cfg.GC, cfg.S, cfg.TC, cfg.NT
    a = np.arange(DG, dtype=np.float64)
    ang = 2.0 * np.pi * np.outer(a, a) / DG
    csc = np.concatenate([np.cos(ang), np.sin(ang)], axis=1) / math.sqrt(DG)
    csc = csc.reshape(GC, 128, 2 * DG).transpose(1, 0, 2)
    n = np.arange(S, dtype=np.int64)
    prod = np.outer(n, n) % S
    ang2 = 2.0 * np.pi * prod.astype(np.float64) / S
    cs = np.cos(ang2)
    ns = -np.sin(ang2)
    H2 = TC // 2
    tab = np.stack([cs, ns], axis=0)
    tab = tab.reshape(2, 2, H2, 128, NT, TT)
    tab = tab.transpose(4, 1, 3, 0, 2, 5)
    return (cmat.astype(bf), np.ascontiguousarray(csc).astype(bf),
            np.ascontiguousarray(tab).astype(bf))


class Op:
    __slots__ = ("eng", "build", "deps", "signal", "count", "chan", "group", "seq", "waits", "gend")


class Prog:
    ENG = ("pe", "act", "dve", "pool", "sp")

    def __init__(self):
        self.ops = {e: [] for e in self.ENG}
        self.lastw = {}
        self.readers = {}
        self.chan_ops = {}
        self.gcache = {}
        self.seq = 0

    def gran(self, ap):
        if str(ap.space) == "DRAM":
            return ()
        dims = ap.ap
        key = (ap.tensor.name, ap.offset, dims, str(ap.dtype))
        g = self.gcache.get(key)
        if g is not None:
            return g
        esz = 2 if ap.dtype == BF16 else 4
        pstep = dims[0][0]
        off = ap.offset % pstep if pstep > 0 else ap.offset
        free = [d for d in dims[1:] if d[1] > 1 and d[0] != 0]
        if not free:
            free = [(1, 1)]
        starts = np.array([off], dtype=np.int64)
        for step, cnt in free[:-1]:
            starts = (starts[:, None] + np.arange(cnt, dtype=np.int64)[None, :] * step).ravel()
        step_l, cnt_l = free[-1]
        lo = starts * esz
        hi = (starts + (cnt_l - 1) * abs(step_l) + 1) * esz
        name = ap.tensor.name
        s = set()
        for l, h in zip(lo.tolist(), hi.tolist()):
            for gi in range(l // GRAN, (h - 1) // GRAN + 1):
                s.add((name, gi))
        g = tuple(s)
        self.gcache[key] = g
        return g

    def add(self, eng, build, reads=(), writes=(), chan=None, group=None):
        op = Op()
        op.eng, op.build, op.chan, op.group, op.signal = eng, build, chan, group, False
        op.seq = self.seq
        self.seq += 1
        deps = {}

        def dep(d):
            if d is None:
                return
            if d.chan is None and d.eng == "pe" and eng == "pe" and chan is None:
                return
            if chan is not None and d.chan == chan and d.group == group:
                return
            key = ("c", d.chan) if d.chan else ("e", d.eng)
            cur = deps.get(key)
            if cur is None or d.seq > cur.seq:
                deps[key] = d
        rg = set()
        for ap in reads:
            rg.update(self.gran(ap))
        wg = set()
        for ap in writes:
            wg.update(self.gran(ap))
        for g in rg:
            dep(self.lastw.get(g))
        for g in wg:
            dep(self.lastw.get(g))
            rd = self.readers.get(g)
            if rd:
                for r in rd.values():
                    dep(r)
        mykey = ("c", chan) if chan else ("e", eng)
        for g in rg:
            if g not in wg:
                self.readers.setdefault(g, {})[mykey] = op
        for g in wg:
            self.lastw[g] = op
            self.readers[g] = {}
        op.deps = list(deps.values())
        for d in op.deps:
            d.signal = True
        self.ops[eng].append(op)
        if chan:
            self.chan_ops.setdefault(chan, []).append(op)
        return op

    def finalize(self):
        for e in self.ENG:
            c = 0
            for op in self.ops[e]:
                if op.chan is None:
                    if op.signal:
                        c += 1
                    op.count = c
        for ch, lst in self.chan_ops.items():
            c = 0
            ends = {}
            for op in lst:
                c += 16
                op.count = c
                ends[op.group] = c
            for op in lst:
                op.gend = ends[op.group]
        for e in self.ENG:
            waited = {}
            for op in self.ops[e]:
                w = []
                for d in op.deps:
                    if d.chan:
                        key, val = ("c", d.chan), d.gend
                    else:
                        key, val = ("e", d.eng), d.count
                    if waited.get(key, 0) < val:
                        waited[key] = val
                        w.append((key, val))
                op.waits = w

    def sem_keys(self):
        keys = [("e", e) for e in self.ENG if any(o.chan is None for o in self.ops[e])]
        keys += [("c", ch) for ch in self.chan_ops]
        return keys

    def emit(self, block, sems, final_waits):
        attr = {"pe": "tensor", "act": "scalar", "dve": "vector", "pool": "gpsimd", "sp": "sync"}
        for e in self.ENG:
            ops = self.ops[e]
            if not ops:
                continue

            def body(eng, ops=ops, e=e):
                for op in ops:
                    for key, val in op.waits:
                        eng.wait_ge(sems[key], val)
                    ins = op.build(eng)
                    if op.chan:
                        ins.then_inc(sems[("c", op.chan)], 16)
                    elif op.signal:
                        ins.then_inc(sems[("e", e)], 1)
                if e == "sp":
                    for key, val in final_waits:
                        eng.wait_ge(sems[key], val)
            getattr(block, attr[e])(body)


def al(x, a=GRAN):
    return (x + a - 1) // a * a


class Rot:
    def __init__(self, items):
        self.items = list(items)
        self.i = 0

    def next(self):
        v = self.items[self.i % len(self.items)]
        self.i += 1
        return v


class Builder:
    def __init__(self, cfg):
        self.cfg = cfg
        self.P = Prog()
        self.nc = bass.Bass("TRN2", target_bir_lowering=False, dynamic_dma_scratch_size=cfg.dma_scratch)
        self.out_final = []

    def mm(self, out, lhsT, rhs, start, stop):
        self.P.add("pe", lambda e: e.matmul(out, lhsT, rhs, start=start, stop=stop),
                   reads=[lhsT, rhs], writes=[out])

    def act(self, out, in_, func, bias=None, scale=None, eng="act"):
        reads = [in_]
        kw = {}
        if bias is not None:
            kw["bias"] = bias
            if not isinstance(bias, (int, float)):
                reads.append(bias)
        if scale is not None:
            kw["scale"] = scale
            if not isinstance(scale, (int, float)):
                reads.append(scale)
        self.P.add(eng, lambda e: e.activation(out, in_, func, **kw), reads=reads, writes=[out])

    def tt(self, out, in0, in1, op, eng="dve"):
        self.P.add(eng, lambda e: e.tensor_tensor(out, in0, in1, op), reads=[in0, in1], writes=[out])

    def ts(self, out, in0, s1, op0, s2=None, op1=None, eng="dve"):
        reads = [in0] + [s for s in (s1, s2) if s is not None and not isinstance(s, (int, float))]
        if op1 is None:
            self.P.add(eng, lambda e: e.tensor_scalar(out, in0, s1, None, op0), reads=reads, writes=[out])
        else:
            self.P.add(eng, lambda e: e.tensor_scalar(out, in0, s1, s2, op0, op1), reads=reads, writes=[out])

    def stt(self, out, in0, scalar, in1, op0, op1, eng="dve"):
        reads = [in0, in1] + ([] if isinstance(scalar, (int, float)) else [scalar])
        self.P.add(eng, lambda e: e.scalar_tensor_tensor(out, in0, scalar, in1, op0, op1),
                   reads=reads, writes=[out])

    def copy(self, out, in_, eng="dve"):
        if eng == "act":
            self.act(out, in_, AF.Copy)
        else:
            self.P.add(eng, lambda e: e.tensor_copy(out, in_), reads=[in_], writes=[out])

    def recip(self, out, in_):
        self.P.add("dve", lambda e: e.reciprocal(out, in_), reads=[in_], writes=[out])

    def memset(self, ap, val, eng="dve"):
        self.P.add(eng, lambda e: e.memset(ap, val), writes=[ap])

    def dma(self, q, chan, group, out, in_, **kw):
        return self.P.add(q, lambda e: e.dma_start(out=out, in_=in_, **kw), reads=[in_], writes=[out],
                          chan=chan, group=group)

    def view(self, off, dtype, dims):
        esz = 2 if dtype == BF16 else 4
        n = 1
        for d in dims:
            n *= d
        nb = n * esz
        assert off % 4 == 0 and nb % 4 == 0, (off, nb)
        assert off + nb <= self.cfg.arena_bytes, ("arena overflow", off, nb, self.cfg.arena_bytes)
        ap = self.arena[:, off // 4:(off + nb) // 4]
        if dtype != F32:
            ap = ap.bitcast(dtype)
        if len(dims) == 2:
            ap = ap.rearrange("p (a b) -> p a b", a=dims[0])
        elif len(dims) == 3:
            ap = ap.rearrange("p (a b c) -> p a b c", a=dims[0], b=dims[1])
        return ap

    def slab_plan(self):
        cfg = self.cfg
        D, DC, DG, GC, FF, SL = cfg.D, cfg.DC, cfg.DG, cfg.GC, cfg.FF, cfg.SL
        W = self.wdram
        plan = []

        def rows(ap):
            return ap.rearrange("(kc p) n -> p kc n", p=128)
        for _s in range(cfg.NSEQ):
            for kind, li in zip(cfg.kinds, cfg.layer_ids):
                if kind == "pool":
                    win, wg, wo = W[f"l{li}_pool_w_in"], W[f"l{li}_pool_w_group"], W[f"l{li}_pool_w_out"]
                    for g in range(4):
                        plan.append([(0, (DC, DG), rows(win[:, g * DG:(g + 1) * DG])),
                                     (DC * DG * 2, (GC, DG), rows(wg[g]))])
                    plan.append([(0, (DC, D), rows(wo))])
                elif kind == "conv":
                    win, wo = W[f"l{li}_conv_w_in"], W[f"l{li}_conv_w_out"]
                    for c in range(DC):
                        plan.append([(0, (DC, 128), rows(win[:, c * 128:(c + 1) * 128])),
                                     (DC * 128 * 2, (DC, 128), rows(win[:, D + c * 128:D + (c + 1) * 128]))])
                    plan.append([(0, (DC, D), rows(wo))])
                else:
                    win, wo = W[f"l{li}_fourier_w_in"], W[f"l{li}_fourier_w_out"]
                    plan.append([(0, (DC, D), rows(win))])
                    plan.append([(0, (DC, D), rows(wo))])
                up, dn = W[f"l{li}_mlp_up"], W[f"l{li}_mlp_down"]
                for s in range(FF // SL):
                    plan.append([(0, (DC, SL), rows(up[:, s * SL:(s + 1) * SL])),
                                 (DC * SL * 2, (SL // 128, D), rows(dn[s * SL:(s + 1) * SL, :]))])
        return plan

    def w_issue(self):
        while self.w_issued < len(self.plan) and self.w_issued < self.w_done + 2:
            k = self.w_issued
            slot = k % 2
            for (off, dims, src) in self.plan[k]:
                dst = self.view(self.W_OFF + slot * self.W_SLOT + off, BF16, dims)
                self.dma("pool", f"w{slot}", k, dst, src)
            self.w_issued += 1

    def w_next(self):
        k = self.w_cur
        self.w_cur += 1
        self.w_issue()
        assert k < self.w_issued
        slot = k % 2
        return [self.view(self.W_OFF + slot * self.W_SLOT + off, BF16, dims) for (off, dims, _src) in self.plan[k]]

    def w_release(self):
        self.w_done += 1
        self.w_issue()

    def vcol(self, name, c, n=1):
        o = self.vlay[name] + c
        return self.VEC[:, o:o + n]

    class NormState:
        def __init__(self, gname, final=False, seq=0):
            self.gname, self.final, self.seq = gname, final, seq
            self.pending = None
            self.deferred = []
            self.done = set()
            self.after = None
            self.loadq = []

    def emit_pre(self, n, j):
        sl = slice(j * TT, (j + 1) * TT)
        for c in range(self.cfg.DC):
            self.act(self.H[:, c, sl], self.X[:, c, sl], AF.Square)
        n.pending = j

    def flush_post(self, n):
        cfg = self.cfg
        j = n.pending
        if j is None:
            return
        n.pending = None
        if n.final and n.loadq:
            self._ready(n.after, n.loadq.pop(0))
        sl = slice(j * TT, (j + 1) * TT)
        bank = self.rotS.next()
        for c in range(cfg.DC):
            self.mm(bank, self.ONES, self.H[:, c, sl], c == 0, c == cfg.DC - 1)
        r = self.RS.next()
        self.act(r, bank, AF.Sqrt, bias=self.vcol("eps", 0), scale=1.0 / cfg.D)
        self.recip(r, r)
        if not n.final:
            for c in range(cfg.DC):
                self.stt(self.H[:, c, sl], self.X[:, c, sl], self.vcol(n.gname, c), r, ALU.mult, ALU.mult)
        else:
            o = self.O[j % 2]
            for c in range(cfg.DC):
                self.stt(o[:, c, :], self.X[:, c, sl], self.vcol(n.gname, c), r, ALU.mult, ALU.mult)
            self.out_final.append(self.dma("sp", f"out{j % 2}", ("o", n.seq, j),
                                           self.outT[n.seq][:, sl].rearrange("(c p) t -> p c t", p=128), o))
            if n.after is not None:
                self.load_x_tile(n.seq + 1, j)
                n.loadq.append(j)
        n.done.add(j)

    def load_x_tile(self, s, j):
        sl = slice(j * TT, (j + 1) * TT)
        self.dma("sp", f"xin{j}", ("x", s, j), self.X[:, :, sl],
                 self.xT[s][:, sl].rearrange("(c p) t -> p c t", p=128))

    def _ready(self, n, j):
        if n is None:
            return
        if self.h_free:
            self.flush_post(n)
            self.emit_pre(n, j)
        else:
            n.deferred.append(j)

    def x_ready(self, j):
        self._ready(self.nxt, j)

    def drain(self, n):
        if n is None:
            return
        assert self.h_free
        while n.deferred:
            self.flush_post(n)
            self.emit_pre(n, n.deferred.pop(0))

    def need_H(self, j):
        n = self.cur
        while j not in n.done:
            assert self.h_free
            if n.pending is not None:
                self.flush_post(n)
            elif n.deferred:
                self.emit_pre(n, n.deferred.pop(0))
            else:
                raise AssertionError(("H tile never produced", j))

    def rms_stats_sq(self, j):
        raise NotImplementedError

    def proj_residual(self, wout, src_of_kc, j, nk):
        cfg = self.cfg
        sl = slice(j * TT, (j + 1) * TT)
        for oc in range(cfg.DC):
            bank = self.rotA.next()
            for kc in range(nk):
                self.mm(bank, wout[:, kc, oc * 128:(oc + 1) * 128], src_of_kc(kc), kc == 0, kc == nk - 1)
            self.tt(self.X[:, oc, sl], self.X[:, oc, sl], bank, ALU.add)
        self.x_ready(j)

    def mlp(self, li):
        cfg = self.cfg
        DC, SL, NT = cfg.DC, cfg.SL, cfg.NT
        FCS = SL // 128
        nslab = cfg.FF // SL
        assert nslab >= 2
        flex = self.FLEX
        A = [self.view(flex + i * FCS * TT * 2, BF16, (FCS, TT)) for i in range(2)]
        toff = flex + 2 * FCS * TT * 2
        Tr = Rot([self.view(toff + i * TT * 4, F32, (TT,)) for i in range(3)])
        steps = [(s, j) for s in range(cfg.FF // SL) for j in range(NT)]
        slabs = {}

        def up(i):
            s, j = steps[i]
            if s not in slabs:
                slabs[s] = self.w_next()
            wu = slabs[s][0]
            sl = slice(j * TT, (j + 1) * TT)
            a = A[i % 2]
            if s == 0:
                self.need_H(j)
            for fc in range(FCS):
                bank = self.rotU.next()
                for kc in range(DC):
                    self.mm(bank, wu[:, kc, fc * 128:(fc + 1) * 128], self.H[:, kc, sl], kc == 0, kc == DC - 1)
                t = Tr.next()
                self.act(t, bank, AF.Relu)
                self.tt(a[:, fc, :], t, t, ALU.mult)

        def down(i):
            s, j = steps[i]
            wd = slabs[s][1]
            sl = slice(j * TT, (j + 1) * TT)
            a = A[i % 2]
            for dc in range(DC):
                bank = self.rotD.next()
                for fc in range(FCS):
                    self.mm(bank, wd[:, fc, dc * 128:(dc + 1) * 128], a[:, fc, :], fc == 0, fc == FCS - 1)
                self.tt(self.X[:, dc, sl], self.X[:, dc, sl], bank, ALU.add)
            if s == nslab - 1:
                self.x_ready(j)
            if j == NT - 1:
                self.w_release()
        up(0)
        for i in range(len(steps)):
            if i + 1 < len(steps):
                up(i + 1)
            down(i)

    def pool_layer(self, li):
        cfg = self.cfg
        DC, GC, DG, S, NT = cfg.DC, cfg.GC, cfg.DG, cfg.S, cfg.NT
        L = S + 16
        off = self.FLEX
        Ub = []
        for _i in range(2):
            Ub.append(self.view(off, F32, (L,))); off = al(off + L * 4)
        T = self.view(off, F32, (L,)); off = al(off + L * 4)
        E1 = self.view(off, F32, (8,)); off = al(off + 32)
        Pb = []
        for _i in range(2):
            Pb.append(self.view(off, BF16, (GC, S))); off += GC * S * 2
        Y = self.view(off, BF16, (DC, S)); off += DC * S * 2
        assert off <= self.FLEX_END, (off, self.FLEX_END)
        for u in Ub:
            self.memset(u[:, 0:8], 0.0)
            self.memset(u[:, 8 + S:L], 0.0)
        eo = self.vlay["edges"]
        nch = 4 * GC
        slabs = {}

        def stageA(ci, tiles=None):
            g, oc = divmod(ci, GC)
            if g not in slabs:
                slabs[g] = self.w_next()
            win = slabs[g][0]
            u = Ub[ci % 2]
            for j in (range(NT) if tiles is None else tiles):
                self.need_H(j)
                bank = self.rotA.next()
                for kc in range(DC):
                    self.mm(bank, win[:, kc, oc * 128:(oc + 1) * 128], self.H[:, kc, j * TT:(j + 1) * TT],
                            kc == 0, kc == DC - 1)
                self.act(u[:, 8 + j * TT:8 + (j + 1) * TT], bank, AF.Copy)

        def stageB(ci):
            g, oc = divmod(ci, GC)
            w = POOL_WINDOWS[g]
            half = w // 2
            Uc = Ub[ci % 2]
            Pg = Pb[g % 2]
            self.tt(T[:, 0:L - 1], Uc[:, 0:L - 1], Uc[:, 1:L], ALU.add)
            cur = 2
            while cur < w:
                n = L - 2 * cur + 1
                self.tt(T[:, 0:n], T[:, 0:n], T[:, cur:cur + n], ALU.add)
                cur *= 2
            self.stt(Pg[:, oc, :], T[:, 8 - half:8 - half + S], 1.0 / w, Uc[:, 8:8 + S], ALU.mult, ALU.subtract)
            el = self.VEC[:, eo + (g * 2) * 8: eo + (g * 2) * 8 + half]
            self.tt(E1[:, 0:half], T[:, 8 - half:8], el, ALU.mult)
            self.tt(Pg[:, oc, 0:half], E1[:, 0:half], Uc[:, 8:8 + half], ALU.subtract)
            nr = half - 1
            if nr > 0:
                i0 = S - half + 1
                er = self.VEC[:, eo + (g * 2 + 1) * 8: eo + (g * 2 + 1) * 8 + nr]
                self.tt(E1[:, 0:nr], T[:, i0 + 8 - half:i0 + 8 - half + nr], er, ALU.mult)
                self.tt(Pg[:, oc, i0:S], E1[:, 0:nr], Uc[:, 8 + i0:8 + S], ALU.subtract)

        def stageC(g):
            wg = slabs[g][1]
            Pg = Pb[g % 2]
            for oc2 in range(GC):
                for j in range(NT):
                    bank = self.rotA.next()
                    for kc in range(GC):
                        self.mm(bank, wg[:, kc, oc2 * 128:(oc2 + 1) * 128], Pg[:, kc, j * TT:(j + 1) * TT],
                                kc == 0, kc == GC - 1)
                    self.act(Y[:, g * GC + oc2, j * TT:(j + 1) * TT], bank, AF.Copy,
                             scale=self.vcol(f"l{li}_pool_scale", g * GC + oc2))
            self.w_release()

        stageA(0, range(NT - 1))
        stageA(1)
        stageA(0, [NT - 1])
        for ci in range(nch):
            stageB(ci)
            if ci % GC == GC - 1:
                stageC(ci // GC)
            if ci + 2 < nch:
                stageA(ci + 2)
        (wout,) = self.w_next()
        for j in range(NT):
            self.proj_residual(wout, lambda kc, j=j: Y[:, kc, j * TT:(j + 1) * TT], j, DC)
        self.w_release()

    def conv_layer(self, li):
        cfg = self.cfg
        DC, S, NT, D = cfg.DC, cfg.S, cfg.NT, cfg.D
        VL = S + 2 * CONV_PAD
        hbytes = DC * S * 2
        vbytes = al(VL * 2)
        if hbytes + (DC - 1) * vbytes > DC * S * 4:
            vbytes = VL * 2
        assert hbytes + (DC - 1) * vbytes <= DC * S * 4
        base = self.H_OFF
        cend = base + DC * S * 4
        cbytes = S * 4

        def voff(c):
            return base + hbytes + c * vbytes if c < DC - 1 else cend

        def Vc(c):
            return self.view(voff(c), BF16, (VL,))
        for c in range(DC):
            for k in range(DC):
                lo, hi = max(base + c * cbytes, voff(k)), min(base + (c + 1) * cbytes, voff(k) + vbytes)
                assert lo >= hi or k < c, (c, k)
        C = self.view(base, F32, (DC, S))
        off = al(cend + vbytes)
        DGb = []
        for _i in range(2):
            DGb.append(self.view(off, BF16, (CONV_W, 128))); off = al(off + CONV_W * 128 * 2)
        lnoff = al(cend + vbytes)
        SG = Rot([self.view(off + i * TT * 4, F32, (TT,)) for i in range(2)]); off += 2 * TT * 4
        assert off <= self.FLEX_END
        NLT = 10
        LT = [self.view(lnoff + i * TT * 4, F32, (TT,)) for i in range(NLT)]; lnoff += NLT * TT * 4
        SLd = []
        for _i in range(2):
            SLd.append(self.view(lnoff, BF16, (DC, TT))); lnoff += DC * TT * 2
        assert lnoff <= self.FLEX_END, (lnoff, self.FLEX_END)
        for c in range(DC):
            v = Vc(c)
            self.memset(v[:, 0:CONV_PAD], 0.0)
            self.memset(v[:, CONV_PAD + S:VL], 0.0)
        gslabs = {}

        def glu(c, tiles):
            if c not in gslabs:
                gslabs[c] = self.w_next()
            wv, wgt = gslabs[c]
            v = Vc(c)
            for j in tiles:
                sl = slice(j * TT, (j + 1) * TT)
                self.need_H(j)
                bv = self.rotA.next()
                bg = self.rotA.next()
                for kc in range(DC):
                    self.mm(bv, wv[:, kc, :], self.H[:, kc, sl], kc == 0, kc == DC - 1)
                for kc in range(DC):
                    self.mm(bg, wgt[:, kc, :], self.H[:, kc, sl], kc == 0, kc == DC - 1)
                sg = SG.next()
                self.act(sg, bg, AF.Sigmoid)
                self.tt(v[:, CONV_PAD + j * TT:CONV_PAD + (j + 1) * TT], bv, sg, ALU.mult)
        glu(0, range(NT - 1))
        glu(1, range(NT))
        glu(0, [NT - 1])
        self.w_release()
        self.w_release()
        for c in range(2, DC):
            glu(c, range(NT))
            self.w_release()
        self.h_free = False
        dwo = self.vlay[f"l{li}_conv_dw"]
        for c in range(DC):
            dwc = self.VEC[:, dwo + c * CONV_W: dwo + (c + 1) * CONV_W]
            DGm = DGb[c % 2]
            self.tt(DGm, self.IDENT.unsqueeze(1).broadcast_to([128, CONV_W, 128]),
                    dwc.unsqueeze(2).broadcast_to([128, CONV_W, 128]), ALU.mult)
            v = Vc(c)
            for j in range(NT):
                bank = self.rotA.next()
                for k in range(CONV_W):
                    self.mm(bank, DGm[:, k, :], v[:, j * TT + k:j * TT + k + TT], k == 0, k == CONV_W - 1)
                self.act(C[:, c, j * TT:(j + 1) * TT], bank, AF.Identity, bias=self.vcol(f"l{li}_conv_dw_bias", c))
        (wout,) = self.w_next()
        MEANb, RSTDb, MSQ = LT[0:3], LT[3:6], LT[6]
        T1 = Rot(LT[7:10])
        sbanks = {}

        def stats_chunk(j, c, copy_eng="act"):
            sl = slice(j * TT, (j + 1) * TT)
            if c == 0:
                sbanks[j] = (self.rotS.next(), self.rotS.next())
            b1, b2 = sbanks[j]
            cb = self.SQ.next()
            self.copy(cb, C[:, c, sl], eng=copy_eng)
            cs = self.SQ.next()
            self.act(cs, C[:, c, sl], AF.Square)
            self.mm(b1, self.ONES, cb, c == 0, c == DC - 1)
            self.mm(b2, self.ONES, cs, c == 0, c == DC - 1)

        def lnmath(j):
            b1, b2 = sbanks[j]
            MEAN, RSTD = MEANb[j % 3], RSTDb[j % 3]
            self.ts(MEAN, b1, 1.0 / D, ALU.mult)
            self.tt(MSQ, MEAN, MEAN, ALU.mult)
            self.stt(RSTD, b2, 1.0 / D, MSQ, ALU.mult, ALU.subtract)
            self.act(RSTD, RSTD, AF.Sqrt, bias=self.vcol("eps", 1))
            self.recip(RSTD, RSTD)

        def normalize(j):
            sl = slice(j * TT, (j + 1) * TT)
            MEAN, RSTD = MEANb[j % 3], RSTDb[j % 3]
            SLb = SLd[j % 2]
            for c in range(DC):
                t1 = T1.next()
                self.tt(t1, C[:, c, sl], MEAN, ALU.subtract)
                self.tt(t1, t1, RSTD, ALU.mult)
                self.act(SLb[:, c, :], t1, AF.Silu, bias=self.vcol(f"l{li}_conv_ln_b", c),
                         scale=self.vcol(f"l{li}_conv_ln_g", c))

        for c in range(DC):
            stats_chunk(0, c, copy_eng="dve")
        lnmath(0)
        if NT > 1:
            for c in range(DC):
                stats_chunk(1, c, copy_eng="dve")
            lnmath(1)
        normalize(0)
        for j in range(NT):
            sl = slice(j * TT, (j + 1) * TT)
            SLb = SLd[j % 2]
            for oc in range(DC):
                bank = self.rotA.next()
                for kc in range(DC):
                    self.mm(bank, wout[:, kc, oc * 128:(oc + 1) * 128], SLb[:, kc, :], kc == 0, kc == DC - 1)
                self.tt(self.X[:, oc, sl], self.X[:, oc, sl], bank, ALU.add)
                if j + 2 < NT:
                    stats_chunk(j + 2, oc)
            self.x_ready(j)
            if j + 2 < NT:
                lnmath(j + 2)
            if j + 1 < NT:
                normalize(j + 1)
        self.w_release()
        self.h_free = True
        self.drain(self.nxt)

    def four_layer(self, li):
        cfg = self.cfg
        DC, GC, DG, S, NT, TC = cfg.DC, cfg.GC, cfg.DG, cfg.S, cfg.NT, cfg.TC
        H2 = TC // 2
        abrow = 4 * 2 * DG
        ab1 = self.H_OFF
        off = self.FLEX
        ab2 = off; off += H2 * abrow * 2
        ut = off; off += DC * S * 2
        fj = off; off += DC * TT * 2
        UT = self.view(ut, BF16, (DC, S))
        Fj = self.view(fj, BF16, (DC, TT))

        def AB(tc):
            o = (ab1 if tc < H2 else ab2) + (tc % H2) * abrow * 2
            return self.view(o, BF16, (4, 2 * DG))
        tabs = [self.view(ut + h * (2 * H2 * TT * 2), BF16, (2, H2, TT)) for h in range(2)]
        assert 2 * (2 * H2 * TT * 2) <= DC * S * 2
        (win,) = self.w_next()

        def proj(oc, tiles):
            for j in tiles:
                sl = slice(j * TT, (j + 1) * TT)
                self.need_H(j)
                bank = self.rotA.next()
                for kc in range(DC):
                    self.mm(bank, win[:, kc, oc * 128:(oc + 1) * 128], self.H[:, kc, sl], kc == 0, kc == DC - 1)
                self.copy(UT[:, oc, sl], bank, eng="act")
        proj(0, range(NT - 1))
        proj(1, range(NT))
        proj(0, [NT - 1])
        for oc in range(2, DC):
            proj(oc, range(NT))
        self.w_release()
        self.h_free = False
        n = 0
        for tc in range(TC):
            ab = AB(tc)
            for g in range(4):
                bank = self.rotA.next()
                for kc in range(GC):
                    self.mm(bank[:, 0:2 * DG], UT[:, g * GC + kc, tc * 128:(tc + 1) * 128], self.CSC[:, kc, :],
                            kc == 0, kc == GC - 1)
                self.copy(ab[:, g, :], bank[:, 0:2 * DG], eng=("act" if n % 2 == 0 else "dve"))
                n += 1
        (wout,) = self.w_next()
        scale = 1.0 / math.sqrt(S)
        for j in range(NT):
            for h in range(2):
                self.dma("sp", f"t{h}", (li, self.seq_i, j), tabs[h], self.dfts[j, h])
            for ch in range(DC):
                g, hh = ch // GC, ch % GC
                bank = self.rotA.next()
                k = 0
                for tc in range(TC):
                    ab = AB(tc)
                    tab = tabs[tc // H2]
                    for cs_i in range(2):
                        col = cs_i * DG + hh * 128
                        self.mm(bank, ab[:, g, col:col + 128], tab[:, cs_i, tc % H2, :], k == 0, k == 2 * TC - 1)
                        k += 1
                self.act(Fj[:, ch, :], bank, AF.Copy, scale=scale)
            self.proj_residual(wout, lambda kc: Fj[:, kc, :], j, DC)
        self.w_release()
        self.h_free = True
        self.drain(self.nxt)

    def finish(self, s):
        cfg = self.cfg
        if not cfg.final:
            self.out_final.append(self.dma("sp", "out0", ("o", s), self.outT[s].rearrange("(c p) t -> p c t", p=128),
                                           self.X))
            return
        for j in range(cfg.NT):
            self.need_H(j)
        n = self.cur
        while n.loadq:
            self._ready(n.after, n.loadq.pop(0))

    def build(self):
        cfg = self.cfg
        nc = self.nc
        DC, S, D = cfg.DC, cfg.S, cfg.D
        lay, nv = vec_layout(cfg)
        self.vlay = lay
        xT = nc.dram_tensor("xT", [cfg.NSEQ, D, S], F32, kind="ExternalInput").ap()
        vecs = nc.dram_tensor("vecs", [128, nv], F32, kind="ExternalInput").ap()
        cmat = nc.dram_tensor("cmat", [128, 256], BF16, kind="ExternalInput").ap()
        has_four = "four" in cfg.kinds
        if has_four:
            dftc = nc.dram_tensor("dftc", [128, cfg.GC, 2 * cfg.DG], BF16, kind="ExternalInput").ap()
            self.dfts = nc.dram_tensor("dfts", [cfg.NT, 2, 128, 2, cfg.TC // 2, TT], BF16, kind="ExternalInput").ap()
        self.wdram = {}
        for name, shape in weight_names(cfg):
            self.wdram[name] = nc.dram_tensor(name, list(shape), F32, kind="ExternalInput").ap()
        self.outT = nc.dram_tensor("outT", [cfg.NSEQ, D, S], F32, kind="ExternalOutput").ap()

        off = 0
        self.X_OFF = off; off += DC * S * 4
        self.H_OFF = off; off += DC * S * 2
        nvb = (nv * 4 + 3) // 4 * 4
        small = nvb + 256 * 2 + (cfg.GC * 2 * cfg.DG * 2 if has_four else 0) + 4 * TT * 2 + 2 * TT * 4
        self.W_SLOT = max(DC * cfg.SL * 2 + (cfg.SL // 128) * D * 2, DC * D * 2)
        self.W_SLOT = (self.W_SLOT + 255) // 256 * 256
        self.SM_OFF = (cfg.arena_bytes - small) // 256 * 256
        self.W_OFF = self.SM_OFF - 2 * self.W_SLOT
        self.FLEX = off
        self.FLEX_END = self.W_OFF
        assert self.FLEX_END > self.FLEX
        real_view = self.view

        with nc.allow_low_precision("bf16 matmul operands, fp32 accumulation"), \
                nc.sbuf_tensor("arena", [128, cfg.arena_bytes // 4], F32) as arena, \
                nc.psum_tensor("ps", [128, 8, TT], F32) as ps:
            self.arena = arena

            def flexview(off_, dtype, dims):
                return real_view(off_, dtype, dims)
            so = self.SM_OFF
            self.VEC = self.view(so, F32, (nv,)); so += nvb
            CM = self.view(so, BF16, (256,)); so += 512
            self.ONES = CM[:, 0:128]
            self.IDENT = CM[:, 128:256]
            if has_four:
                self.CSC = self.view(so, BF16, (cfg.GC, 2 * cfg.DG)); so += cfg.GC * 2 * cfg.DG * 2
            self.SQ = Rot([self.view(so + i * TT * 2, BF16, (TT,)) for i in range(4)]); so += 4 * TT * 2
            self.RS = Rot([self.view(so + i * TT * 4, F32, (TT,)) for i in range(2)]); so += 2 * TT * 4
            assert so <= cfg.arena_bytes
            self.X = self.view(self.X_OFF, F32, (DC, S))
            self.H = self.view(self.H_OFF, BF16, (DC, S))
            banks = [ps[:, i, :] for i in range(8)]
            self.rotS = Rot(banks[0:2])
            self.rotA = Rot(banks[2:8])
            self.rotU = Rot(banks[2:5])
            self.rotD = Rot(banks[5:8])

            self.dma("sp", "c0", "c", self.VEC, vecs)
            self.dma("sp", "c0", "c", CM, cmat)
            if has_four:
                self.dma("sp", "c0", "c", self.CSC, dftc)

            self.plan = self.slab_plan()
            self.w_issued = self.w_done = self.w_cur = 0
            self.xT = xT
            self.h_free = True
            ooff = self.FLEX + 2 * (cfg.SL // 128) * TT * 2 + 3 * TT * 4
            self.O = [self.view(ooff + i * DC * TT * 4, F32, (DC, TT)) for i in range(2)]
            assert ooff + 2 * DC * TT * 4 <= self.FLEX_END
            NS = Builder.NormState
            norms = []
            for s in range(cfg.NSEQ):
                row = []
                for kind, li in zip(cfg.kinds, cfg.layer_ids):
                    row.append(NS(f"l{li}_norm_mix", seq=s))
                    row.append(NS(f"l{li}_norm_mlp", seq=s))
                if cfg.final:
                    row.append(NS("final_norm", final=True, seq=s))
                else:
                    row.append(None)
                norms.append(row)
            if cfg.final:
                for s in range(cfg.NSEQ - 1):
                    norms[s][-1].after = norms[s + 1][0]
            self.cur = self.nxt = None
            for s in range(cfg.NSEQ):
                self.seq_i = s
                row = norms[s]
                if s == 0 or not cfg.final:
                    for j in range(cfg.NT):
                        self.load_x_tile(s, j)
                        self._ready(row[0], j)
                k = 0
                for kind, li in zip(cfg.kinds, cfg.layer_ids):
                    self.cur, self.nxt = row[k], row[k + 1]
                    if kind == "pool":
                        self.pool_layer(li)
                    elif kind == "conv":
                        self.conv_layer(li)
                    else:
                        self.four_layer(li)
                    k += 1
                    self.cur, self.nxt = row[k], row[k + 1]
                    self.mlp(li)
                    k += 1
                self.cur, self.nxt = row[k], None
                self.finish(s)
            assert self.w_cur == len(self.plan)

            P = self.P
            P.finalize()
            finals = {}
            for op in self.out_final:
                key = ("c", op.chan)
                finals[key] = max(finals.get(key, 0), op.gend)
            keys = P.sem_keys()
            import contextlib
            with contextlib.ExitStack() as es:
                sems = {}
                for k in keys:
                    sems[k] = es.enter_context(nc.semaphore(f"s_{k[0]}_{k[1]}"))
                block = es.enter_context(nc.Block())
                P.emit(block, sems, list(finals.items()))
        return nc


_KINDS = ("pool", "conv", "four", "pool")


def make_in_maps(cfg, inputs, x_shards):
    vec = pack_vecs(cfg, inputs)
    cmat, csc, tab = const_tables(cfg)
    base = {"vecs": vec, "cmat": cmat}
    if "four" in cfg.kinds:
        base["dftc"] = csc
        base["dfts"] = tab
    for name, _shape in weight_names(cfg):
        base[name] = np.ascontiguousarray(np.asarray(inputs[name], np.float32))
    maps = []
    for xs in x_shards:
        m = dict(base)
        m["xT"] = xs
        maps.append(m)
    return maps


def kernel(**inputs):
    x = np.asarray(inputs["x"], np.float32)
    B, S, D = x.shape
    ncores = 8
    nseq = B // ncores
    cfg = Cfg(S=S, D=D, FF=4 * D, NSEQ=nseq, kinds=_KINDS, final=True)
    xT = np.ascontiguousarray(x.transpose(0, 2, 1))
    shards = [xT[i * nseq:(i + 1) * nseq] for i in range(ncores)]
    nc = Builder(cfg).build()
    in_maps = make_in_maps(cfg, inputs, shards)
    res = run_bass_kernel_spmd(nc, in_maps, core_ids=list(range(ncores)))
    outT = np.concatenate([np.asarray(r["outT"]) for r in res.results], axis=0)
    return np.ascontiguousarray(outT.transpose(0, 2, 1)).astype(np.float32)
```

```python
import math
import numpy as np
import ml_dtypes
import concourse.bass as bass
import concourse.mybir as mybir
from concourse.bass_utils import run_bass_kernel_spmd

F32 = mybir.dt.float32
BF16 = mybir.dt.bfloat16
AF = mybir.ActivationFunctionType
ALU = mybir.AluOpType

POOL_WINDOWS = (2, 4, 8, 16)
CONV_W = 31
CONV_PAD = 15
NORM_EPS = 1e-6
LN_EPS = 1e-5
TT = 512
GRAN = 256


class Cfg:
    def __init__(self, S=2048, D=1024, FF=4096, NSEQ=2, kinds=("pool", "conv", "four", "pool"),
                 layer_ids=None, final=True, SL=512, arena_bytes=220160, dma_scratch=8192):
        self.S, self.D, self.FF, self.NSEQ = S, D, FF, NSEQ
        self.kinds = tuple(kinds)
        self.layer_ids = tuple(layer_ids) if layer_ids is not None else tuple(range(len(kinds)))
        self.final = final
        self.DC = D // 128
        self.GC = self.DC // 4
        self.DG = D // 4
        self.NT = S // TT
        self.TC = S // 128
        self.SL = SL
        self.arena_bytes = arena_bytes
        self.dma_scratch = dma_scratch
        assert D % 512 == 0 and S % TT == 0 and FF % SL == 0 and self.TC % 2 == 0


def vec_layout(cfg):
    lay, off = {}, 0

    def put(name, n):
        nonlocal off
        lay[name] = off
        off += n
    DC = cfg.DC
    for kind, li in zip(cfg.kinds, cfg.layer_ids):
        put(f"l{li}_norm_mix", DC)
        put(f"l{li}_norm_mlp", DC)
        if kind == "pool":
            put(f"l{li}_pool_scale", DC)
        if kind == "conv":
            put(f"l{li}_conv_dw_bias", DC)
            put(f"l{li}_conv_ln_g", DC)
            put(f"l{li}_conv_ln_b", DC)
            put(f"l{li}_conv_dw", DC * CONV_W)
    put("final_norm", DC)
    put("edges", 4 * 2 * 8)
    put("eps", 2)
    return lay, off


def weight_names(cfg):
    names = []
    for kind, li in zip(cfg.kinds, cfg.layer_ids):
        if kind == "pool":
            names += [(f"l{li}_pool_w_in", (cfg.D, cfg.D)), (f"l{li}_pool_w_group", (4, cfg.DG, cfg.DG)),
                      (f"l{li}_pool_w_out", (cfg.D, cfg.D))]
        elif kind == "conv":
            names += [(f"l{li}_conv_w_in", (cfg.D, 2 * cfg.D)), (f"l{li}_conv_w_out", (cfg.D, cfg.D))]
        else:
            names += [(f"l{li}_fourier_w_in", (cfg.D, cfg.D)), (f"l{li}_fourier_w_out_perm", (cfg.D, cfg.D)),
                      (f"l{li}_fourier_w_out_x", (128, cfg.D))]
        names += [(f"l{li}_mlp_up", (cfg.D, cfg.FF)), (f"l{li}_mlp_down", (cfg.FF, cfg.D))]
    return names


def pack_vecs(cfg, inputs):
    lay, nv = vec_layout(cfg)
    DC = cfg.DC
    V = np.zeros((128, nv), np.float32)

    def colmajor(v):
        return np.asarray(v, np.float32).reshape(DC, 128).T
    for name, off in lay.items():
        if name == "edges":
            for wi, w in enumerate(POOL_WINDOWS):
                h = w // 2
                for i in range(h):
                    V[:, off + (wi * 2 + 0) * 8 + i] = 1.0 / (i + h)
                for m in range(h - 1):
                    i = cfg.S - h + 1 + m
                    V[:, off + (wi * 2 + 1) * 8 + m] = 1.0 / (cfg.S - i + h)
        elif name == "eps":
            V[:, off] = NORM_EPS
            V[:, off + 1] = LN_EPS
        elif name.endswith("conv_dw"):
            dw = np.asarray(inputs[name], np.float32)
            V[:, off:off + DC * CONV_W] = dw.T.reshape(DC, 128, CONV_W).transpose(1, 0, 2).reshape(128, DC * CONV_W)
        else:
            V[:, off:off + DC] = colmajor(inputs[name])
    return V


def const_tables(cfg):
    bf = ml_dtypes.bfloat16
    cmat = np.zeros((128, 256), np.float32)
    cmat[:, 0:128] = 1.0
    cmat[:, 128:256] = np.eye(128, dtype=np.float32)
    DG, GC, S, TC, NT = cfg.DG, cfg.GC, cfg.S, cfg.TC, cfg.NT
    a = np.arange(DG, dtype=np.float64)
    hk = DG // 2
    k = np.arange(hk, dtype=np.float64)
    ang = 2.0 * np.pi * np.outer(a, k) / DG
    csc = np.zeros((DG, 2 * hk + 2), np.float64)
    csc[:, 0:hk] = np.cos(ang)
    csc[:, hk:2 * hk] = np.sin(ang)
    csc[:, 2 * hk] = np.cos(np.pi * a)
    csc /= math.sqrt(DG)
    csc = csc.reshape(GC, 128, 2 * hk + 2).transpose(1, 0, 2)
    n = np.arange(S, dtype=np.int64)
    prod = np.outer(n, n) % S
    ang2 = 2.0 * np.pi * prod.astype(np.float64) / S
    cs = np.cos(ang2)
    ns = np.sin(ang2)
    H2 = TC // 2
    tab = np.stack([cs, ns], axis=0)
    tab = tab.reshape(2, 2, H2, 128, NT, TT)
    tab = tab.transpose(4, 1, 3, 0, 2, 5)
    return (cmat.astype(bf), np.ascontiguousarray(csc).astype(bf),
            np.ascontiguousarray(tab).astype(bf))


class Op:
    __slots__ = ("eng", "build", "deps", "signal", "count", "chan", "group", "seq", "waits", "gend")


class Prog:
    ENG = ("pe", "act", "dve", "pool", "sp")

    def __init__(self):
        self.ops = {e: [] for e in self.ENG}
        self.lastw = {}
        self.readers = {}
        self.chan_ops = {}
        self.gcache = {}
        self.seq = 0

    def gran(self, ap):
        if str(ap.space) == "DRAM":
            return ()
        dims = ap.ap
        key = (ap.tensor.name, ap.offset, dims, str(ap.dtype))
        g = self.gcache.get(key)
        if g is not None:
            return g
        esz = 2 if ap.dtype == BF16 else 4
        pstep = dims[0][0]
        off = ap.offset % pstep if pstep > 0 else ap.offset
        free = [d for d in dims[1:] if d[1] > 1 and d[0] != 0]
        if not free:
            free = [(1, 1)]
        starts = np.array([off], dtype=np.int64)
        for step, cnt in free[:-1]:
            starts = (starts[:, None] + np.arange(cnt, dtype=np.int64)[None, :] * step).ravel()
        step_l, cnt_l = free[-1]
        lo = starts * esz
        hi = (starts + (cnt_l - 1) * abs(step_l) + 1) * esz
        name = ap.tensor.name
        s = set()
        for l, h in zip(lo.tolist(), hi.tolist()):
            for gi in range(l // GRAN, (h - 1) // GRAN + 1):
                s.add((name, gi))
        g = tuple(s)
        self.gcache[key] = g
        return g

    def add(self, eng, build, reads=(), writes=(), chan=None, group=None):
        op = Op()
        op.eng, op.build, op.chan, op.group, op.signal = eng, build, chan, group, False
        op.seq = self.seq
        self.seq += 1
        deps = {}

        def dep(d):
            if d is None:
                return
            if d.chan is None and d.eng == "pe" and eng == "pe" and chan is None:
                return
            if chan is not None and d.chan == chan and d.group == group:
                return
            key = ("c", d.chan) if d.chan else ("e", d.eng)
            cur = deps.get(key)
            if cur is None or d.seq > cur.seq:
                deps[key] = d
        rg = set()
        for ap in reads:
            rg.update(self.gran(ap))
        wg = set()
        for ap in writes:
            wg.update(self.gran(ap))
        for g in rg:
            dep(self.lastw.get(g))
        for g in wg:
            dep(self.lastw.get(g))
            rd = self.readers.get(g)
            if rd:
                for r in rd.values():
                    dep(r)
        mykey = ("c", chan) if chan else ("e", eng)
        for g in rg:
            if g not in wg:
                self.readers.setdefault(g, {})[mykey] = op
        for g in wg:
            self.lastw[g] = op
            self.readers[g] = {}
        op.deps = list(deps.values())
        for d in op.deps:
            d.signal = True
        self.ops[eng].append(op)
        if chan:
            self.chan_ops.setdefault(chan, []).append(op)
        return op

    def finalize(self):
        for e in self.ENG:
            c = 0
            for op in self.ops[e]:
                if op.chan is None:
                    if op.signal:
                        c += 1
                    op.count = c
        for ch, lst in self.chan_ops.items():
            c = 0
            ends = {}
            for op in lst:
                c += 16
                op.count = c
                ends[op.group] = c
            for op in lst:
                op.gend = ends[op.group]
        for e in self.ENG:
            waited = {}
            for op in self.ops[e]:
                w = []
                for d in op.deps:
                    if d.chan:
                        key, val = ("c", d.chan), d.gend
                    else:
                        key, val = ("e", d.eng), d.count
                    if waited.get(key, 0) < val:
                        waited[key] = val
                        w.append((key, val))
                op.waits = w

    def sem_keys(self):
        keys = [("e", e) for e in self.ENG if any(o.chan is None for o in self.ops[e])]
        keys += [("c", ch) for ch in self.chan_ops]
        return keys

    def emit(self, block, sems, final_waits):
        attr = {"pe": "tensor", "act": "scalar", "dve": "vector", "pool": "gpsimd", "sp": "sync"}
        for e in self.ENG:
            ops = self.ops[e]
            if not ops:
                continue

            def body(eng, ops=ops, e=e):
                for op in ops:
                    for key, val in op.waits:
                        eng.wait_ge(sems[key], val)
                    ins = op.build(eng)
                    if op.chan:
                        ins.then_inc(sems[("c", op.chan)], 16)
                    elif op.signal:
                        ins.then_inc(sems[("e", e)], 1)
                if e == "sp":
                    for key, val in final_waits:
                        eng.wait_ge(sems[key], val)
            getattr(block, attr[e])(body)


def al(x, a=GRAN):
    return (x + a - 1) // a * a


class Rot:
    def __init__(self, items):
        self.items = list(items)
        self.i = 0

    def next(self):
        v = self.items[self.i % len(self.items)]
        self.i += 1
        return v


class Builder:
    def __init__(self, cfg):
        self.cfg = cfg
        self.P = Prog()
        self.nc = bass.Bass("TRN2", target_bir_lowering=False, dynamic_dma_scratch_size=cfg.dma_scratch)
        self.out_final = []

    def mm(self, out, lhsT, rhs, start, stop):
        self.P.add("pe", lambda e: e.matmul(out, lhsT, rhs, start=start, stop=stop),
                   reads=[lhsT, rhs], writes=[out])

    def act(self, out, in_, func, bias=None, scale=None, eng="act"):
        reads = [in_]
        kw = {}
        if bias is not None:
            kw["bias"] = bias
            if not isinstance(bias, (int, float)):
                reads.append(bias)
        if scale is not None:
            kw["scale"] = scale
            if not isinstance(scale, (int, float)):
                reads.append(scale)
        self.P.add(eng, lambda e: e.activation(out, in_, func, **kw), reads=reads, writes=[out])

    def tt(self, out, in0, in1, op, eng="dve"):
        self.P.add(eng, lambda e: e.tensor_tensor(out, in0, in1, op), reads=[in0, in1], writes=[out])

    def ts(self, out, in0, s1, op0, s2=None, op1=None, eng="dve"):
        reads = [in0] + [s for s in (s1, s2) if s is not None and not isinstance(s, (int, float))]
        if op1 is None:
            self.P.add(eng, lambda e: e.tensor_scalar(out, in0, s1, None, op0), reads=reads, writes=[out])
        else:
            self.P.add(eng, lambda e: e.tensor_scalar(out, in0, s1, s2, op0, op1), reads=reads, writes=[out])

    def stt(self, out, in0, scalar, in1, op0, op1, eng="dve"):
        reads = [in0, in1] + ([] if isinstance(scalar, (int, float)) else [scalar])
        self.P.add(eng, lambda e: e.scalar_tensor_tensor(out, in0, scalar, in1, op0, op1),
                   reads=reads, writes=[out])

    def copy(self, out, in_, eng="dve"):
        if eng == "act":
            self.act(out, in_, AF.Copy)
        else:
            self.P.add(eng, lambda e: e.tensor_copy(out, in_), reads=[in_], writes=[out])

    def recip(self, out, in_):
        self.P.add("dve", lambda e: e.reciprocal(out, in_), reads=[in_], writes=[out])

    def memset(self, ap, val, eng="dve"):
        self.P.add(eng, lambda e: e.memset(ap, val), writes=[ap])

    def dma(self, q, chan, group, out, in_, **kw):
        return self.P.add(q, lambda e: e.dma_start(out=out, in_=in_, **kw), reads=[in_], writes=[out],
                          chan=chan, group=group)

    def view(self, off, dtype, dims):
        esz = 2 if dtype == BF16 else 4
        n = 1
        for d in dims:
            n *= d
        nb = n * esz
        assert off % 4 == 0 and nb % 4 == 0, (off, nb)
        assert off + nb <= self.cfg.arena_bytes, ("arena overflow", off, nb, self.cfg.arena_bytes)
        ap = self.arena[:, off // 4:(off + nb) // 4]
        if dtype != F32:
            ap = ap.bitcast(dtype)
        if len(dims) == 2:
            ap = ap.rearrange("p (a b) -> p a b", a=dims[0])
        elif len(dims) == 3:
            ap = ap.rearrange("p (a b c) -> p a b c", a=dims[0], b=dims[1])
        return ap

    def slab_plan(self):
        cfg = self.cfg
        D, DC, DG, GC, FF, SL = cfg.D, cfg.DC, cfg.DG, cfg.GC, cfg.FF, cfg.SL
        W = self.wdram
        plan = []

        def rows(ap):
            return ap.rearrange("(kc p) n -> p kc n", p=128)
        for _s in range(cfg.NSEQ):
            for kind, li in zip(cfg.kinds, cfg.layer_ids):
                if kind == "pool":
                    win, wg, wo = W[f"l{li}_pool_w_in"], W[f"l{li}_pool_w_group"], W[f"l{li}_pool_w_out"]
                    for g in range(4):
                        plan.append([(0, (DC, DG), rows(win[:, g * DG:(g + 1) * DG])),
                                     (DC * DG * 2, (GC, DG), rows(wg[g]))])
                    plan.append([(0, (DC, D), rows(wo))])
                elif kind == "conv":
                    win, wo = W[f"l{li}_conv_w_in"], W[f"l{li}_conv_w_out"]
                    for c in range(DC):
                        plan.append([(0, (DC, 128), rows(win[:, c * 128:(c + 1) * 128])),
                                     (DC * 128 * 2, (DC, 128), rows(win[:, D + c * 128:D + (c + 1) * 128]))])
                    plan.append([(0, (DC, D), rows(wo))])
                else:
                    win, wo = W[f"l{li}_fourier_w_in"], W[f"l{li}_fourier_w_out_perm"]
                    plan.append([(0, (DC, D), rows(win))])
                    plan.append([(0, (DC, D), rows(wo))])
                up, dn = W[f"l{li}_mlp_up"], W[f"l{li}_mlp_down"]
                for s in range(FF // SL):
                    plan.append([(0, (DC, SL), rows(up[:, s * SL:(s + 1) * SL])),
                                 (DC * SL * 2, (SL // 128, D), rows(dn[s * SL:(s + 1) * SL, :]))])
        return plan

    def w_issue(self):
        while self.w_issued < len(self.plan) and self.w_issued < self.w_done + 2:
            k = self.w_issued
            slot = k % 2
            for (off, dims, src) in self.plan[k]:
                dst = self.view(self.W_OFF + slot * self.W_SLOT + off, BF16, dims)
                self.dma("pool", f"w{slot}", k, dst, src)
            self.w_issued += 1

    def w_next(self):
        k = self.w_cur
        self.w_cur += 1
        self.w_issue()
        assert k < self.w_issued
        slot = k % 2
        return [self.view(self.W_OFF + slot * self.W_SLOT + off, BF16, dims) for (off, dims, _src) in self.plan[k]]

    def w_release(self):
        self.w_done += 1
        self.w_issue()

    def vcol(self, name, c, n=1):
        o = self.vlay[name] + c
        return self.VEC[:, o:o + n]

    class NormState:
        def __init__(self, gname, final=False, seq=0):
            self.gname, self.final, self.seq = gname, final, seq
            self.pending = None
            self.deferred = []
            self.done = set()
            self.after = None
            self.loadq = []

    def emit_pre(self, n, j):
        sl = slice(j * TT, (j + 1) * TT)
        for c in range(self.cfg.DC):
            self.act(self.H[:, c, sl], self.X[:, c, sl], AF.Square)
        n.pending = j

    def flush_post(self, n):
        cfg = self.cfg
        j = n.pending
        if j is None:
            return
        n.pending = None
        if n.final and n.loadq:
            self._ready(n.after, n.loadq.pop(0))
        sl = slice(j * TT, (j + 1) * TT)
        bank = self.rotS.next()
        for c in range(cfg.DC):
            self.mm(bank, self.ONES, self.H[:, c, sl], c == 0, c == cfg.DC - 1)
        r = self.RS.next()
        self.act(r, bank, AF.Sqrt, bias=self.vcol("eps", 0), scale=1.0 / cfg.D)
        self.recip(r, r)
        if not n.final:
            for c in range(cfg.DC):
                self.stt(self.H[:, c, sl], self.X[:, c, sl], self.vcol(n.gname, c), r, ALU.mult, ALU.mult)
        else:
            o = self.O[j % 2]
            for c in range(cfg.DC):
                self.stt(o[:, c, :], self.X[:, c, sl], self.vcol(n.gname, c), r, ALU.mult, ALU.mult)
            self.out_final.append(self.dma("sp", f"out{j % 2}", ("o", n.seq, j),
                                           self.outT[n.seq][:, sl].rearrange("(c p) t -> p c t", p=128), o))
            if n.after is not None:
                self.load_x_tile(n.seq + 1, j)
                n.loadq.append(j)
        n.done.add(j)

    def load_x_tile(self, s, j):
        sl = slice(j * TT, (j + 1) * TT)
        self.dma("sp", f"xin{j}", ("x", s, j), self.X[:, :, sl],
                 self.xT[s][:, sl].rearrange("(c p) t -> p c t", p=128))

    def _ready(self, n, j):
        if n is None:
            return
        if self.h_free:
            self.flush_post(n)
            self.emit_pre(n, j)
        else:
            n.deferred.append(j)

    def x_ready(self, j):
        self._ready(self.nxt, j)

    def drain(self, n):
        if n is None:
            return
        assert self.h_free
        while n.deferred:
            self.flush_post(n)
            self.emit_pre(n, n.deferred.pop(0))

    def need_H(self, j):
        n = self.cur
        while j not in n.done:
            assert self.h_free
            if n.pending is not None:
                self.flush_post(n)
            elif n.deferred:
                self.emit_pre(n, n.deferred.pop(0))
            else:
                raise AssertionError(("H tile never produced", j))

    def rms_stats_sq(self, j):
        raise NotImplementedError

    def proj_residual(self, wout, src_of_kc, j, nk):
        cfg = self.cfg
        sl = slice(j * TT, (j + 1) * TT)
        for oc in range(cfg.DC):
            bank = self.rotA.next()
            for kc in range(nk):
                self.mm(bank, wout[:, kc, oc * 128:(oc + 1) * 128], src_of_kc(kc), kc == 0, kc == nk - 1)
            self.tt(self.X[:, oc, sl], self.X[:, oc, sl], bank, ALU.add)
        self.x_ready(j)

    def mlp(self, li):
        cfg = self.cfg
        DC, SL, NT = cfg.DC, cfg.SL, cfg.NT
        FCS = SL // 128
        nslab = cfg.FF // SL
        assert nslab >= 2
        flex = self.FLEX
        A = [self.view(flex + i * FCS * TT * 2, BF16, (FCS, TT)) for i in range(2)]
        toff = flex + 2 * FCS * TT * 2
        Tr = Rot([self.view(toff + i * TT * 4, F32, (TT,)) for i in range(3)])
        steps = [(s, j) for s in range(cfg.FF // SL) for j in range(NT)]
        slabs = {}

        def up(i):
            s, j = steps[i]
            if s not in slabs:
                slabs[s] = self.w_next()
            wu = slabs[s][0]
            sl = slice(j * TT, (j + 1) * TT)
            a = A[i % 2]
            if s == 0:
                self.need_H(j)
            for fc in range(FCS):
                bank = self.rotU.next()
                for kc in range(DC):
                    self.mm(bank, wu[:, kc, fc * 128:(fc + 1) * 128], self.H[:, kc, sl], kc == 0, kc == DC - 1)
                t = Tr.next()
                self.act(t, bank, AF.Relu)
                self.tt(a[:, fc, :], t, t, ALU.mult)

        def down(i):
            s, j = steps[i]
            wd = slabs[s][1]
            sl = slice(j * TT, (j + 1) * TT)
            a = A[i % 2]
            for dc in range(DC):
                bank = self.rotD.next()
                for fc in range(FCS):
                    self.mm(bank, wd[:, fc, dc * 128:(dc + 1) * 128], a[:, fc, :], fc == 0, fc == FCS - 1)
                self.tt(self.X[:, dc, sl], self.X[:, dc, sl], bank, ALU.add)
            if s == nslab - 1:
                self.x_ready(j)
            if j == NT - 1:
                self.w_release()
        up(0)
        for i in range(len(steps)):
            if i + 1 < len(steps):
                up(i + 1)
            down(i)

    def pool_layer(self, li):
        cfg = self.cfg
        DC, GC, DG, S, NT = cfg.DC, cfg.GC, cfg.DG, cfg.S, cfg.NT
        L = S + 16
        off = self.FLEX
        Ub = []
        for _i in range(2):
            Ub.append(self.view(off, F32, (L,))); off = al(off + L * 4)
        T = self.view(off, F32, (L,)); off = al(off + L * 4)
        E1 = self.view(off, F32, (8,)); off = al(off + 32)
        Pb = []
        for _i in range(2):
            Pb.append(self.view(off, BF16, (GC, S))); off += GC * S * 2
        Y = self.view(off, BF16, (DC, S)); off += DC * S * 2
        assert off <= self.FLEX_END, (off, self.FLEX_END)
        for u in Ub:
            self.memset(u[:, 0:8], 0.0)
            self.memset(u[:, 8 + S:L], 0.0)
        eo = self.vlay["edges"]
        nch = 4 * GC
        slabs = {}

        def stageA(ci, tiles=None):
            g, oc = divmod(ci, GC)
            if g not in slabs:
                slabs[g] = self.w_next()
            win = slabs[g][0]
            u = Ub[ci % 2]
            for j in (range(NT) if tiles is None else tiles):
                self.need_H(j)
                bank = self.rotA.next()
                for kc in range(DC):
                    self.mm(bank, win[:, kc, oc * 128:(oc + 1) * 128], self.H[:, kc, j * TT:(j + 1) * TT],
                            kc == 0, kc == DC - 1)
                self.act(u[:, 8 + j * TT:8 + (j + 1) * TT], bank, AF.Copy)

        def stageB(ci):
            g, oc = divmod(ci, GC)
            w = POOL_WINDOWS[g]
            half = w // 2
            Uc = Ub[ci % 2]
            Pg = Pb[g % 2]
            self.tt(T[:, 0:L - 1], Uc[:, 0:L - 1], Uc[:, 1:L], ALU.add)
            cur = 2
            while cur < w:
                n = L - 2 * cur + 1
                self.tt(T[:, 0:n], T[:, 0:n], T[:, cur:cur + n], ALU.add)
                cur *= 2
            self.stt(Pg[:, oc, :], T[:, 8 - half:8 - half + S], 1.0 / w, Uc[:, 8:8 + S], ALU.mult, ALU.subtract)
            el = self.VEC[:, eo + (g * 2) * 8: eo + (g * 2) * 8 + half]
            self.tt(E1[:, 0:half], T[:, 8 - half:8], el, ALU.mult)
            self.tt(Pg[:, oc, 0:half], E1[:, 0:half], Uc[:, 8:8 + half], ALU.subtract)
            nr = half - 1
            if nr > 0:
                i0 = S - half + 1
                er = self.VEC[:, eo + (g * 2 + 1) * 8: eo + (g * 2 + 1) * 8 + nr]
                self.tt(E1[:, 0:nr], T[:, i0 + 8 - half:i0 + 8 - half + nr], er, ALU.mult)
                self.tt(Pg[:, oc, i0:S], E1[:, 0:nr], Uc[:, 8 + i0:8 + S], ALU.subtract)

        def stageC(g):
            wg = slabs[g][1]
            Pg = Pb[g % 2]
            for oc2 in range(GC):
                for j in range(NT):
                    bank = self.rotA.next()
                    for kc in range(GC):
                        self.mm(bank, wg[:, kc, oc2 * 128:(oc2 + 1) * 128], Pg[:, kc, j * TT:(j + 1) * TT],
                                kc == 0, kc == GC - 1)
                    self.act(Y[:, g * GC + oc2, j * TT:(j + 1) * TT], bank, AF.Copy,
                             scale=self.vcol(f"l{li}_pool_scale", g * GC + oc2))
            self.w_release()

        stageA(0, range(NT - 1))
        stageA(1)
        stageA(0, [NT - 1])
        for ci in range(nch):
            stageB(ci)
            if ci % GC == GC - 1:
                stageC(ci // GC)
            if ci + 2 < nch:
                stageA(ci + 2)
        (wout,) = self.w_next()
        for j in range(NT):
            self.proj_residual(wout, lambda kc, j=j: Y[:, kc, j * TT:(j + 1) * TT], j, DC)
        self.w_release()

    def conv_layer(self, li):
        cfg = self.cfg
        DC, S, NT, D = cfg.DC, cfg.S, cfg.NT, cfg.D
        VL = S + 2 * CONV_PAD
        hbytes = DC * S * 2
        vbytes = al(VL * 2)
        if hbytes + (DC - 1) * vbytes > DC * S * 4:
            vbytes = VL * 2
        assert hbytes + (DC - 1) * vbytes <= DC * S * 4
        base = self.H_OFF
        cend = base + DC * S * 4
        cbytes = S * 4

        def voff(c):
            return base + hbytes + c * vbytes if c < DC - 1 else cend

        def Vc(c):
            return self.view(voff(c), BF16, (VL,))
        for c in range(DC):
            for k in range(DC):
                lo, hi = max(base + c * cbytes, voff(k)), min(base + (c + 1) * cbytes, voff(k) + vbytes)
                assert lo >= hi or k < c, (c, k)
        C = self.view(base, F32, (DC, S))
        off = al(cend + vbytes)
        DGb = []
        for _i in range(2):
            DGb.append(self.view(off, BF16, (CONV_W, 128))); off = al(off + CONV_W * 128 * 2)
        lnoff = al(cend + vbytes)
        SG = Rot([self.view(off + i * TT * 4, F32, (TT,)) for i in range(2)]); off += 2 * TT * 4
        assert off <= self.FLEX_END
        NLT = 10
        LT = [self.view(lnoff + i * TT * 4, F32, (TT,)) for i in range(NLT)]; lnoff += NLT * TT * 4
        SLd = []
        for _i in range(2):
            SLd.append(self.view(lnoff, BF16, (DC, TT))); lnoff += DC * TT * 2
        assert lnoff <= self.FLEX_END, (lnoff, self.FLEX_END)
        for c in range(DC):
            v = Vc(c)
            self.memset(v[:, 0:CONV_PAD], 0.0)
            self.memset(v[:, CONV_PAD + S:VL], 0.0)
        gslabs = {}

        def glu(c, tiles):
            if c not in gslabs:
                gslabs[c] = self.w_next()
            wv, wgt = gslabs[c]
            v = Vc(c)
            for j in tiles:
                sl = slice(j * TT, (j + 1) * TT)
                self.need_H(j)
                bv = self.rotA.next()
                bg = self.rotA.next()
                for kc in range(DC):
                    self.mm(bv, wv[:, kc, :], self.H[:, kc, sl], kc == 0, kc == DC - 1)
                for kc in range(DC):
                    self.mm(bg, wgt[:, kc, :], self.H[:, kc, sl], kc == 0, kc == DC - 1)
                sg = SG.next()
                self.act(sg, bg, AF.Sigmoid)
                self.tt(v[:, CONV_PAD + j * TT:CONV_PAD + (j + 1) * TT], bv, sg, ALU.mult)
        glu(0, range(NT - 1))
        glu(1, range(NT))
        glu(0, [NT - 1])
        self.w_release()
        self.w_release()
        for c in range(2, DC):
            glu(c, range(NT))
            self.w_release()
        self.h_free = False
        dwo = self.vlay[f"l{li}_conv_dw"]
        for c in range(DC):
            dwc = self.VEC[:, dwo + c * CONV_W: dwo + (c + 1) * CONV_W]
            DGm = DGb[c % 2]
            self.tt(DGm, self.IDENT.unsqueeze(1).broadcast_to([128, CONV_W, 128]),
                    dwc.unsqueeze(2).broadcast_to([128, CONV_W, 128]), ALU.mult)
            v = Vc(c)
            for j in range(NT):
                bank = self.rotA.next()
                for k in range(CONV_W):
                    self.mm(bank, DGm[:, k, :], v[:, j * TT + k:j * TT + k + TT], k == 0, k == CONV_W - 1)
                self.act(C[:, c, j * TT:(j + 1) * TT], bank, AF.Identity, bias=self.vcol(f"l{li}_conv_dw_bias", c))
        (wout,) = self.w_next()
        MEANb, RSTDb, MSQ = LT[0:3], LT[3:6], LT[6]
        T1 = Rot(LT[7:10])
        sbanks = {}

        def stats_chunk(j, c, copy_eng="act"):
            sl = slice(j * TT, (j + 1) * TT)
            if c == 0:
                sbanks[j] = (self.rotS.next(), self.rotS.next())
            b1, b2 = sbanks[j]
            cb = self.SQ.next()
            self.copy(cb, C[:, c, sl], eng=copy_eng)
            cs = self.SQ.next()
            self.act(cs, C[:, c, sl], AF.Square)
            self.mm(b1, self.ONES, cb, c == 0, c == DC - 1)
            self.mm(b2, self.ONES, cs, c == 0, c == DC - 1)

        def lnmath(j):
            b1, b2 = sbanks[j]
            MEAN, RSTD = MEANb[j % 3], RSTDb[j % 3]
            self.ts(MEAN, b1, 1.0 / D, ALU.mult)
            self.tt(MSQ, MEAN, MEAN, ALU.mult)
            self.stt(RSTD, b2, 1.0 / D, MSQ, ALU.mult, ALU.subtract)
            self.act(RSTD, RSTD, AF.Sqrt, bias=self.vcol("eps", 1))
            self.recip(RSTD, RSTD)

        def normalize(j):
            sl = slice(j * TT, (j + 1) * TT)
            MEAN, RSTD = MEANb[j % 3], RSTDb[j % 3]
            SLb = SLd[j % 2]
            for c in range(DC):
                t1 = T1.next()
                self.tt(t1, C[:, c, sl], MEAN, ALU.subtract)
                self.tt(t1, t1, RSTD, ALU.mult)
                self.act(SLb[:, c, :], t1, AF.Silu, bias=self.vcol(f"l{li}_conv_ln_b", c),
                         scale=self.vcol(f"l{li}_conv_ln_g", c))

        for c in range(DC):
            stats_chunk(0, c, copy_eng="dve")
        lnmath(0)
        if NT > 1:
            for c in range(DC):
                stats_chunk(1, c, copy_eng="dve")
            lnmath(1)
        normalize(0)
        for j in range(NT):
            sl = slice(j * TT, (j + 1) * TT)
            SLb = SLd[j % 2]
            for oc in range(DC):
                bank = self.rotA.next()
                for kc in range(DC):
                    self.mm(bank, wout[:, kc, oc * 128:(oc + 1) * 128], SLb[:, kc, :], kc == 0, kc == DC - 1)
                self.tt(self.X[:, oc, sl], self.X[:, oc, sl], bank, ALU.add)
                if j + 2 < NT:
                    stats_chunk(j + 2, oc)
            self.x_ready(j)
            if j + 2 < NT:
                lnmath(j + 2)
            if j + 1 < NT:
                normalize(j + 1)
        self.w_release()
        self.h_free = True
        self.drain(self.nxt)

    def four_layer(self, li):
        cfg = self.cfg
        DC, GC, DG, S, NT, TC, D = cfg.DC, cfg.GC, cfg.DG, cfg.S, cfg.NT, cfg.TC, cfg.D
        assert GC == 2 and DG == 256
        H2 = TC // 2
        NCOL = DG + 2
        abrow = 4 * NCOL
        ab0 = self.H_OFF
        off = al(max(self.FLEX, ab0 + TC * abrow * 2))
        ut = off; off = al(off + DC * S * 2)
        Fd = []
        for _i in range(2):
            Fd.append(self.view(off, BF16, (DC, TT))); off += DC * TT * 2
        F9 = []
        for _i in range(2):
            F9.append(self.view(off, BF16, (TT,))); off += TT * 2
        QsR = Rot([self.view(off + i * TT * 4, F32, (TT,)) for i in range(2)]); off += 2 * TT * 4
        A128 = self.view(off, BF16, (TC, 128)); off += TC * 128 * 2
        W9 = self.view(off, BF16, (D,)); off += D * 2
        assert off <= self.FLEX_END, (off, self.FLEX_END)
        UT = self.view(ut, BF16, (DC, S))

        def AB(tc):
            return self.view(ab0 + tc * abrow * 2, BF16, (4, NCOL))
        ABall = self.view(ab0, BF16, (TC, 4, NCOL))
        tabs = [self.view(ut + h * (2 * H2 * TT * 2), BF16, (2, H2, TT)) for h in range(2)]
        assert 2 * (2 * H2 * TT * 2) <= DC * S * 2
        (win,) = self.w_next()
        self.dma("pool", "wx", (li, self.seq_i), W9, self.wdram[f"l{li}_fourier_w_out_x"])
        self.memset(A128, 0.0)

        def proj(oc, tiles):
            for j in tiles:
                sl = slice(j * TT, (j + 1) * TT)
                self.need_H(j)
                bank = self.rotA.next()
                for kc in range(DC):
                    self.mm(bank, win[:, kc, oc * 128:(oc + 1) * 128], self.H[:, kc, sl], kc == 0, kc == DC - 1)
                self.copy(UT[:, oc, sl], bank, eng="act")
        proj(0, range(NT - 1))
        proj(1, range(NT))
        proj(0, [NT - 1])
        for oc in range(2, DC):
            proj(oc, range(NT))
        self.w_release()
        self.h_free = False
        n = 0
        for tc in range(TC):
            ab = AB(tc)
            for g in range(4):
                bank = self.rotA.next()
                for kc in range(GC):
                    self.mm(bank[:, 0:NCOL], UT[:, g * GC + kc, tc * 128:(tc + 1) * 128], self.CSC[:, kc, :],
                            kc == 0, kc == GC - 1)
                self.copy(ab[:, g, :], bank[:, 0:NCOL], eng=("act" if n % 2 == 0 else "dve"))
                n += 1
        self.copy(A128[:, :, 0:4], ABall[:, :, :, DG], eng="dve")
        (wout,) = self.w_next()
        scale = 1.0 / math.sqrt(S)
        for j in range(NT):
            sl = slice(j * TT, (j + 1) * TT)
            for h in range(2):
                self.dma("sp", f"t{h}", (li, self.seq_i, j), tabs[h], self.dfts[j, h])
            Fj, f9 = Fd[j % 2], F9[j % 2]
            for g in range(4):
                Pb = self.rotA.next()
                Qb = self.rotA.next()
                for tc in range(TC):
                    self.mm(Pb, AB(tc)[:, g, 0:128], tabs[tc // H2][:, 0, tc % H2, :], tc == 0, tc == TC - 1)
                for tc in range(TC):
                    self.mm(Qb, AB(tc)[:, g, 128:256], tabs[tc // H2][:, 1, tc % H2, :], tc == 0, tc == TC - 1)
                qs = QsR.next()
                self.act(qs, Qb, AF.Copy, scale=scale)
                self.stt(Fj[:, 2 * g, :], Pb, scale, qs, ALU.mult, ALU.subtract)
                self.stt(Fj[:, 2 * g + 1, :], Pb, scale, qs, ALU.mult, ALU.add)
            Sb = self.rotA.next()
            for tc in range(TC):
                self.mm(Sb, A128[:, tc, :], tabs[tc // H2][:, 0, tc % H2, :], tc == 0, tc == TC - 1)
            self.act(f9, Sb, AF.Copy, scale=scale)
            for oc in range(DC):
                bank = self.rotA.next()
                for kc in range(DC):
                    self.mm(bank, wout[:, kc, oc * 128:(oc + 1) * 128], Fj[:, kc, :], kc == 0, False)
                self.mm(bank, W9[:, oc * 128:(oc + 1) * 128], f9, False, True)
                self.tt(self.X[:, oc, sl], self.X[:, oc, sl], bank, ALU.add)
            self.x_ready(j)
        self.w_release()
        self.h_free = True
        self.drain(self.nxt)

    def finish(self, s):
        cfg = self.cfg
        if not cfg.final:
            self.out_final.append(self.dma("sp", "out0", ("o", s), self.outT[s].rearrange("(c p) t -> p c t", p=128),
                                           self.X))
            return
        for j in range(cfg.NT):
            self.need_H(j)
        n = self.cur
        while n.loadq:
            self._ready(n.after, n.loadq.pop(0))

    def build(self):
        cfg = self.cfg
        nc = self.nc
        DC, S, D = cfg.DC, cfg.S, cfg.D
        lay, nv = vec_layout(cfg)
        self.vlay = lay
        xT = nc.dram_tensor("xT", [cfg.NSEQ, D, S], F32, kind="ExternalInput").ap()
        vecs = nc.dram_tensor("vecs", [128, nv], F32, kind="ExternalInput").ap()
        cmat = nc.dram_tensor("cmat", [128, 256], BF16, kind="ExternalInput").ap()
        has_four = "four" in cfg.kinds
        if has_four:
            dftc = nc.dram_tensor("dftc", [128, cfg.GC, cfg.DG + 2], BF16, kind="ExternalInput").ap()
            self.dfts = nc.dram_tensor("dfts", [cfg.NT, 2, 128, 2, cfg.TC // 2, TT], BF16, kind="ExternalInput").ap()
        self.wdram = {}
        for name, shape in weight_names(cfg):
            self.wdram[name] = nc.dram_tensor(name, list(shape), F32, kind="ExternalInput").ap()
        self.outT = nc.dram_tensor("outT", [cfg.NSEQ, D, S], F32, kind="ExternalOutput").ap()

        off = 0
        self.X_OFF = off; off += DC * S * 4
        self.H_OFF = off; off += DC * S * 2
        nvb = (nv * 4 + 3) // 4 * 4
        small = nvb + 256 * 2 + (cfg.GC * (cfg.DG + 2) * 2 if has_four else 0) + 4 * TT * 2 + 2 * TT * 4
        self.W_SLOT = max(DC * cfg.SL * 2 + (cfg.SL // 128) * D * 2, DC * D * 2)
        self.W_SLOT = (self.W_SLOT + 255) // 256 * 256
        self.SM_OFF = (cfg.arena_bytes - small) // 256 * 256
        self.W_OFF = self.SM_OFF - 2 * self.W_SLOT
        self.FLEX = off
        self.FLEX_END = self.W_OFF
        assert self.FLEX_END > self.FLEX
        real_view = self.view

        with nc.allow_low_precision("bf16 matmul operands, fp32 accumulation"), \
                nc.sbuf_tensor("arena", [128, cfg.arena_bytes // 4], F32) as arena, \
                nc.psum_tensor("ps", [128, 8, TT], F32) as ps:
            self.arena = arena

            def flexview(off_, dtype, dims):
                return real_view(off_, dtype, dims)
            so = self.SM_OFF
            self.VEC = self.view(so, F32, (nv,)); so += nvb
            CM = self.view(so, BF16, (256,)); so += 512
            self.ONES = CM[:, 0:128]
            self.IDENT = CM[:, 128:256]
            if has_four:
                self.CSC = self.view(so, BF16, (cfg.GC, cfg.DG + 2)); so += cfg.GC * (cfg.DG + 2) * 2
            self.SQ = Rot([self.view(so + i * TT * 2, BF16, (TT,)) for i in range(4)]); so += 4 * TT * 2
            self.RS = Rot([self.view(so + i * TT * 4, F32, (TT,)) for i in range(2)]); so += 2 * TT * 4
            assert so <= cfg.arena_bytes
            self.X = self.view(self.X_OFF, F32, (DC, S))
            self.H = self.view(self.H_OFF, BF16, (DC, S))
            banks = [ps[:, i, :] for i in range(8)]
            self.rotS = Rot(banks[0:2])
            self.rotA = Rot(banks[2:8])
            self.rotU = Rot(banks[2:5])
            self.rotD = Rot(banks[5:8])

            self.dma("sp", "c0", "c", self.VEC, vecs)
            self.dma("sp", "c0", "c", CM, cmat)
            if has_four:
                self.dma("sp", "c0", "c", self.CSC, dftc)

            self.plan = self.slab_plan()
            self.w_issued = self.w_done = self.w_cur = 0
            self.xT = xT
            self.h_free = True
            ooff = self.FLEX + 2 * (cfg.SL // 128) * TT * 2 + 3 * TT * 4
            self.O = [self.view(ooff + i * DC * TT * 4, F32, (DC, TT)) for i in range(2)]
            assert ooff + 2 * DC * TT * 4 <= self.FLEX_END
            NS = Builder.NormState
            norms = []
            for s in range(cfg.NSEQ):
                row = []
                for kind, li in zip(cfg.kinds, cfg.layer_ids):
                    row.append(NS(f"l{li}_norm_mix", seq=s))
                    row.append(NS(f"l{li}_norm_mlp", seq=s))
                if cfg.final:
                    row.append(NS("final_norm", final=True, seq=s))
                else:
                    row.append(None)
                norms.append(row)
            if cfg.final:
                for s in range(cfg.NSEQ - 1):
                    norms[s][-1].after = norms[s + 1][0]
            self.cur = self.nxt = None
            for s in range(cfg.NSEQ):
                self.seq_i = s
                row = norms[s]
                if s == 0 or not cfg.final:
                    for j in range(cfg.NT):
                        self.load_x_tile(s, j)
                        self._ready(row[0], j)
                k = 0
                for kind, li in zip(cfg.kinds, cfg.layer_ids):
                    self.cur, self.nxt = row[k], row[k + 1]
                    if kind == "pool":
                        self.pool_layer(li)
                    elif kind == "conv":
                        self.conv_layer(li)
                    else:
                        self.four_layer(li)
                    k += 1
                    self.cur, self.nxt = row[k], row[k + 1]
                    self.mlp(li)
                    k += 1
                self.cur, self.nxt = row[k], None
                self.finish(s)
            assert self.w_cur == len(self.plan)

            P = self.P
            P.finalize()
            finals = {}
            for op in self.out_final:
                key = ("c", op.chan)
                finals[key] = max(finals.get(key, 0), op.gend)
            keys = P.sem_keys()
            import contextlib
            with contextlib.ExitStack() as es:
                sems = {}
                for k in keys:
                    sems[k] = es.enter_context(nc.semaphore(f"s_{k[0]}_{k[1]}"))
                block = es.enter_context(nc.Block())
                P.emit(block, sems, list(finals.items()))
        return nc


_KINDS = ("pool", "conv", "four", "pool")


def make_in_maps(cfg, inputs, x_shards):
    vec = pack_vecs(cfg, inputs)
    cmat, csc, tab = const_tables(cfg)
    base = {"vecs": vec, "cmat": cmat}
    if "four" in cfg.kinds:
        base["dftc"] = csc
        base["dfts"] = tab
    for name, _shape in weight_names(cfg):
        if name.endswith("fourier_w_out_perm") or name.endswith("fourier_w_out_x"):
            continue
        base[name] = np.ascontiguousarray(np.asarray(inputs[name], np.float32))
    for kind, li in zip(cfg.kinds, cfg.layer_ids):
        if kind != "four":
            continue
        wo = np.asarray(inputs[f"l{li}_fourier_w_out"], np.float32)
        DG = cfg.DG
        perm = np.zeros_like(wo)
        wx = np.zeros((128, cfg.D), np.float32)
        for g in range(4):
            perm[(2 * g) * 128:(2 * g + 1) * 128] = wo[g * DG:g * DG + 128]
            perm[(2 * g + 1) * 128 + 1:(2 * g + 2) * 128] = wo[g * DG + DG - 1:g * DG + 128:-1]
            wx[g] = wo[g * DG + 128]
        base[f"l{li}_fourier_w_out_perm"] = perm
        base[f"l{li}_fourier_w_out_x"] = wx
    maps = []
    for xs in x_shards:
        m = dict(base)
        m["xT"] = xs
        maps.append(m)
    return maps


def kernel(**inputs):
    x = np.asarray(inputs["x"], np.float32)
    B, S, D = x.shape
    ncores = 8
    nseq = B // ncores
    cfg = Cfg(S=S, D=D, FF=4 * D, NSEQ=nseq, kinds=_KINDS, final=True)
    xT = np.ascontiguousarray(x.transpose(0, 2, 1))
    shards = [xT[i * nseq:(i + 1) * nseq] for i in range(ncores)]
    nc = Builder(cfg).build()
    in_maps = make_in_maps(cfg, inputs, shards)
    res = run_bass_kernel_spmd(nc, in_maps, core_ids=list(range(ncores)))
    outT = np.concatenate([np.asarray(r["outT"]) for r in res.results], axis=0)
    return np.ascontiguousarray(outT.transpose(0, 2, 1)).astype(np.float32)
```

```python
import math
import numpy as np
import ml_dtypes
import concourse.bass as bass
import concourse.mybir as mybir
from concourse.bass_utils import run_bass_kernel_spmd

F32 = mybir.dt.float32
BF16 = mybir.dt.bfloat16
AF = mybir.ActivationFunctionType
ALU = mybir.AluOpType

POOL_WINDOWS = (2, 4, 8, 16)
CONV_W = 31
CONV_PAD = 15
NORM_EPS = 1e-6
LN_EPS = 1e-5
TT = 512
GRAN = 256


class Cfg:
    def __init__(self, S=2048, D=1024, FF=4096, NSEQ=2, kinds=("pool", "conv", "four", "pool"),
                 layer_ids=None, final=True, SL=512, arena_bytes=220160, dma_scratch=8192, conv_dve_taps=7):
        self.S, self.D, self.FF, self.NSEQ = S, D, FF, NSEQ
        self.kinds = tuple(kinds)
        self.layer_ids = tuple(layer_ids) if layer_ids is not None else tuple(range(len(kinds)))
        self.final = final
        self.DC = D // 128
        self.GC = self.DC // 4
        self.DG = D // 4
        self.NT = S // TT
        self.TC = S // 128
        self.SL = SL
        self.arena_bytes = arena_bytes
        self.dma_scratch = dma_scratch
        self.conv_dve_taps = conv_dve_taps
        assert D % 512 == 0 and S % TT == 0 and FF % SL == 0 and self.TC % 2 == 0


def vec_layout(cfg):
    lay, off = {}, 0

    def put(name, n):
        nonlocal off
        lay[name] = off
        off += n
    DC = cfg.DC
    for kind, li in zip(cfg.kinds, cfg.layer_ids):
        put(f"l{li}_norm_mix", DC)
        put(f"l{li}_norm_mlp", DC)
        if kind == "pool":
            put(f"l{li}_pool_scale", DC)
        if kind == "conv":
            put(f"l{li}_conv_dw_bias", DC)
            put(f"l{li}_conv_ln_g", DC)
            put(f"l{li}_conv_ln_b", DC)
            put(f"l{li}_conv_dw", DC * CONV_W)
    put("final_norm", DC)
    put("edges", 4 * 2 * 8)
    put("eps", 2)
    return lay, off


def weight_names(cfg):
    names = []
    for kind, li in zip(cfg.kinds, cfg.layer_ids):
        if kind == "pool":
            names += [(f"l{li}_pool_w_in", (cfg.D, cfg.D)), (f"l{li}_pool_w_group", (4, cfg.DG, cfg.DG)),
                      (f"l{li}_pool_w_out", (cfg.D, cfg.D))]
        elif kind == "conv":
            names += [(f"l{li}_conv_w_in", (cfg.D, 2 * cfg.D)), (f"l{li}_conv_w_out", (cfg.D, cfg.D))]
        else:
            names += [(f"l{li}_fourier_w_in", (cfg.D, cfg.D)), (f"l{li}_fourier_w_out_perm", (cfg.D, cfg.D)),
                      (f"l{li}_fourier_w_out_x", (128, cfg.D))]
        names += [(f"l{li}_mlp_up", (cfg.D, cfg.FF)), (f"l{li}_mlp_down", (cfg.FF, cfg.D))]
    return names


def pack_vecs(cfg, inputs):
    lay, nv = vec_layout(cfg)
    DC = cfg.DC
    V = np.zeros((128, nv), np.float32)

    def colmajor(v):
        return np.asarray(v, np.float32).reshape(DC, 128).T
    for name, off in lay.items():
        if name == "edges":
            for wi, w in enumerate(POOL_WINDOWS):
                h = w // 2
                for i in range(h):
                    V[:, off + (wi * 2 + 0) * 8 + i] = 1.0 / (i + h)
                for m in range(h - 1):
                    i = cfg.S - h + 1 + m
                    V[:, off + (wi * 2 + 1) * 8 + m] = 1.0 / (cfg.S - i + h)
        elif name == "eps":
            V[:, off] = NORM_EPS
            V[:, off + 1] = LN_EPS
        elif name.endswith("conv_dw"):
            dw = np.asarray(inputs[name], np.float32)
            V[:, off:off + DC * CONV_W] = dw.T.reshape(DC, 128, CONV_W).transpose(1, 0, 2).reshape(128, DC * CONV_W)
        else:
            V[:, off:off + DC] = colmajor(inputs[name])
    return V


def const_tables(cfg):
    bf = ml_dtypes.bfloat16
    cmat = np.zeros((128, 256), np.float32)
    cmat[:, 0:128] = 1.0
    cmat[:, 128:256] = np.eye(128, dtype=np.float32)
    DG, GC, S, TC, NT = cfg.DG, cfg.GC, cfg.S, cfg.TC, cfg.NT
    a = np.arange(DG, dtype=np.float64)
    hk = DG // 2
    k = np.arange(hk, dtype=np.float64)
    ang = 2.0 * np.pi * np.outer(a, k) / DG
    csc = np.zeros((DG, 2 * hk + 2), np.float64)
    csc[:, 0:hk] = np.cos(ang)
    csc[:, hk:2 * hk] = np.sin(ang)
    csc[:, 2 * hk] = np.cos(np.pi * a)
    csc /= math.sqrt(DG)
    csc = csc.reshape(GC, 128, 2 * hk + 2).transpose(1, 0, 2)
    n = np.arange(S, dtype=np.int64)
    prod = np.outer(n, n) % S
    ang2 = 2.0 * np.pi * prod.astype(np.float64) / S
    cs = np.cos(ang2)
    ns = np.sin(ang2)
    H2 = TC // 2
    tab = np.stack([cs, ns], axis=0)
    tab = tab.reshape(2, 2, H2, 128, NT, TT)
    tab = tab.transpose(4, 1, 3, 0, 2, 5)
    return (cmat.astype(bf), np.ascontiguousarray(csc).astype(bf),
            np.ascontiguousarray(tab).astype(bf))


class Op:
    __slots__ = ("eng", "build", "deps", "signal", "count", "chan", "group", "seq", "waits", "gend")


class Prog:
    ENG = ("pe", "act", "dve", "pool", "sp")

    def __init__(self):
        self.ops = {e: [] for e in self.ENG}
        self.lastw = {}
        self.readers = {}
        self.chan_ops = {}
        self.gcache = {}
        self.seq = 0

    def gran(self, ap):
        if str(ap.space) == "DRAM":
            return ()
        dims = ap.ap
        key = (ap.tensor.name, ap.offset, dims, str(ap.dtype))
        g = self.gcache.get(key)
        if g is not None:
            return g
        esz = 2 if ap.dtype == BF16 else 4
        pstep = dims[0][0]
        off = ap.offset % pstep if pstep > 0 else ap.offset
        free = [d for d in dims[1:] if d[1] > 1 and d[0] != 0]
        if not free:
            free = [(1, 1)]
        starts = np.array([off], dtype=np.int64)
        for step, cnt in free[:-1]:
            starts = (starts[:, None] + np.arange(cnt, dtype=np.int64)[None, :] * step).ravel()
        step_l, cnt_l = free[-1]
        lo = starts * esz
        hi = (starts + (cnt_l - 1) * abs(step_l) + 1) * esz
        name = ap.tensor.name
        s = set()
        for l, h in zip(lo.tolist(), hi.tolist()):
            for gi in range(l // GRAN, (h - 1) // GRAN + 1):
                s.add((name, gi))
        g = tuple(s)
        self.gcache[key] = g
        return g

    def add(self, eng, build, reads=(), writes=(), chan=None, group=None):
        op = Op()
        op.eng, op.build, op.chan, op.group, op.signal = eng, build, chan, group, False
        op.seq = self.seq
        self.seq += 1
        deps = {}

        def dep(d):
            if d is None:
                return
            if d.chan is None and d.eng == "pe" and eng == "pe" and chan is None:
                return
            if chan is not None and d.chan == chan and d.group == group:
                return
            key = ("c", d.chan) if d.chan else ("e", d.eng)
            cur = deps.get(key)
            if cur is None or d.seq > cur.seq:
                deps[key] = d
        rg = set()
        for ap in reads:
            rg.update(self.gran(ap))
        wg = set()
        for ap in writes:
            wg.update(self.gran(ap))
        for g in rg:
            dep(self.lastw.get(g))
        for g in wg:
            dep(self.lastw.get(g))
            rd = self.readers.get(g)
            if rd:
                for r in rd.values():
                    dep(r)
        mykey = ("c", chan) if chan else ("e", eng)
        for g in rg:
            if g not in wg:
                self.readers.setdefault(g, {})[mykey] = op
        for g in wg:
            self.lastw[g] = op
            self.readers[g] = {}
        op.deps = list(deps.values())
        for d in op.deps:
            d.signal = True
        self.ops[eng].append(op)
        if chan:
            self.chan_ops.setdefault(chan, []).append(op)
        return op

    def finalize(self):
        for e in self.ENG:
            c = 0
            for op in self.ops[e]:
                if op.chan is None:
                    if op.signal:
                        c += 1
                    op.count = c
        for ch, lst in self.chan_ops.items():
            c = 0
            ends = {}
            for op in lst:
                c += 16
                op.count = c
                ends[op.group] = c
            for op in lst:
                op.gend = ends[op.group]
        for e in self.ENG:
            waited = {}
            for op in self.ops[e]:
                w = []
                for d in op.deps:
                    if d.chan:
                        key, val = ("c", d.chan), d.gend
                    else:
                        key, val = ("e", d.eng), d.count
                    if waited.get(key, 0) < val:
                        waited[key] = val
                        w.append((key, val))
                op.waits = w

    def sem_keys(self):
        keys = [("e", e) for e in self.ENG if any(o.chan is None for o in self.ops[e])]
        keys += [("c", ch) for ch in self.chan_ops]
        return keys

    def emit(self, block, sems, final_waits):
        attr = {"pe": "tensor", "act": "scalar", "dve": "vector", "pool": "gpsimd", "sp": "sync"}
        for e in self.ENG:
            ops = self.ops[e]
            if not ops:
                continue

            def body(eng, ops=ops, e=e):
                for op in ops:
                    for key, val in op.waits:
                        eng.wait_ge(sems[key], val)
                    ins = op.build(eng)
                    if op.chan:
                        ins.then_inc(sems[("c", op.chan)], 16)
                    elif op.signal:
                        ins.then_inc(sems[("e", e)], 1)
                if e == "sp":
                    for key, val in final_waits:
                        eng.wait_ge(sems[key], val)
            getattr(block, attr[e])(body)


def al(x, a=GRAN):
    return (x + a - 1) // a * a


class Rot:
    def __init__(self, items):
        self.items = list(items)
        self.i = 0

    def next(self):
        v = self.items[self.i % len(self.items)]
        self.i += 1
        return v


class Builder:
    def __init__(self, cfg):
        self.cfg = cfg
        self.P = Prog()
        self.nc = bass.Bass("TRN2", target_bir_lowering=False, dynamic_dma_scratch_size=cfg.dma_scratch)
        self.out_final = []

    def mm(self, out, lhsT, rhs, start, stop):
        self.P.add("pe", lambda e: e.matmul(out, lhsT, rhs, start=start, stop=stop),
                   reads=[lhsT, rhs], writes=[out])

    def act(self, out, in_, func, bias=None, scale=None, eng="act"):
        reads = [in_]
        kw = {}
        if bias is not None:
            kw["bias"] = bias
            if not isinstance(bias, (int, float)):
                reads.append(bias)
        if scale is not None:
            kw["scale"] = scale
            if not isinstance(scale, (int, float)):
                reads.append(scale)
        self.P.add(eng, lambda e: e.activation(out, in_, func, **kw), reads=reads, writes=[out])

    def tt(self, out, in0, in1, op, eng="dve"):
        self.P.add(eng, lambda e: e.tensor_tensor(out, in0, in1, op), reads=[in0, in1], writes=[out])

    def ts(self, out, in0, s1, op0, s2=None, op1=None, eng="dve"):
        reads = [in0] + [s for s in (s1, s2) if s is not None and not isinstance(s, (int, float))]
        if op1 is None:
            self.P.add(eng, lambda e: e.tensor_scalar(out, in0, s1, None, op0), reads=reads, writes=[out])
        else:
            self.P.add(eng, lambda e: e.tensor_scalar(out, in0, s1, s2, op0, op1), reads=reads, writes=[out])

    def stt(self, out, in0, scalar, in1, op0, op1, eng="dve"):
        reads = [in0, in1] + ([] if isinstance(scalar, (int, float)) else [scalar])
        self.P.add(eng, lambda e: e.scalar_tensor_tensor(out, in0, scalar, in1, op0, op1),
                   reads=reads, writes=[out])

    def copy(self, out, in_, eng="dve"):
        if eng == "act":
            self.act(out, in_, AF.Copy)
        else:
            self.P.add(eng, lambda e: e.tensor_copy(out, in_), reads=[in_], writes=[out])

    def recip(self, out, in_):
        self.P.add("dve", lambda e: e.reciprocal(out, in_), reads=[in_], writes=[out])

    def memset(self, ap, val, eng="dve"):
        self.P.add(eng, lambda e: e.memset(ap, val), writes=[ap])

    def dma(self, q, chan, group, out, in_, **kw):
        return self.P.add(q, lambda e: e.dma_start(out=out, in_=in_, **kw), reads=[in_], writes=[out],
                          chan=chan, group=group)

    def view(self, off, dtype, dims):
        esz = 2 if dtype == BF16 else 4
        n = 1
        for d in dims:
            n *= d
        nb = n * esz
        assert off % 4 == 0 and nb % 4 == 0, (off, nb)
        assert off + nb <= self.cfg.arena_bytes, ("arena overflow", off, nb, self.cfg.arena_bytes)
        ap = self.arena[:, off // 4:(off + nb) // 4]
        if dtype != F32:
            ap = ap.bitcast(dtype)
        if len(dims) == 2:
            ap = ap.rearrange("p (a b) -> p a b", a=dims[0])
        elif len(dims) == 3:
            ap = ap.rearrange("p (a b c) -> p a b c", a=dims[0], b=dims[1])
        return ap

    def slab_plan(self):
        cfg = self.cfg
        D, DC, DG, GC, FF, SL = cfg.D, cfg.DC, cfg.DG, cfg.GC, cfg.FF, cfg.SL
        W = self.wdram
        plan = []

        def rows(ap):
            return ap.rearrange("(kc p) n -> p kc n", p=128)
        for _s in range(cfg.NSEQ):
            for kind, li in zip(cfg.kinds, cfg.layer_ids):
                if kind == "pool":
                    win, wg, wo = W[f"l{li}_pool_w_in"], W[f"l{li}_pool_w_group"], W[f"l{li}_pool_w_out"]
                    for g in range(4):
                        plan.append([(0, (DC, DG), rows(win[:, g * DG:(g + 1) * DG])),
                                     (DC * DG * 2, (GC, DG), rows(wg[g]))])
                    plan.append([(0, (DC, D), rows(wo))])
                elif kind == "conv":
                    win, wo = W[f"l{li}_conv_w_in"], W[f"l{li}_conv_w_out"]
                    for c in range(DC):
                        plan.append([(0, (DC, 128), rows(win[:, c * 128:(c + 1) * 128])),
                                     (DC * 128 * 2, (DC, 128), rows(win[:, D + c * 128:D + (c + 1) * 128]))])
                    plan.append([(0, (DC, D), rows(wo))])
                else:
                    win, wo = W[f"l{li}_fourier_w_in"], W[f"l{li}_fourier_w_out_perm"]
                    plan.append([(0, (DC, D), rows(win))])
                    plan.append([(0, (DC, D), rows(wo))])
                up, dn = W[f"l{li}_mlp_up"], W[f"l{li}_mlp_down"]
                for s in range(FF // SL):
                    plan.append([(0, (DC, SL), rows(up[:, s * SL:(s + 1) * SL])),
                                 (DC * SL * 2, (SL // 128, D), rows(dn[s * SL:(s + 1) * SL, :]))])
        return plan

    def w_issue(self):
        while self.w_issued < len(self.plan) and self.w_issued < self.w_done + 2:
            k = self.w_issued
            slot = k % 2
            for (off, dims, src) in self.plan[k]:
                dst = self.view(self.W_OFF + slot * self.W_SLOT + off, BF16, dims)
                self.dma("pool", f"w{slot}", k, dst, src)
            self.w_issued += 1

    def w_next(self):
        k = self.w_cur
        self.w_cur += 1
        self.w_issue()
        assert k < self.w_issued
        slot = k % 2
        return [self.view(self.W_OFF + slot * self.W_SLOT + off, BF16, dims) for (off, dims, _src) in self.plan[k]]

    def w_release(self):
        self.w_done += 1
        self.w_issue()

    def vcol(self, name, c, n=1):
        o = self.vlay[name] + c
        return self.VEC[:, o:o + n]

    class NormState:
        def __init__(self, gname, final=False, seq=0):
            self.gname, self.final, self.seq = gname, final, seq
            self.pending = None
            self.deferred = []
            self.done = set()
            self.after = None
            self.loadq = []

    def emit_pre(self, n, j):
        sl = slice(j * TT, (j + 1) * TT)
        for c in range(self.cfg.DC):
            self.act(self.H[:, c, sl], self.X[:, c, sl], AF.Square)
        n.pending = j

    def flush_post(self, n):
        cfg = self.cfg
        j = n.pending
        if j is None:
            return
        n.pending = None
        if n.final and n.loadq:
            self._ready(n.after, n.loadq.pop(0))
        sl = slice(j * TT, (j + 1) * TT)
        bank = self.rotS.next()
        for c in range(cfg.DC):
            self.mm(bank, self.ONES, self.H[:, c, sl], c == 0, c == cfg.DC - 1)
        r = self.RS.next()
        self.act(r, bank, AF.Sqrt, bias=self.vcol("eps", 0), scale=1.0 / cfg.D)
        self.recip(r, r)
        if not n.final:
            for c in range(cfg.DC):
                self.stt(self.H[:, c, sl], self.X[:, c, sl], self.vcol(n.gname, c), r, ALU.mult, ALU.mult)
        else:
            o = self.O[j % 2]
            for c in range(cfg.DC):
                self.stt(o[:, c, :], self.X[:, c, sl], self.vcol(n.gname, c), r, ALU.mult, ALU.mult)
            self.out_final.append(self.dma("sp", f"out{j % 2}", ("o", n.seq, j),
                                           self.outT[n.seq][:, sl].rearrange("(c p) t -> p c t", p=128), o))
            if n.after is not None:
                self.load_x_tile(n.seq + 1, j)
                n.loadq.append(j)
        n.done.add(j)

    def load_x_tile(self, s, j):
        sl = slice(j * TT, (j + 1) * TT)
        self.dma("sp", f"xin{j}", ("x", s, j), self.X[:, :, sl],
                 self.xT[s][:, sl].rearrange("(c p) t -> p c t", p=128))

    def _ready(self, n, j):
        if n is None:
            return
        if self.h_free:
            self.flush_post(n)
            self.emit_pre(n, j)
        else:
            n.deferred.append(j)

    def x_ready(self, j):
        self._ready(self.nxt, j)

    def drain(self, n):
        if n is None:
            return
        assert self.h_free
        while n.deferred:
            self.flush_post(n)
            self.emit_pre(n, n.deferred.pop(0))

    def need_H(self, j):
        n = self.cur
        while j not in n.done:
            assert self.h_free
            if n.pending is not None:
                self.flush_post(n)
            elif n.deferred:
                self.emit_pre(n, n.deferred.pop(0))
            else:
                raise AssertionError(("H tile never produced", j))

    def rms_stats_sq(self, j):
        raise NotImplementedError

    def proj_residual(self, wout, src_of_kc, j, nk):
        cfg = self.cfg
        sl = slice(j * TT, (j + 1) * TT)
        for oc in range(cfg.DC):
            bank = self.rotA.next()
            for kc in range(nk):
                self.mm(bank, wout[:, kc, oc * 128:(oc + 1) * 128], src_of_kc(kc), kc == 0, kc == nk - 1)
            self.tt(self.X[:, oc, sl], self.X[:, oc, sl], bank, ALU.add)
        self.x_ready(j)

    def mlp(self, li):
        cfg = self.cfg
        DC, SL, NT = cfg.DC, cfg.SL, cfg.NT
        FCS = SL // 128
        nslab = cfg.FF // SL
        assert nslab >= 2
        flex = self.FLEX
        A = [self.view(flex + i * FCS * TT * 2, BF16, (FCS, TT)) for i in range(2)]
        toff = flex + 2 * FCS * TT * 2
        Tr = Rot([self.view(toff + i * TT * 4, F32, (TT,)) for i in range(3)])
        steps = [(s, j) for s in range(cfg.FF // SL) for j in range(NT)]
        slabs = {}

        def up(i):
            s, j = steps[i]
            if s not in slabs:
                slabs[s] = self.w_next()
            wu = slabs[s][0]
            sl = slice(j * TT, (j + 1) * TT)
            a = A[i % 2]
            if s == 0:
                self.need_H(j)
            for fc in range(FCS):
                bank = self.rotU.next()
                for kc in range(DC):
                    self.mm(bank, wu[:, kc, fc * 128:(fc + 1) * 128], self.H[:, kc, sl], kc == 0, kc == DC - 1)
                t = Tr.next()
                self.act(t, bank, AF.Relu)
                self.tt(a[:, fc, :], t, t, ALU.mult)

        def down(i):
            s, j = steps[i]
            wd = slabs[s][1]
            sl = slice(j * TT, (j + 1) * TT)
            a = A[i % 2]
            for dc in range(DC):
                bank = self.rotD.next()
                for fc in range(FCS):
                    self.mm(bank, wd[:, fc, dc * 128:(dc + 1) * 128], a[:, fc, :], fc == 0, fc == FCS - 1)
                self.tt(self.X[:, dc, sl], self.X[:, dc, sl], bank, ALU.add)
            if s == nslab - 1:
                self.x_ready(j)
            if j == NT - 1:
                self.w_release()
        up(0)
        for i in range(len(steps)):
            if i + 1 < len(steps):
                up(i + 1)
            down(i)

    def pool_layer(self, li):
        cfg = self.cfg
        DC, GC, DG, S, NT = cfg.DC, cfg.GC, cfg.DG, cfg.S, cfg.NT
        L = S + 16
        off = self.FLEX
        Ub = []
        for _i in range(2):
            Ub.append(self.view(off, F32, (L,))); off = al(off + L * 4)
        T = self.view(off, F32, (L,)); off = al(off + L * 4)
        E1 = self.view(off, F32, (8,)); off = al(off + 32)
        Pb = []
        for _i in range(2):
            Pb.append(self.view(off, BF16, (GC, S))); off += GC * S * 2
        Y = self.view(off, BF16, (DC, S)); off += DC * S * 2
        assert off <= self.FLEX_END, (off, self.FLEX_END)
        for u in Ub:
            self.memset(u[:, 0:8], 0.0)
            self.memset(u[:, 8 + S:L], 0.0)
        eo = self.vlay["edges"]
        nch = 4 * GC
        slabs = {}

        def stageA(ci, tiles=None):
            g, oc = divmod(ci, GC)
            if g not in slabs:
                slabs[g] = self.w_next()
            win = slabs[g][0]
            u = Ub[ci % 2]
            for j in (range(NT) if tiles is None else tiles):
                self.need_H(j)
                bank = self.rotA.next()
                for kc in range(DC):
                    self.mm(bank, win[:, kc, oc * 128:(oc + 1) * 128], self.H[:, kc, j * TT:(j + 1) * TT],
                            kc == 0, kc == DC - 1)
                self.act(u[:, 8 + j * TT:8 + (j + 1) * TT], bank, AF.Copy)

        def stageB(ci):
            g, oc = divmod(ci, GC)
            w = POOL_WINDOWS[g]
            half = w // 2
            Uc = Ub[ci % 2]
            Pg = Pb[g % 2]
            self.tt(T[:, 0:L - 1], Uc[:, 0:L - 1], Uc[:, 1:L], ALU.add)
            cur = 2
            while cur < w:
                n = L - 2 * cur + 1
                self.tt(T[:, 0:n], T[:, 0:n], T[:, cur:cur + n], ALU.add)
                cur *= 2
            self.stt(Pg[:, oc, :], T[:, 8 - half:8 - half + S], 1.0 / w, Uc[:, 8:8 + S], ALU.mult, ALU.subtract)
            el = self.VEC[:, eo + (g * 2) * 8: eo + (g * 2) * 8 + half]
            self.tt(E1[:, 0:half], T[:, 8 - half:8], el, ALU.mult)
            self.tt(Pg[:, oc, 0:half], E1[:, 0:half], Uc[:, 8:8 + half], ALU.subtract)
            nr = half - 1
            if nr > 0:
                i0 = S - half + 1
                er = self.VEC[:, eo + (g * 2 + 1) * 8: eo + (g * 2 + 1) * 8 + nr]
                self.tt(E1[:, 0:nr], T[:, i0 + 8 - half:i0 + 8 - half + nr], er, ALU.mult)
                self.tt(Pg[:, oc, i0:S], E1[:, 0:nr], Uc[:, 8 + i0:8 + S], ALU.subtract)

        def stageC(g):
            wg = slabs[g][1]
            Pg = Pb[g % 2]
            for oc2 in range(GC):
                for j in range(NT):
                    bank = self.rotA.next()
                    for kc in range(GC):
                        self.mm(bank, wg[:, kc, oc2 * 128:(oc2 + 1) * 128], Pg[:, kc, j * TT:(j + 1) * TT],
                                kc == 0, kc == GC - 1)
                    self.act(Y[:, g * GC + oc2, j * TT:(j + 1) * TT], bank, AF.Copy,
                             scale=self.vcol(f"l{li}_pool_scale", g * GC + oc2))
            self.w_release()

        stageA(0, range(NT - 1))
        stageA(1)
        stageA(0, [NT - 1])
        for ci in range(nch):
            stageB(ci)
            if ci % GC == GC - 1:
                stageC(ci // GC)
            if ci + 2 < nch:
                stageA(ci + 2)
        (wout,) = self.w_next()
        for j in range(NT):
            self.proj_residual(wout, lambda kc, j=j: Y[:, kc, j * TT:(j + 1) * TT], j, DC)
        self.w_release()

    def conv_layer(self, li):
        cfg = self.cfg
        DC, S, NT, D = cfg.DC, cfg.S, cfg.NT, cfg.D
        VL = S + 2 * CONV_PAD
        hbytes = DC * S * 2
        vbytes = al(VL * 2)
        if hbytes + (DC - 1) * vbytes > DC * S * 4:
            vbytes = VL * 2
        assert hbytes + (DC - 1) * vbytes <= DC * S * 4
        base = self.H_OFF
        cend = base + DC * S * 4
        cbytes = S * 4

        def voff(c):
            return base + hbytes + c * vbytes if c < DC - 1 else cend

        def Vc(c):
            return self.view(voff(c), BF16, (VL,))
        for c in range(DC):
            for k in range(DC):
                lo, hi = max(base + c * cbytes, voff(k)), min(base + (c + 1) * cbytes, voff(k) + vbytes)
                assert lo >= hi or k < c, (c, k)
        C = self.view(base, F32, (DC, S))
        off = al(cend + vbytes)
        DGb = []
        for _i in range(2):
            DGb.append(self.view(off, BF16, (CONV_W, 128))); off = al(off + CONV_W * 128 * 2)
        lnoff = al(cend + vbytes)
        SG = Rot([self.view(off + i * TT * 4, F32, (TT,)) for i in range(2)]); off += 2 * TT * 4
        assert off <= self.FLEX_END
        NLT = 10
        LT = [self.view(lnoff + i * TT * 4, F32, (TT,)) for i in range(NLT)]; lnoff += NLT * TT * 4
        SLd = []
        for _i in range(2):
            SLd.append(self.view(lnoff, BF16, (DC, TT))); lnoff += DC * TT * 2
        assert lnoff <= self.FLEX_END, (lnoff, self.FLEX_END)
        for c in range(DC):
            v = Vc(c)
            self.memset(v[:, 0:CONV_PAD], 0.0)
            self.memset(v[:, CONV_PAD + S:VL], 0.0)
        gslabs = {}

        def glu(c, tiles):
            if c not in gslabs:
                gslabs[c] = self.w_next()
            wv, wgt = gslabs[c]
            v = Vc(c)
            for j in tiles:
                sl = slice(j * TT, (j + 1) * TT)
                self.need_H(j)
                bv = self.rotA.next()
                bg = self.rotA.next()
                for kc in range(DC):
                    self.mm(bv, wv[:, kc, :], self.H[:, kc, sl], kc == 0, kc == DC - 1)
                for kc in range(DC):
                    self.mm(bg, wgt[:, kc, :], self.H[:, kc, sl], kc == 0, kc == DC - 1)
                sg = SG.next()
                self.act(sg, bg, AF.Sigmoid)
                self.tt(v[:, CONV_PAD + j * TT:CONV_PAD + (j + 1) * TT], bv, sg, ALU.mult)
        glu(0, range(NT - 1))
        glu(1, range(NT))
        glu(0, [NT - 1])
        self.w_release()
        self.w_release()
        for c in range(2, DC):
            glu(c, range(NT))
            self.w_release()
        self.h_free = False
        MEANb, RSTDb, MSQ = LT[0:3], LT[3:6], LT[6]
        T1 = Rot(LT[7:10])
        sbanks = {}

        def stats_chunk(j, c, copy_eng="act"):
            sl = slice(j * TT, (j + 1) * TT)
            if c == 0:
                sbanks[j] = (self.banks[2], self.banks[3]) if j == 1 else (self.rotS.next(), self.rotS.next())
            b1, b2 = sbanks[j]
            cb = self.SQ.next()
            self.copy(cb, C[:, c, sl], eng=copy_eng)
            cs = self.SQ.next()
            self.act(cs, C[:, c, sl], AF.Square)
            self.mm(b1, self.ONES, cb, c == 0, c == DC - 1)
            self.mm(b2, self.ONES, cs, c == 0, c == DC - 1)

        dwo = self.vlay[f"l{li}_conv_dw"]
        NPE = CONV_W - cfg.conv_dve_taps

        def build_diag(c):
            dwc = self.VEC[:, dwo + c * CONV_W: dwo + c * CONV_W + NPE]
            self.tt(DGb[c % 2][:, 0:NPE, :], self.IDENT.unsqueeze(1).broadcast_to([128, NPE, 128]),
                    dwc.unsqueeze(2).broadcast_to([128, NPE, 128]), ALU.mult)

        pre_tiles = list(range(min(2, NT)))
        pre_items = [(jj, cc) for jj in pre_tiles for cc in range(DC - 1)]
        pre_every = max(1, (NPE * NT) // max(1, len(pre_items)))
        lastrot = Rot(self.banks[4:8])
        build_diag(0)
        for c in range(DC):
            v = Vc(c)
            DGm = DGb[c % 2]
            Cc = C[:, c, :]
            for k in range(NPE, CONV_W):
                wk = self.VEC[:, dwo + c * CONV_W + k: dwo + c * CONV_W + k + 1]
                if k == NPE:
                    self.ts(Cc, v[:, k:k + S], wk, ALU.mult)
                else:
                    self.stt(Cc, v[:, k:k + S], wk, Cc, ALU.mult, ALU.add)
            if c + 1 < DC:
                build_diag(c + 1)
            for j in range(NT):
                sl = slice(j * TT, (j + 1) * TT)
                bank = self.rotA.next() if c < DC - 1 else lastrot.next()
                for k in range(NPE):
                    self.mm(bank, DGm[:, k, :], v[:, j * TT + k:j * TT + k + TT], k == 0, k == NPE - 1)
                    if c == DC - 1 and pre_items and (k + 1) % pre_every == 0:
                        jj, cc = pre_items.pop(0)
                        stats_chunk(jj, cc)
                bias = self.vcol(f"l{li}_conv_dw_bias", c)
                if NPE < CONV_W:
                    self.stt(C[:, c, sl], bank, bias, C[:, c, sl], ALU.add, ALU.add)
                else:
                    self.act(C[:, c, sl], bank, AF.Identity, bias=bias)
        (wout,) = self.w_next()
        def lnmath(j):
            b1, b2 = sbanks[j]
            MEAN, RSTD = MEANb[j % 3], RSTDb[j % 3]
            self.ts(MEAN, b1, 1.0 / D, ALU.mult)
            self.tt(MSQ, MEAN, MEAN, ALU.mult)
            self.stt(RSTD, b2, 1.0 / D, MSQ, ALU.mult, ALU.subtract)
            self.act(RSTD, RSTD, AF.Sqrt, bias=self.vcol("eps", 1))
            self.recip(RSTD, RSTD)

        def normalize(j):
            sl = slice(j * TT, (j + 1) * TT)
            MEAN, RSTD = MEANb[j % 3], RSTDb[j % 3]
            SLb = SLd[j % 2]
            for c in range(DC):
                t1 = T1.next()
                self.tt(t1, C[:, c, sl], MEAN, ALU.subtract)
                self.tt(t1, t1, RSTD, ALU.mult)
                self.act(SLb[:, c, :], t1, AF.Silu, bias=self.vcol(f"l{li}_conv_ln_b", c),
                         scale=self.vcol(f"l{li}_conv_ln_g", c))

        while pre_items:
            jj, cc = pre_items.pop(0)
            stats_chunk(jj, cc)
        stats_chunk(0, DC - 1, copy_eng="dve")
        lnmath(0)
        if NT > 1:
            stats_chunk(1, DC - 1, copy_eng="dve")
            lnmath(1)
        normalize(0)
        for j in range(NT):
            sl = slice(j * TT, (j + 1) * TT)
            SLb = SLd[j % 2]
            for oc in range(DC):
                bank = self.rotA.next()
                for kc in range(DC):
                    self.mm(bank, wout[:, kc, oc * 128:(oc + 1) * 128], SLb[:, kc, :], kc == 0, kc == DC - 1)
                self.tt(self.X[:, oc, sl], self.X[:, oc, sl], bank, ALU.add)
                if j + 2 < NT:
                    stats_chunk(j + 2, oc)
            self.x_ready(j)
            if j + 2 < NT:
                lnmath(j + 2)
            if j + 1 < NT:
                normalize(j + 1)
        self.w_release()
        self.h_free = True
        self.drain(self.nxt)

    def four_layer(self, li):
        cfg = self.cfg
        DC, GC, DG, S, NT, TC, D = cfg.DC, cfg.GC, cfg.DG, cfg.S, cfg.NT, cfg.TC, cfg.D
        assert GC == 2 and DG == 256
        H2 = TC // 2
        NCOL = DG + 2
        abrow = 4 * NCOL
        ab0 = self.H_OFF
        off = al(max(self.FLEX, ab0 + TC * abrow * 2))
        ut = off; off = al(off + DC * S * 2)
        Fd = []
        for _i in range(2):
            Fd.append(self.view(off, BF16, (DC, TT))); off += DC * TT * 2
        F9 = []
        for _i in range(2):
            F9.append(self.view(off, BF16, (TT,))); off += TT * 2
        QsR = Rot([self.view(off + i * TT * 4, F32, (TT,)) for i in range(2)]); off += 2 * TT * 4
        A128 = self.view(off, BF16, (TC, 128)); off += TC * 128 * 2
        W9 = self.view(off, BF16, (D,)); off += D * 2
        assert off <= self.FLEX_END, (off, self.FLEX_END)
        UT = self.view(ut, BF16, (DC, S))

        def AB(tc):
            return self.view(ab0 + tc * abrow * 2, BF16, (4, NCOL))
        ABall = self.view(ab0, BF16, (TC, 4, NCOL))
        tabs = [self.view(ut + h * (2 * H2 * TT * 2), BF16, (2, H2, TT)) for h in range(2)]
        assert 2 * (2 * H2 * TT * 2) <= DC * S * 2
        (win,) = self.w_next()
        self.dma("pool", "wx", (li, self.seq_i), W9, self.wdram[f"l{li}_fourier_w_out_x"])
        self.memset(A128, 0.0)

        def proj(oc, tiles):
            for j in tiles:
                sl = slice(j * TT, (j + 1) * TT)
                self.need_H(j)
                bank = self.rotA.next()
                for kc in range(DC):
                    self.mm(bank, win[:, kc, oc * 128:(oc + 1) * 128], self.H[:, kc, sl], kc == 0, kc == DC - 1)
                self.copy(UT[:, oc, sl], bank, eng="act")
        proj(0, range(NT - 1))
        proj(1, range(NT))
        proj(0, [NT - 1])
        for oc in range(2, DC):
            proj(oc, range(NT))
        self.w_release()
        self.h_free = False
        n = 0
        for tc in range(TC):
            ab = AB(tc)
            for g in range(4):
                bank = self.rotA.next()
                for kc in range(GC):
                    self.mm(bank[:, 0:NCOL], UT[:, g * GC + kc, tc * 128:(tc + 1) * 128], self.CSC[:, kc, :],
                            kc == 0, kc == GC - 1)
                self.copy(ab[:, g, :], bank[:, 0:NCOL], eng=("act" if n % 2 == 0 else "dve"))
                n += 1
        self.copy(A128[:, :, 0:4], ABall[:, :, :, DG], eng="dve")
        (wout,) = self.w_next()
        scale = 1.0 / math.sqrt(S)
        for j in range(NT):
            sl = slice(j * TT, (j + 1) * TT)
            for h in range(2):
                self.dma("sp", f"t{h}", (li, self.seq_i, j), tabs[h], self.dfts[j, h])
            Fj, f9 = Fd[j % 2], F9[j % 2]
            for g in range(4):
                Pb = self.rotA.next()
                Qb = self.rotA.next()
                for tc in range(TC):
                    self.mm(Pb, AB(tc)[:, g, 0:128], tabs[tc // H2][:, 0, tc % H2, :], tc == 0, tc == TC - 1)
                for tc in range(TC):
                    self.mm(Qb, AB(tc)[:, g, 128:256], tabs[tc // H2][:, 1, tc % H2, :], tc == 0, tc == TC - 1)
                qs = QsR.next()
                self.act(qs, Qb, AF.Copy, scale=scale)
                self.stt(Fj[:, 2 * g, :], Pb, scale, qs, ALU.mult, ALU.subtract)
                self.stt(Fj[:, 2 * g + 1, :], Pb, scale, qs, ALU.mult, ALU.add)
            Sb = self.rotA.next()
            for tc in range(TC):
                self.mm(Sb, A128[:, tc, :], tabs[tc // H2][:, 0, tc % H2, :], tc == 0, tc == TC - 1)
            self.act(f9, Sb, AF.Copy, scale=scale)
            for oc in range(DC):
                bank = self.rotA.next()
                for kc in range(DC):
                    self.mm(bank, wout[:, kc, oc * 128:(oc + 1) * 128], Fj[:, kc, :], kc == 0, False)
                self.mm(bank, W9[:, oc * 128:(oc + 1) * 128], f9, False, True)
                self.tt(self.X[:, oc, sl], self.X[:, oc, sl], bank, ALU.add)
            self.x_ready(j)
        self.w_release()
        self.h_free = True
        self.drain(self.nxt)

    def finish(self, s):
        cfg = self.cfg
        if not cfg.final:
            self.out_final.append(self.dma("sp", "out0", ("o", s), self.outT[s].rearrange("(c p) t -> p c t", p=128),
                                           self.X))
            return
        for j in range(cfg.NT):
            self.need_H(j)
        n = self.cur
        while n.loadq:
            self._ready(n.after, n.loadq.pop(0))

    def build(self):
        cfg = self.cfg
        nc = self.nc
        DC, S, D = cfg.DC, cfg.S, cfg.D
        lay, nv = vec_layout(cfg)
        self.vlay = lay
        xT = nc.dram_tensor("xT", [cfg.NSEQ, D, S], F32, kind="ExternalInput").ap()
        vecs = nc.dram_tensor("vecs", [128, nv], F32, kind="ExternalInput").ap()
        cmat = nc.dram_tensor("cmat", [128, 256], BF16, kind="ExternalInput").ap()
        has_four = "four" in cfg.kinds
        if has_four:
            dftc = nc.dram_tensor("dftc", [128, cfg.GC, cfg.DG + 2], BF16, kind="ExternalInput").ap()
            self.dfts = nc.dram_tensor("dfts", [cfg.NT, 2, 128, 2, cfg.TC // 2, TT], BF16, kind="ExternalInput").ap()
        self.wdram = {}
        for name, shape in weight_names(cfg):
            self.wdram[name] = nc.dram_tensor(name, list(shape), F32, kind="ExternalInput").ap()
        self.outT = nc.dram_tensor("outT", [cfg.NSEQ, D, S], F32, kind="ExternalOutput").ap()

        off = 0
        self.X_OFF = off; off += DC * S * 4
        self.H_OFF = off; off += DC * S * 2
        nvb = (nv * 4 + 3) // 4 * 4
        small = nvb + 256 * 2 + (cfg.GC * (cfg.DG + 2) * 2 if has_four else 0) + 4 * TT * 2 + 2 * TT * 4
        self.W_SLOT = max(DC * cfg.SL * 2 + (cfg.SL // 128) * D * 2, DC * D * 2)
        self.W_SLOT = (self.W_SLOT + 255) // 256 * 256
        self.SM_OFF = (cfg.arena_bytes - small) // 256 * 256
        self.W_OFF = self.SM_OFF - 2 * self.W_SLOT
        self.FLEX = off
        self.FLEX_END = self.W_OFF
        assert self.FLEX_END > self.FLEX
        real_view = self.view

        with nc.allow_low_precision("bf16 matmul operands, fp32 accumulation"), \
                nc.sbuf_tensor("arena", [128, cfg.arena_bytes // 4], F32) as arena, \
                nc.psum_tensor("ps", [128, 8, TT], F32) as ps:
            self.arena = arena

            def flexview(off_, dtype, dims):
                return real_view(off_, dtype, dims)
            so = self.SM_OFF
            self.VEC = self.view(so, F32, (nv,)); so += nvb
            CM = self.view(so, BF16, (256,)); so += 512
            self.ONES = CM[:, 0:128]
            self.IDENT = CM[:, 128:256]
            if has_four:
                self.CSC = self.view(so, BF16, (cfg.GC, cfg.DG + 2)); so += cfg.GC * (cfg.DG + 2) * 2
            self.SQ = Rot([self.view(so + i * TT * 2, BF16, (TT,)) for i in range(4)]); so += 4 * TT * 2
            self.RS = Rot([self.view(so + i * TT * 4, F32, (TT,)) for i in range(2)]); so += 2 * TT * 4
            assert so <= cfg.arena_bytes
            self.X = self.view(self.X_OFF, F32, (DC, S))
            self.H = self.view(self.H_OFF, BF16, (DC, S))
            banks = [ps[:, i, :] for i in range(8)]
            self.banks = banks
            self.rotS = Rot(banks[0:2])
            self.rotA = Rot(banks[2:8])
            self.rotU = Rot(banks[2:5])
            self.rotD = Rot(banks[5:8])

            self.dma("sp", "c0", "c", self.VEC, vecs)
            self.dma("sp", "c0", "c", CM, cmat)
            if has_four:
                self.dma("sp", "c0", "c", self.CSC, dftc)

            self.plan = self.slab_plan()
            self.w_issued = self.w_done = self.w_cur = 0
            self.xT = xT
            self.h_free = True
            ooff = self.FLEX + 2 * (cfg.SL // 128) * TT * 2 + 3 * TT * 4
            self.O = [self.view(ooff + i * DC * TT * 4, F32, (DC, TT)) for i in range(2)]
            assert ooff + 2 * DC * TT * 4 <= self.FLEX_END
            NS = Builder.NormState
            norms = []
            for s in range(cfg.NSEQ):
                row = []
                for kind, li in zip(cfg.kinds, cfg.layer_ids):
                    row.append(NS(f"l{li}_norm_mix", seq=s))
                    row.append(NS(f"l{li}_norm_mlp", seq=s))
                if cfg.final:
                    row.append(NS("final_norm", final=True, seq=s))
                else:
                    row.append(None)
                norms.append(row)
            if cfg.final:
                for s in range(cfg.NSEQ - 1):
                    norms[s][-1].after = norms[s + 1][0]
            self.cur = self.nxt = None
            for s in range(cfg.NSEQ):
                self.seq_i = s
                row = norms[s]
                if s == 0 or not cfg.final:
                    for j in range(cfg.NT):
                        self.load_x_tile(s, j)
                        self._ready(row[0], j)
                k = 0
                for kind, li in zip(cfg.kinds, cfg.layer_ids):
                    self.cur, self.nxt = row[k], row[k + 1]
                    if kind == "pool":
                        self.pool_layer(li)
                    elif kind == "conv":
                        self.conv_layer(li)
                    else:
                        self.four_layer(li)
                    k += 1
                    self.cur, self.nxt = row[k], row[k + 1]
                    self.mlp(li)
                    k += 1
                self.cur, self.nxt = row[k], None
                self.finish(s)
            assert self.w_cur == len(self.plan)

            P = self.P
            P.finalize()
            finals = {}
            for op in self.out_final:
                key = ("c", op.chan)
                finals[key] = max(finals.get(key, 0), op.gend)
            keys = P.sem_keys()
            import contextlib
            with contextlib.ExitStack() as es:
                sems = {}
                for k in keys:
                    sems[k] = es.enter_context(nc.semaphore(f"s_{k[0]}_{k[1]}"))
                block = es.enter_context(nc.Block())
                P.emit(block, sems, list(finals.items()))
        return nc


_KINDS = ("pool", "conv", "four", "pool")


def make_in_maps(cfg, inputs, x_shards):
    vec = pack_vecs(cfg, inputs)
    cmat, csc, tab = const_tables(cfg)
    base = {"vecs": vec, "cmat": cmat}
    if "four" in cfg.kinds:
        base["dftc"] = csc
        base["dfts"] = tab
    for name, _shape in weight_names(cfg):
        if name.endswith("fourier_w_out_perm") or name.endswith("fourier_w_out_x"):
            continue
        base[name] = np.ascontiguousarray(np.asarray(inputs[name], np.float32))
    for kind, li in zip(cfg.kinds, cfg.layer_ids):
        if kind != "four":
            continue
        wo = np.asarray(inputs[f"l{li}_fourier_w_out"], np.float32)
        DG = cfg.DG
        perm = np.zeros_like(wo)
        wx = np.zeros((128, cfg.D), np.float32)
        for g in range(4):
            perm[(2 * g) * 128:(2 * g + 1) * 128] = wo[g * DG:g * DG + 128]
            perm[(2 * g + 1) * 128 + 1:(2 * g + 2) * 128] = wo[g * DG + DG - 1:g * DG + 128:-1]
            wx[g] = wo[g * DG + 128]
        base[f"l{li}_fourier_w_out_perm"] = perm
        base[f"l{li}_fourier_w_out_x"] = wx
    maps = []
    for xs in x_shards:
        m = dict(base)
        m["xT"] = xs
        maps.append(m)
    return maps


def kernel(**inputs):
    x = np.asarray(inputs["x"], np.float32)
    B, S, D = x.shape
    ncores = 8
    nseq = B // ncores
    cfg = Cfg(S=S, D=D, FF=4 * D, NSEQ=nseq, kinds=_KINDS, final=True)
    xT = np.ascontiguousarray(x.transpose(0, 2, 1))
    shards = [xT[i * nseq:(i + 1) * nseq] for i in range(ncores)]
    nc = Builder(cfg).build()
    in_maps = make_in_maps(cfg, inputs, shards)
    res = run_bass_kernel_spmd(nc, in_maps, core_ids=list(range(ncores)))
    outT = np.concatenate([np.asarray(r["outT"]) for r in res.results], axis=0)
    return np.ascontiguousarray(outT.transpose(0, 2, 1)).astype(np.float32)
```
